# Optimizing a Trainium2 kernel written in Bass

```python
import math
import jax, jax.numpy as jnp
from jax import lax
import numpy as np

D_MODEL = 1024
BATCH = 16
SEQ = 4096
DEPTH = 4
DEC_BATCH = 2
DEC_SEQ = 16384
PAST_LEN = 128

N_MIXERS = 3
D_FF = 2816
N_ADA = 9
EPS = 1e-6
SG_CHUNK = 128
SG_D_FFN = 6 * D_MODEL
SG_HALF = SG_D_FFN // 2
SG_GROUPS = 8
SG_GROUP_DIM = SG_HALF // SG_GROUPS
DA_HEADS = 8
DA_HEAD_DIM = D_MODEL // (2 * DA_HEADS)
DA_V_DIM = 2 * DA_HEAD_DIM
ROT_DIM = DA_HEAD_DIM // 4
ROPE_THETA = 500000.0
Q_BLOCK = 128
CONV_WIDTH = 3

N_LAYERS_A = (DEPTH + 2) // 3
N_LAYERS_B = (DEPTH + 1) // 3
N_LAYERS_C = DEPTH // 3

kernel_name = "hybrid_bidir_encoder_macaron_adaln"


def rms_norm(x, g):
    xf = x.astype(jnp.float32)
    y = xf * lax.rsqrt(jnp.mean(xf * xf, axis=-1, keepdims=True) + EPS)
    return (y * g.astype(jnp.float32)).astype(x.dtype)


def layer_norm(x, g, b):
    xf = x.astype(jnp.float32)
    mu = jnp.mean(xf, axis=-1, keepdims=True)
    xc = xf - mu
    y = xc * lax.rsqrt(jnp.mean(xc * xc, axis=-1, keepdims=True) + EPS)
    return (y * g.astype(jnp.float32) + b.astype(jnp.float32)).astype(x.dtype)


def modulate(h, shift, scale):
    return h * (1 + scale[:, None, :]) + shift[:, None, :]


def swiglu(h, w_gate, w_up, w_down):
    return (jax.nn.silu(h @ w_gate) * (h @ w_up)) @ w_down


def spatial_gating_mixer(h, w_in, b_in, ln_g, ln_b, w_s, b_s, w_out):
    bsz, s, _ = h.shape
    z = jax.nn.gelu(h @ w_in + b_in, approximate=False)
    u, v = jnp.split(z, 2, axis=-1)
    v = layer_norm(v, ln_g, ln_b)
    v = v.reshape(bsz, s // SG_CHUNK, SG_CHUNK, SG_GROUPS, SG_GROUP_DIM)
    vs = jnp.einsum('gpq,bnqgd->bnpgd', w_s, v) + b_s.T[:, :, None]
    return (u * vs.reshape(bsz, s, SG_HALF)) @ w_out


def partial_rotary(x, cos, sin):
    half = ROT_DIM // 2
    x1 = x[..., :half]
    x2 = x[..., half:ROT_DIM]
    rest = x[..., ROT_DIM:]
    c = cos[None, :, None, None, :]
    s = sin[None, :, None, None, :]
    return jnp.concatenate([x1 * c - x2 * s, x2 * c + x1 * s, rest], axis=-1)


def diff_attention_mixer(h, w_qkv, lam, subln_g, w_out, lambda_init):
    bsz, s, _ = h.shape
    q, k, v = jnp.split(h @ w_qkv, 3, axis=-1)
    q = q.reshape(bsz, s, DA_HEADS, 2, DA_HEAD_DIM)
    k = k.reshape(bsz, s, DA_HEADS, 2, DA_HEAD_DIM)
    v = v.reshape(bsz, s, DA_HEADS, DA_V_DIM)
    pos = jnp.arange(s, dtype=jnp.float32)
    inv_freq = ROPE_THETA ** (-jnp.arange(0, ROT_DIM, 2, dtype=jnp.float32) / ROT_DIM)
    ang = pos[:, None] * inv_freq[None, :]
    cos = jnp.cos(ang).astype(h.dtype)
    sin = jnp.sin(ang).astype(h.dtype)
    q = partial_rotary(q, cos, sin) * (DA_HEAD_DIM ** -0.5)
    k = partial_rotary(k, cos, sin)
    lamf = lam.astype(jnp.float32)
    lambda_full = (jnp.exp(jnp.sum(lamf[0] * lamf[1])) - jnp.exp(jnp.sum(lamf[2] * lamf[3]))
                   + lambda_init)
    q_blocks = q.reshape(bsz, s // Q_BLOCK, Q_BLOCK, DA_HEADS, 2, DA_HEAD_DIM).transpose(1, 0, 2, 3, 4, 5)

    def attend(qb):
        scores = jnp.einsum('bqhcd,bkhcd->bhcqk', qb, k).astype(jnp.float32)
        p = jax.nn.softmax(scores, axis=-1)
        a = (p[:, :, 0] - lambda_full * p[:, :, 1]).astype(v.dtype)
        return jnp.einsum('bhqk,bkhe->bqhe', a, v)

    o = lax.map(attend, q_blocks)
    o = o.transpose(1, 0, 2, 3, 4).reshape(bsz, s, DA_HEADS, DA_V_DIM)
    o = rms_norm(o, subln_g) * (1.0 - lambda_init)
    return o.reshape(bsz, s, D_MODEL) @ w_out


def short_conv_mixer(h, w_in, conv_w, w_out):
    b_gate, c_gate, hv = jnp.split(h @ w_in, 3, axis=-1)
    g = jnp.pad(c_gate * hv, ((0, 0), (1, 1), (0, 0)))
    conv = g[:, :-2] * conv_w[0] + g[:, 1:-1] * conv_w[1] + g[:, 2:] * conv_w[2]
    return (b_gate * conv) @ w_out


def run_trunk(x, c, norm_g, ada_w, ada_b, ffn_w_gate, ffn_w_up, ffn_w_down,
              sg_w_in, sg_b_in, sg_ln_g, sg_ln_b, sg_w_s, sg_b_s, sg_w_out,
              da_w_qkv, da_lambda, da_subln_g, da_w_out,
              conv_w_in, conv_kernel, conv_w_out, final_norm_g):
    c_act = jax.nn.silu(c)
    for i in range(DEPTH):
        mod = c_act @ ada_w[i] + ada_b[i]
        sh1, sc1, g1, sh2, sc2, g2, sh3, sc3, g3 = jnp.split(mod, N_ADA, axis=-1)
        h = modulate(rms_norm(x, norm_g[i, 0]), sh1, sc1)
        x = x + 0.5 * g1[:, None, :] * swiglu(h, ffn_w_gate[i, 0], ffn_w_up[i, 0], ffn_w_down[i, 0])
        h = modulate(rms_norm(x, norm_g[i, 1]), sh2, sc2)
        j = i // N_MIXERS
        kind = i % N_MIXERS
        if kind == 0:
            y = spatial_gating_mixer(h, sg_w_in[j], sg_b_in[j], sg_ln_g[j], sg_ln_b[j],
                                     sg_w_s[j], sg_b_s[j], sg_w_out[j])
        elif kind == 1:
            lambda_init = 0.8 - 0.6 * math.exp(-0.3 * i)
            y = diff_attention_mixer(h, da_w_qkv[j], da_lambda[j], da_subln_g[j], da_w_out[j], lambda_init)
        else:
            y = short_conv_mixer(h, conv_w_in[j], conv_kernel[j], conv_w_out[j])
        x = x + g2[:, None, :] * y
        h = modulate(rms_norm(x, norm_g[i, 2]), sh3, sc3)
        x = x + 0.5 * g3[:, None, :] * swiglu(h, ffn_w_gate[i, 1], ffn_w_up[i, 1], ffn_w_down[i, 1])
    return rms_norm(x, final_norm_g)


def setup_inputs(seed: int = 0) -> dict:
    key = jax.random.key(seed)
    ks = jax.random.split(key, 25)
    f32 = jnp.float32

    def nrm(k, shape, scale):
        return jax.random.normal(k, shape, f32) * scale

    D = D_MODEL
    return {
        "x_prompt": nrm(ks[0], (BATCH, SEQ, D), 1.0),
        "x_sample": nrm(ks[1], (DEC_BATCH, DEC_SEQ, D), 1.0),
        "c_prompt": nrm(ks[2], (BATCH, D), 1.0),
        "c_sample": nrm(ks[3], (DEC_BATCH, D), 1.0),
        "norm_g": 1.0 + nrm(ks[4], (DEPTH, 3, D), 0.1),
        "ada_w": nrm(ks[5], (DEPTH, D, N_ADA * D), 0.5 * D ** -0.5),
        "ada_b": nrm(ks[6], (DEPTH, N_ADA * D), 0.02),
        "ffn_w_gate": nrm(ks[7], (DEPTH, 2, D, D_FF), D ** -0.5),
        "ffn_w_up": nrm(ks[8], (DEPTH, 2, D, D_FF), D ** -0.5),
        "ffn_w_down": nrm(ks[9], (DEPTH, 2, D_FF, D), D_FF ** -0.5),
        "sg_w_in": nrm(ks[10], (N_LAYERS_A, D, SG_D_FFN), D ** -0.5),
        "sg_b_in": nrm(ks[11], (N_LAYERS_A, SG_D_FFN), 0.02),
        "sg_ln_g": 1.0 + nrm(ks[12], (N_LAYERS_A, SG_HALF), 0.1),
        "sg_ln_b": nrm(ks[13], (N_LAYERS_A, SG_HALF), 0.02),
        "sg_w_s": nrm(ks[14], (N_LAYERS_A, SG_GROUPS, SG_CHUNK, SG_CHUNK), SG_CHUNK ** -0.5),
        "sg_b_s": 1.0 + nrm(ks[15], (N_LAYERS_A, SG_GROUPS, SG_CHUNK), 0.1),
        "sg_w_out": nrm(ks[16], (N_LAYERS_A, SG_HALF, D), SG_HALF ** -0.5),
        "da_w_qkv": nrm(ks[17], (N_LAYERS_B, D, 3 * D), D ** -0.5),
        "da_lambda": nrm(ks[18], (N_LAYERS_B, 4, DA_HEAD_DIM), 0.1),
        "da_subln_g": 1.0 + nrm(ks[19], (N_LAYERS_B, DA_V_DIM), 0.1),
        "da_w_out": nrm(ks[20], (N_LAYERS_B, D, D), D ** -0.5),
        "conv_w_in": nrm(ks[21], (N_LAYERS_C, D, 3 * D), D ** -0.5),
        "conv_kernel": nrm(ks[22], (N_LAYERS_C, CONV_WIDTH, D), CONV_WIDTH ** -0.5),
        "conv_w_out": nrm(ks[23], (N_LAYERS_C, D, D), D ** -0.5),
        "final_norm_g": 1.0 + nrm(ks[24], (D,), 0.1),
    }


def reference(x_prompt, x_sample, c_prompt, c_sample, norm_g, ada_w, ada_b,
              ffn_w_gate, ffn_w_up, ffn_w_down,
              sg_w_in, sg_b_in, sg_ln_g, sg_ln_b, sg_w_s, sg_b_s, sg_w_out,
              da_w_qkv, da_lambda, da_subln_g, da_w_out,
              conv_w_in, conv_kernel, conv_w_out, final_norm_g):
    y_prompt = run_trunk(x_prompt, c_prompt, norm_g, ada_w, ada_b, ffn_w_gate, ffn_w_up, ffn_w_down,
                         sg_w_in, sg_b_in, sg_ln_g, sg_ln_b, sg_w_s, sg_b_s, sg_w_out,
                         da_w_qkv, da_lambda, da_subln_g, da_w_out,
                         conv_w_in, conv_kernel, conv_w_out, final_norm_g)
    y_sample = run_trunk(x_sample, c_sample, norm_g, ada_w, ada_b, ffn_w_gate, ffn_w_up, ffn_w_down,
                         sg_w_in, sg_b_in, sg_ln_g, sg_ln_b, sg_w_s, sg_b_s, sg_w_out,
                         da_w_qkv, da_lambda, da_subln_g, da_w_out,
                         conv_w_in, conv_kernel, conv_w_out, final_norm_g)
    return (y_prompt, y_sample)
```

```python
import math
from contextlib import ExitStack

import numpy as np
import concourse.bass as bass
import concourse.mybir as mybir
from concourse.bass_utils import run_bass_kernel_spmd

F32 = mybir.dt.float32
BF16 = mybir.dt.bfloat16
AF = mybir.ActivationFunctionType
ALU = mybir.AluOpType
AX = mybir.AxisListType

D = 1024
KC = 8
FF = 2816
FC = 22
SGH = 3072
VC = 24
T = 512
NCORES = 8
EPS = 1e-6
DEPTH = 4
SLOT_ELEMS = 6144
NSLOTS = 4


class Sched:
    ENGS = ("pe", "act", "dve", "pool", "sp")

    def __init__(self):
        self.ops = []
        self.last_w = {}
        self.readers = {}
        self.pending_bar = {}

    def add(self, eng, fn, reads=(), writes=(), dsem=None):
        idx = len(self.ops)
        deps = set()
        lw = self.last_w
        rdrs = self.readers
        for r in reads:
            w = lw.get(r)
            if w is not None:
                deps.add(w)
        for w_ in writes:
            w = lw.get(w_)
            if w is not None:
                deps.add(w)
            rr = rdrs.get(w_)
            if rr:
                deps.update(rr)
        for r in reads:
            l = rdrs.get(r)
            if l is None:
                rdrs[r] = [idx]
            else:
                l.append(idx)
        for w_ in writes:
            lw[w_] = idx
            rdrs[w_] = []
        if dsem == "misc":
            w = lw.get("__misc_chain__")
            if w is not None:
                deps.add(w)
            lw["__misc_chain__"] = idx
        pb = self.pending_bar.pop(eng, None)
        if pb:
            deps.update(pb)
        self.ops.append([eng, fn, deps, dsem, False])
        return idx

    def barrier(self):
        last = {}
        for i, op in enumerate(self.ops):
            if op[3] is not None:
                last[("d", op[3])] = i
            else:
                last[("e", op[0])] = i
        s = set(last.values())
        for e in self.ENGS:
            self.pending_bar.setdefault(e, set()).update(s)
        self.last_w = {}
        self.readers = {}

    def emit(self, nc, block, sems, dsems):
        ops = self.ops
        for i, op in enumerate(ops):
            for j in op[2]:
                oj = ops[j]
                if oj[3] is None and oj[0] == "pe" and op[0] == "pe" and op[3] is None:
                    continue
                oj[4] = True
        last = {}
        for i, op in enumerate(ops):
            if op[3] is not None:
                last[("d", op[3])] = i
            else:
                last[("e", op[0])] = i
        for i in last.values():
            ops[i][4] = True
        cnt = {}
        val = [0] * len(ops)
        for i, op in enumerate(ops):
            if op[3] is not None:
                k = ("d", op[3])
                cnt[k] = cnt.get(k, 0) + 16
                val[i] = cnt[k]
            elif op[4]:
                k = ("e", op[0])
                cnt[k] = cnt.get(k, 0) + 1
                val[i] = cnt[k]
        per_eng = {e: [] for e in self.ENGS}
        for i, op in enumerate(ops):
            per_eng[op[0]].append(i)

        def semof(j):
            oj = ops[j]
            if oj[3] is not None:
                return ("d", oj[3]), dsems[oj[3]]
            return ("e", oj[0]), sems[oj[0]]

        def run_engine(ename, e):
            seen = {}
            for i in per_eng[ename]:
                op = ops[i]
                need = {}
                for j in op[2]:
                    oj = ops[j]
                    if oj[3] is None and oj[0] == "pe" and ename == "pe" and op[3] is None:
                        continue
                    k, sh = semof(j)
                    v = val[j]
                    if seen.get(k, 0) >= v:
                        continue
                    if k not in need or need[k][1] < v:
                        need[k] = (sh, v)
                for k, (sh, v) in need.items():
                    e.wait_ge(sh, v)
                    seen[k] = v
                ins = op[1](e)
                if op[3] is not None:
                    ins.then_inc(dsems[op[3]], 16)
                elif op[4]:
                    ins.then_inc(sems[ename], 1)
            if ename == "sp":
                for k, i in last.items():
                    _, sh = semof(i)
                    if seen.get(k, 0) < val[i]:
                        e.wait_ge(sh, val[i])

        @block.tensor
        def _(e):
            run_engine("pe", e)

        @block.scalar
        def _(e):
            run_engine("act", e)

        @block.vector
        def _(e):
            run_engine("dve", e)

        @block.gpsimd
        def _(e):
            run_engine("pool", e)

        @block.sync
        def _(e):
            run_engine("sp", e)


class Geo:
    def __init__(self, S_P, S_S):
        self.S_P = S_P
        self.S_S = S_S
        self.Q = S_S // 4
        assert S_P % T == 0 and self.Q % T == 0
        self.NA = 2 * S_P + S_S
        self.NOWN = 2 * S_P + self.Q
        self.ntA = self.NA // T
        self.ntP = S_P // T
        self.ntQ = self.Q // T
        self.regions = [
            (0, self.ntP, self.ntP),
            (self.ntP, self.ntP, self.ntP),
            (2 * self.ntP, S_S // T, self.ntQ),
        ]
        self.halo_tile = 2 * self.ntP + self.ntQ
        self.own = []
        for r, (t0, nk, no) in enumerate(self.regions):
            for i in range(no):
                self.own.append((t0 + i, r, i))

    def sample_order(self, rank):
        nch = self.S_S // 128
        qch = self.Q // 128
        own = list(range(rank * qch, (rank + 1) * qch))
        nxt = ((rank + 1) * qch) % nch
        prv = (rank * qch - 1) % nch
        rest = [c for c in range(nch) if c not in own and c != nxt and c != prv]
        return own + [nxt, prv] + rest


def _rope_table(positions):
    inv_freq = (500000.0 ** (-(np.arange(0, 16, 2, dtype=np.float32)) / np.float32(16))).astype(np.float32)
    ang = positions.astype(np.float32)[:, None] * inv_freq[None, :]
    ang = ang.astype(np.float32)
    return np.concatenate([np.cos(ang), np.sin(ang)], axis=1).astype(np.float32)


def build_program(geo):
    nc = bass.Bass("TRN2", target_bir_lowering=False)
    NA, NOWN = geo.NA, geo.NOWN
    ntA = geo.ntA

    def din(name, shape, dt=F32):
        return nc.dram_tensor(name, list(shape), dt, kind="ExternalInput").ap()

    def dscr(name, shape, dt):
        return nc.dram_tensor(name, list(shape), dt, kind="Internal").ap()

    xA = din("xA", [NA, D])
    rope = din("rope", [NA, 16])
    smallp = din("smallp", [640, 128])
    ident_in = din("ident", [128, 128])
    hmask_in = din("hmask", [128, 2])
    ada_w = din("ada_w", [DEPTH, D, 9 * D])
    w_gate = din("ffn_w_gate", [DEPTH, 2, D, FF])
    w_up = din("ffn_w_up", [DEPTH, 2, D, FF])
    w_down = din("ffn_w_down", [DEPTH, 2, FF, D])
    sg_w_in = din("sg_w_in", [2, D, 6144])
    sg_ln_g = din("sg_ln_g", [2, SGH])
    sg_ln_b = din("sg_ln_b", [2, SGH])
    sg_w_s = din("sg_w_s", [2, 8, 128, 128])
    sg_b_s = din("sg_b_s", [2, 8, 128])
    sg_b_in = din("sg_b_in", [2, 6144])
    sg_w_out = din("sg_w_out", [2, SGH, D])
    da_w_qkv = din("da_w_qkv", [1, D, 3 * D])
    da_lambda = din("da_lambda", [1, 4, 64])
    da_w_out = din("da_w_out", [1, D, D])
    conv_w_in = din("conv_w_in", [1, D, 3 * D])
    conv_w_out = din("conv_w_out", [1, D, D])
    y_out = nc.dram_tensor("y", [NOWN, D], F32, kind="ExternalOutput").ap()
    import os as _os2
    DBG = int(_os2.environ.get("KSTOP", "99")) != 99
    dbg_out = nc.dram_tensor("dbg", [128, KC, T], F32, kind="ExternalOutput").ap() if DBG else None

    wg_s = dscr("wg_s", [DEPTH, 2, 11, 128, KC, 256], BF16)
    wu_s = dscr("wu_s", [DEPTH, 2, 11, 128, KC, 256], BF16)
    wd_s = dscr("wd_s", [DEPTH, 2, 4, 128, FC, 256], BF16)
    sgu_s = dscr("sgu_s", [2, 12, 128, KC, 256], BF16)
    sgv_s = dscr("sgv_s", [2, 6, 128, KC, 512], BF16)
    sgo_s = dscr("sgo_s", [2, 4, 128, VC, 256], BF16)
    qkv_s = dscr("qkv_s", [6, 128, KC, 512], BF16)
    wo_s = dscr("wo_s", [4, 128, KC, 256], BF16)
    cin_s = dscr("cin_s", [4, 128, KC, 3, 256], BF16)
    cout_s = dscr("cout_s", [4, 128, KC, 256], BF16)
    xa_s = dscr("xa_s", [ntA, 128, KC, T], F32)
    qT_s = dscr("qT_s", [ntA, 128, KC, T], BF16)
    kT_s = dscr("kT_s", [8, 128, NA], BF16)
    v_s = dscr("v_s", [NA, D], BF16)
    nown_t = len(geo.own)
    xb_s = dscr("xb_s", [nown_t, 128, KC, T], F32)
    bg_s = dscr("bg_s", [nown_t, 128, KC, T], F32)
    g_s = [dscr(f"g_s{r}", [128, KC, geo.regions[r][2] * T + 2], F32) for r in range(3)]

    es = ExitStack()
    with es:
        def sb(name, shape, dt=F32):
            return es.enter_context(nc.sbuf_tensor("sb_" + name, list(shape), dt))

        sems = {e: es.enter_context(nc.semaphore("sem_" + e)) for e in Sched.ENGS}
        dsem_names = (["slot%d" % i for i in range(NSLOTS)] +
                      ["x", "qt", "st_x", "st_q", "st_k", "st_v", "misc", "big", "prep", "rope", "gw", "bgl", "out"])
        dsems = {n: es.enter_context(nc.semaphore("dsem_" + n)) for n in dsem_names}
        block = es.enter_context(nc.Block())

        S = Sched()
        psum = [es.enter_context(nc.psum_tensor("ps%d" % b, [128, 512], F32)) for b in range(8)]
        ps_rr = [0]

        def ps_next(banks=(0, 1, 2, 3, 4, 5, 6, 7)):
            b = banks[ps_rr[0] % len(banks)]
            ps_rr[0] += 1
            return b

        x = sb("x", [128, KC, T])
        h = sb("h", [128, KC, T], BF16)
        big = sb("big", [128, 48 * 512], BF16)
        slots = [sb("slot%d" % i, [128, SLOT_ELEMS], BF16) for i in range(NSLOTS)]
        rstd = sb("rstd", [128, T])
        tmpA = sb("tmpA", [128, T])
        tmpB = sb("tmpB", [128, T])
        tmpC = sb("tmpC", [128, T])
        ones_f = sb("ones_f", [128, 128])
        ones_b = sb("ones_b", [128, 128], BF16)
        ident_f = sb("ident_f", [128, 128])
        ident_b = sb("ident_b", [128, 128], BF16)
        smallT = sb("smallT", [128, 640])
        stage = sb("stage", [128, 5, 128])
        modt = sb("modt", [128, DEPTH, 72, 3])
        A_t = sb("A_t", [128, DEPTH, 3, 3, KC])
        B_t = sb("B_t", [128, DEPTH, 3, 3, KC])
        G_t = sb("G_t", [128, DEPTH, 3, 3, KC])
        cact = sb("cact", [128, KC, 3])
        neglam = sb("neglam", [128, 1])
        sublng = sb("sublng", [128, 1])
        hmask = sb("hmask", [128, 2])
        lamrow = sb("lamrow", [1, 256])
        lamw = sb("lamw", [1, 8])
        vg = sb("vg", [128, SGH])
        vn = [sb("vn%d" % i, [128, SGH], BF16) for i in range(2)]
        biasT = sb("biasT", [128, VC, 128])
        gsc = sb("gsc", [128, VC])
        wsT = sb("wsT", [128, 8, 128], BF16)
        bvrow = sb("bvrow", [1, SGH], BF16)
        bn6 = sb("bn6", [128, 6, 6])
        mv = sb("mv", [128, 4])
        ropet = sb("ropet", [128, 4, 16])
        pT = [sb("pT%d" % i, [128, T], BF16) for i in range(4)]
        spt_bufs = [sb("spt%d" % i, [128, T]) for i in range(2)]
        lnb_fm = sb("lnb_fm", [128, VC])
        ob = sb("ob", [128, KC, T], BF16)
        zcol = sb("zcol", [128, KC, 1])
        eps_t = sb("eps_t", [128, 1])
        hcol = sb("hcol", [128, KC, 2])

        C_ADAB, C_NORMG, C_BIN, C_CONVK, C_FING, C_SUBLN, C_C = 0, 288, 384, 480, 504, 512, 513

        def bigrows(r0, n):
            return [("big", r) for r in range(r0, r0 + n)]

        def big_bf(r0, n):
            return big[:, r0 * 512:(r0 + n) * 512].rearrange("p (f t) -> p f t", t=512)

        def big_f32(r0, nrows, inner):
            v = big[:, r0 * 512:(r0 + nrows) * 512].bitcast(F32)
            return v.rearrange("p (a b) -> p a b", b=inner)

        class WStream:
            def __init__(self, rec=None):
                self.rec = rec
                self.out = []
                self.i = 0
                self.issued = 0

            def _issue(self, S, idx):
                req, rds = self.rec[idx]
                s = idx % NSLOTS
                for (off, n, src, dshape) in req:
                    dst = slots[s][:, off:off + n]
                    if dshape is not None:
                        dst = dst.rearrange(dshape[0], **dshape[1])
                    S.add("sp", (lambda e, dst=dst, src=src: e.dma_start(out=dst, in_=src)),
                          reads=rds, writes=[("slot", s)], dsem="slot%d" % s)

            def next(self, S, req, rds=()):
                idx = self.i
                self.i += 1
                if self.rec is None:
                    self.out.append((req, list(rds)))
                    return slots[idx % NSLOTS], idx % NSLOTS
                while self.issued < len(self.rec) and self.issued <= idx + NSLOTS - 1:
                    self._issue(S, self.issued)
                    self.issued += 1
                return slots[idx % NSLOTS], idx % NSLOTS

        def mm(S, out, lhsT, rhs, start, stop, reads, bank):
            S.add("pe", (lambda e: e.matmul(out=out, lhsT=lhsT, rhs=rhs, start=start, stop=stop)),
                  reads=reads, writes=[("ps", bank)])

        def norm_mod(S, li, sl, b, hout=None, final=False):
            sq = big_f32(0, 16, T)
            for c in range(KC):
                S.add("act", (lambda e, c=c: e.activation(out=sq[:, c, :], in_=x[:, c, :], func=AF.Square)),
                      reads=[("x", c)], writes=bigrows(2 * c, 2))
            bk = ps_next()
            for c in range(KC):
                mm(S, psum[bk][:, :], ones_f[:, :], sq[:, c, :], c == 0, c == KC - 1,
                   bigrows(2 * c, 2), bk)
            S.add("act", (lambda e: e.activation(out=tmpA[:, :], in_=psum[bk][:, :], func=AF.Sqrt,
                                                 bias=eps_t[:, 0:1], scale=1.0 / D)),
                  reads=[("ps", bk), "eps_t"], writes=["tmpA"])
            S.add("dve", (lambda e: e.reciprocal(out=rstd[:, :], in_=tmpA[:, :])),
                  reads=["tmpA"], writes=["rstd"])
            for c in range(KC):
                if final:
                    a_ap = smallT[:, C_FING + c:C_FING + c + 1]
                    S.add("dve", (lambda e, c=c, a_ap=a_ap: e.scalar_tensor_tensor(
                        out=hout[:, c, :], in0=x[:, c, :], scalar=a_ap, in1=rstd[:, :],
                        op0=ALU.mult, op1=ALU.mult)),
                          reads=[("x", c), "rstd"], writes=bigrows(2 * c, 2))
                    continue
                tt = tmpB if c % 2 == 0 else tmpC
                tk = "tmpB" if c % 2 == 0 else "tmpC"
                a_ap = A_t[:, li, sl, b, c:c + 1]
                b_ap = B_t[:, li, sl, b, c:c + 1]
                S.add("dve", (lambda e, c=c, tt=tt, a_ap=a_ap: e.scalar_tensor_tensor(
                    out=tt[:, :], in0=x[:, c, :], scalar=a_ap, in1=rstd[:, :],
                    op0=ALU.mult, op1=ALU.mult)),
                      reads=[("x", c), "rstd"], writes=[tk])
                S.add("act", (lambda e, c=c, tt=tt, b_ap=b_ap: e.activation(
                    out=h[:, c, :], in_=tt[:, :], func=AF.Identity, bias=b_ap, scale=1.0)),
                      reads=[tk], writes=[("h", c)])

        def ffn(S, ws, li, fi, b, sl):
            norm_mod(S, li, sl, b)
            hid = big_bf(0, FC)
            HB = (0, 1, 2, 3)
            for fb in range(11):
                sl_t, sidx = ws.next(S, [(0, 2048, wg_s[li, fi, fb], ("p (k f) -> p k f", dict(f=256))),
                                         (2048, 2048, wu_s[li, fi, fb], ("p (k f) -> p k f", dict(f=256)))])
                wgv = sl_t[:, 0:2048].rearrange("p (k f) -> p k f", f=256)
                wuv = sl_t[:, 2048:4096].rearrange("p (k f) -> p k f", f=256)
                for fcl in range(2):
                    f = fb * 2 + fcl
                    bg = ps_next(HB)
                    for kc in range(KC):
                        mm(S, psum[bg][:, :], wgv[:, kc, fcl * 128:(fcl + 1) * 128], h[:, kc, :],
                           kc == 0, kc == KC - 1, [("slot", sidx), ("h", kc)], bg)
                    bu = ps_next(HB)
                    for kc in range(KC):
                        mm(S, psum[bu][:, :], wuv[:, kc, fcl * 128:(fcl + 1) * 128], h[:, kc, :],
                           kc == 0, kc == KC - 1, [("slot", sidx), ("h", kc)], bu)
                    tt = tmpB if f % 2 == 0 else tmpC
                    tk = "tmpB" if f % 2 == 0 else "tmpC"
                    S.add("act", (lambda e, bg=bg, tt=tt: e.activation(out=tt[:, :], in_=psum[bg][:, :], func=AF.Silu)),
                          reads=[("ps", bg)], writes=[tk])
                    S.add("dve", (lambda e, bu=bu, tt=tt, f=f: e.tensor_tensor(
                        out=hid[:, f, :], in0=psum[bu][:, :], in1=tt[:, :], op=ALU.mult)),
                          reads=[("ps", bu), tk], writes=bigrows(f, 1))
            DB = (4, 5, 6, 7)
            for db in range(4):
                sl_t, sidx = ws.next(S, [(0, FC * 256, wd_s[li, fi, db], ("p (k f) -> p k f", dict(f=256)))])
                wdv = sl_t[:, 0:FC * 256].rearrange("p (k f) -> p k f", f=256)
                for dcl in range(2):
                    dc = db * 2 + dcl
                    bk = ps_next(DB)
                    for f in range(FC):
                        mm(S, psum[bk][:, :], wdv[:, f, dcl * 128:(dcl + 1) * 128], hid[:, f, :],
                           f == 0, f == FC - 1, [("slot", sidx)] + bigrows(f, 1), bk)
                    g_ap = G_t[:, li, sl, b, dc:dc + 1]
                    S.add("dve", (lambda e, bk=bk, dc=dc, g_ap=g_ap: e.scalar_tensor_tensor(
                        out=x[:, dc, :], in0=psum[bk][:, :], scalar=g_ap, in1=x[:, dc, :],
                        op0=ALU.mult, op1=ALU.add)),
                          reads=[("ps", bk), ("x", dc)], writes=[("x", dc)])

        def gmlp_setup(S, j):
            S.add("sp", (lambda e: e.dma_start(out=stage[0:24, 0, :], in_=sg_ln_g[j].rearrange("(a f) -> a f", f=128))),
                  reads=(), writes=[("stage", 0)], dsem="misc")
            S.add("sp", (lambda e: e.dma_start(out=stage[0:24, 1, :], in_=sg_ln_b[j].rearrange("(a f) -> a f", f=128))),
                  reads=(), writes=[("stage", 1)], dsem="misc")
            bk = ps_next()
            S.add("pe", (lambda e, bk=bk: e.transpose(out=psum[bk][:, 0:24], in_=stage[0:24, 0, :], identity=ident_f[0:24, 0:24])),
                  reads=[("stage", 0), "ident_f"], writes=[("ps", bk)])
            S.add("dve", (lambda e, bk=bk: e.tensor_copy(out=gsc[:, :], in_=psum[bk][:, 0:24])),
                  reads=[("ps", bk)], writes=["gsc"])
            bk2 = ps_next()
            S.add("pe", (lambda e, bk2=bk2: e.transpose(out=psum[bk2][:, 0:24], in_=stage[0:24, 1, :], identity=ident_f[0:24, 0:24])),
                  reads=[("stage", 1), "ident_f"], writes=[("ps", bk2)])
            S.add("dve", (lambda e, bk2=bk2: e.tensor_copy(out=lnb_fm[:, :], in_=psum[bk2][:, 0:24])),
                  reads=[("ps", bk2)], writes=["lnb_fm"])
            S.add("sp", (lambda e: e.dma_start(out=vg[:, 0:1024],
                                               in_=sg_b_s[j].rearrange("g q -> (g q)").partition_broadcast(128))),
                  reads=(), writes=[("vg", 0), ("vg", 1)], dsem="misc")
            for g in range(8):
                sidx = 2 + g % 2
                S.add("sp", (lambda e, g=g, sidx=sidx: e.dma_start(out=stage[:, sidx, :], in_=sg_w_s[j, g])),
                      reads=(), writes=[("stage", sidx)], dsem="misc")
                bk = ps_next()
                S.add("pe", (lambda e, sidx=sidx, bk=bk: e.transpose(out=psum[bk][:, 0:128], in_=stage[:, sidx, :],
                                                                      identity=ident_f[:, :])),
                      reads=[("stage", sidx), "ident_f"], writes=[("ps", bk)])
                S.add("act", (lambda e, bk=bk: e.activation(out=stage[:, 4, :], in_=psum[bk][:, 0:128], func=AF.Copy)),
                      reads=[("ps", bk)], writes=[("stage", 4)])
                S.add("dve", (lambda e, g=g: e.tensor_copy(out=wsT[:, g, :], in_=stage[:, 4, :])),
                      reads=[("stage", 4)], writes=["wsT"])
                bk3 = ps_next()
                S.add("pe", (lambda e, bk3=bk3: e.matmul(out=psum[bk3][:, 0:128], lhsT=ones_f[:, :], rhs=stage[:, 4, :],
                                                          start=True, stop=True)),
                      reads=[("stage", 4), "ones_f"], writes=[("ps", bk3)])
                for dcl in range(3):
                    vc = g * 3 + dcl
                    S.add("dve", (lambda e, g=g, vc=vc, bk3=bk3: e.scalar_tensor_tensor(
                        out=biasT[:, vc, :], in0=psum[bk3][:, 0:128], scalar=lnb_fm[:, vc:vc + 1],
                        in1=vg[:, g * 128:(g + 1) * 128], op0=ALU.mult, op1=ALU.add)),
                          reads=[("ps", bk3), "lnb_fm", ("vg", 0), ("vg", 1)], writes=[("biasT", vc)])
            S.add("sp", (lambda e: e.dma_start(out=vg[0:1, :], in_=sg_b_in[j:j + 1, SGH:2 * SGH])),
                  reads=[("biasT", v_) for v_ in range(VC)], writes=[("vg", i_) for i_ in range(6)], dsem="misc")
            S.add("dve", (lambda e: e.tensor_copy(out=bvrow[:, :], in_=vg[0:1, :])),
                  reads=[("vg", i_) for i_ in range(6)], writes=["bvrow"])

        def gmlp(S, ws, li, j, b):
            norm_mod(S, li, 1, b)
            u = big_bf(0, VC)
            m = big_bf(24, VC)
            UB = (0, 1)
            for ub in range(12):
                sl_t, sidx = ws.next(S, [(0, 2048, sgu_s[j, ub], None)])
                wv = sl_t[:, 0:2048].rearrange("p (k f) -> p k f", f=256)
                for fcl in range(2):
                    fcx = ub * 2 + fcl
                    bk = ps_next(UB)
                    for kc in range(KC):
                        mm(S, psum[bk][:, :], wv[:, kc, fcl * 128:(fcl + 1) * 128], h[:, kc, :],
                           kc == 0, kc == KC - 1, [("slot", sidx), ("h", kc)], bk)
                    bias_ap = smallT[:, C_BIN + j * 48 + fcx:C_BIN + j * 48 + fcx + 1]
                    S.add("act", (lambda e, bk=bk, fcx=fcx, bias_ap=bias_ap: e.activation(
                        out=u[:, fcx, :], in_=psum[bk][:, :], func=AF.Gelu, bias=bias_ap, scale=1.0)),
                          reads=[("ps", bk)], writes=bigrows(fcx, 1))
            VB = (2, 3, 4, 5)
            SB_ = (6, 7)
            for st in range(4):
                vnb = vn[st % 2]
                vnk = "vn%d" % (st % 2)
                for vb in range(6):
                    sl_t, sidx = ws.next(S, [(0, 4096, sgv_s[j, vb], None)])
                    wv = sl_t[:, 0:4096].rearrange("p (k f) -> p k f", f=512)
                    bk = ps_next(VB)
                    for kc in range(KC):
                        mm(S, psum[bk][:, :], h[:, kc, st * 128:(st + 1) * 128], wv[:, kc, :],
                           kc == 0, False, [("slot", sidx), ("h", kc)], bk)
                    mm(S, psum[bk][:, :], ones_b[0:1, :], bvrow[0:1, vb * 512:(vb + 1) * 512],
                       False, True, ["bvrow"], bk)
                    S.add("act", (lambda e, bk=bk, vb=vb: e.activation(
                        out=vg[:, vb * 512:(vb + 1) * 512], in_=psum[bk][:, :], func=AF.Gelu)),
                          reads=[("ps", bk)], writes=[("vg", vb)])
                    S.add("dve", (lambda e, vb=vb: e.bn_stats(out=bn6[:, vb, :], in_=vg[:, vb * 512:(vb + 1) * 512])),
                          reads=[("vg", vb)], writes=[("bn6", vb)])
                S.add("dve", (lambda e: e.bn_aggr(out=mv[:, 0:2], in_=bn6[:, :, :])),
                      reads=[("bn6", i) for i in range(6)], writes=["mv"])
                S.add("act", (lambda e: e.activation(out=mv[:, 2:3], in_=mv[:, 1:2], func=AF.Sqrt,
                                                     bias=eps_t[:, 0:1], scale=1.0)),
                      reads=["mv", "eps_t"], writes=["mv2"])
                S.add("dve", (lambda e: e.reciprocal(out=mv[:, 2:3], in_=mv[:, 2:3])),
                      reads=["mv2"], writes=["mv2"])
                S.add("dve", (lambda e: e.scalar_tensor_tensor(out=mv[:, 3:4], in0=mv[:, 0:1], scalar=-1.0,
                                                               in1=mv[:, 2:3], op0=ALU.mult, op1=ALU.mult)),
                      reads=["mv", "mv2"], writes=["mv3"])
                for vb in range(6):
                    S.add("act", (lambda e, vb=vb, vnb=vnb: e.activation(
                        out=vnb[:, vb * 512:(vb + 1) * 512], in_=vg[:, vb * 512:(vb + 1) * 512],
                        func=AF.Identity, bias=mv[:, 3:4], scale=mv[:, 2:3])),
                          reads=[("vg", vb), "mv2", "mv3"], writes=[(vnk, vb)])
                for grp in range(6):
                    bk = ps_next(SB_)
                    for i4 in range(4):
                        vc = grp * 4 + i4
                        g = vc // 3
                        mm(S, psum[bk][:, i4 * 128:(i4 + 1) * 128], vnb[:, vc * 128:(vc + 1) * 128], wsT[:, g, :],
                           True, True, [(vnk, vc // 4), "wsT"], bk)
                    sbi = (st * 6 + grp) % 2
                    spt = spt_bufs[sbi]
                    for i4 in range(4):
                        vc = grp * 4 + i4
                        S.add("dve", (lambda e, bk=bk, i4=i4, vc=vc, spt=spt: e.scalar_tensor_tensor(
                            out=spt[:, i4 * 128:(i4 + 1) * 128], in0=psum[bk][:, i4 * 128:(i4 + 1) * 128],
                            scalar=gsc[:, vc:vc + 1], in1=biasT[:, vc, :], op0=ALU.mult, op1=ALU.add)),
                              reads=[("ps", bk), ("biasT", vc), "gsc"], writes=[("spt", sbi, i4)])
                    S.add("pool", (lambda e, grp=grp, st=st, spt=spt: e.tensor_tensor(
                        out=m[:, grp * 4:(grp + 1) * 4, st * 128:(st + 1) * 128],
                        in0=spt[:, :].rearrange("p (a q) -> p a q", q=128),
                        in1=u[:, grp * 4:(grp + 1) * 4, st * 128:(st + 1) * 128], op=ALU.mult)),
                          reads=[("spt", sbi, i) for i in range(4)] + bigrows(grp * 4, 4),
                          writes=bigrows(24 + grp * 4, 4))
            OB = (0, 1, 2, 3)
            for db in range(4):
                sl_t, sidx = ws.next(S, [(0, VC * 256, sgo_s[j, db], None)])
                wv = sl_t[:, 0:VC * 256].rearrange("p (k f) -> p k f", f=256)
                for dcl in range(2):
                    dc = db * 2 + dcl
                    bk = ps_next(OB)
                    for vc in range(VC):
                        mm(S, psum[bk][:, :], wv[:, vc, dcl * 128:(dcl + 1) * 128], m[:, vc, :],
                           vc == 0, vc == VC - 1, [("slot", sidx)] + bigrows(24 + vc, 1), bk)
                    g_ap = G_t[:, li, 1, b, dc:dc + 1]
                    S.add("dve", (lambda e, bk=bk, dc=dc, g_ap=g_ap: e.scalar_tensor_tensor(
                        out=x[:, dc, :], in0=psum[bk][:, :], scalar=g_ap, in1=x[:, dc, :],
                        op0=ALU.mult, op1=ALU.add)),
                          reads=[("ps", bk), ("x", dc)], writes=[("x", dc)])

        def load_x_tokens(S, tile_idx):
            xin = big_f32(0, 16, D)
            S.add("sp", (lambda e: e.dma_start(out=xin, in_=xA[tile_idx * T:(tile_idx + 1) * T, :]
                                               .rearrange("(s p) d -> p s d", p=128))),
                  reads=(), writes=bigrows(0, 16), dsem="big")
            for c in range(KC):
                bk = ps_next()
                for st in range(4):
                    S.add("pe", (lambda e, c=c, st=st, bk=bk: e.transpose(
                        out=psum[bk][:, st * 128:(st + 1) * 128], in_=xin[:, st, c * 128:(c + 1) * 128],
                        identity=ident_f[:, :])),
                          reads=bigrows(4 * st, 4) + ["ident_f"], writes=[("ps", bk)])
                eng = "act" if c % 2 == 0 else "dve"
                if eng == "act":
                    S.add("act", (lambda e, c=c, bk=bk: e.activation(out=x[:, c, :], in_=psum[bk][:, :], func=AF.Copy)),
                          reads=[("ps", bk)], writes=[("x", c)])
                else:
                    S.add("dve", (lambda e, c=c, bk=bk: e.tensor_copy(out=x[:, c, :], in_=psum[bk][:, :])),
                          reads=[("ps", bk)], writes=[("x", c)])

        def qkv_rotary(S, ws, tile_idx, b):
            norm_mod(S, 1, 1, b)
            qk = big_f32(0, 32, 2048)
            vbf = big_bf(32, 8).rearrange("p a t -> p (a t)").rearrange("p (s d) -> p s d", d=D)
            qT = big_bf(40, 8)
            S.add("sp", (lambda e: e.dma_start(out=ropet[:, :, :], in_=rope[tile_idx * T:(tile_idx + 1) * T, :]
                                               .rearrange("(s p) c -> p s c", p=128))),
                  reads=(), writes=["ropet"], dsem="rope")
            QB = (0, 1, 2, 3)
            for st in range(4):
                for cb in range(6):
                    sl_t, sidx = ws.next(S, [(0, 4096, qkv_s[cb], None)])
                    wv = sl_t[:, 0:4096].rearrange("p (k f) -> p k f", f=512)
                    bk = ps_next(QB)
                    for kc in range(KC):
                        mm(S, psum[bk][:, :], h[:, kc, st * 128:(st + 1) * 128], wv[:, kc, :],
                           kc == 0, kc == KC - 1, [("slot", sidx), ("h", kc)], bk)
                    if cb < 2:
                        S.add("act", (lambda e, bk=bk, st=st, cb=cb: e.activation(
                            out=qk[:, st, cb * 512:(cb + 1) * 512], in_=psum[bk][:, :], func=AF.Copy, scale=0.125)),
                              reads=[("ps", bk)], writes=bigrows(8 * st + 2 * cb, 2))
                    elif cb < 4:
                        S.add("dve", (lambda e, bk=bk, st=st, cb=cb: e.tensor_copy(
                            out=qk[:, st, cb * 512:(cb + 1) * 512], in_=psum[bk][:, :])),
                              reads=[("ps", bk)], writes=bigrows(8 * st + 2 * cb, 2))
                    else:
                        S.add("act", (lambda e, bk=bk, st=st, cb=cb: e.activation(
                            out=vbf[:, st, (cb - 4) * 512:(cb - 3) * 512], in_=psum[bk][:, :], func=AF.Copy)),
                              reads=[("ps", bk)], writes=bigrows(32 + 2 * st + (cb - 4), 1))
                blk = qk[:, st, :].rearrange("p (a d) -> p a d", d=64)
                x1 = blk[:, :, 0:8]
                x2 = blk[:, :, 8:16]
                cosb = ropet[:, st, 0:8].unsqueeze(1).broadcast_to([128, 32, 8])
                sinb = ropet[:, st, 8:16].unsqueeze(1).broadcast_to([128, 32, 8])
                t1 = tmpA[:, 0:256].rearrange("p (a d) -> p a d", d=8)
                t2 = tmpA[:, 256:512].rearrange("p (a d) -> p a d", d=8)
                t3 = tmpB[:, 0:256].rearrange("p (a d) -> p a d", d=8)
                t4 = tmpB[:, 256:512].rearrange("p (a d) -> p a d", d=8)
                rows = bigrows(8 * st, 8)
                S.add("dve", (lambda e, x1=x1, cosb=cosb, t1=t1: e.tensor_tensor(out=t1, in0=x1, in1=cosb, op=ALU.mult)),
                      reads=rows + ["ropet"], writes=["t1"])
                S.add("pool", (lambda e, x2=x2, sinb=sinb, t2=t2: e.tensor_tensor(out=t2, in0=x2, in1=sinb, op=ALU.mult)),
                      reads=rows + ["ropet"], writes=["t2"])
                S.add("dve", (lambda e, x2=x2, cosb=cosb, t3=t3: e.tensor_tensor(out=t3, in0=x2, in1=cosb, op=ALU.mult)),
                      reads=rows + ["ropet"], writes=["t3"])
                S.add("pool", (lambda e, x1=x1, sinb=sinb, t4=t4: e.tensor_tensor(out=t4, in0=x1, in1=sinb, op=ALU.mult)),
                      reads=rows + ["ropet"], writes=["t4"])
                S.add("dve", (lambda e, x1=x1, t1=t1, t2=t2: e.tensor_tensor(out=x1, in0=t1, in1=t2, op=ALU.subtract)),
                      reads=["t1", "t2"], writes=rows)
                S.add("pool", (lambda e, x2=x2, t3=t3, t4=t4: e.tensor_tensor(out=x2, in0=t3, in1=t4, op=ALU.add)),
                      reads=["t3", "t4"], writes=rows)
                qkb = vn[st % 2][:, 0:2048]
                qkbk = "vn%d" % (st % 2)
                S.add("act", (lambda e, st=st, qkb=qkb: e.activation(out=qkb, in_=qk[:, st, :], func=AF.Copy)),
                      reads=rows, writes=[(qkbk, i) for i in range(6)])
                for half in range(2):
                    bk = ps_next((4, 5, 6, 7))
                    pst = psum[bk][:, :].bitcast(BF16)
                    for hh in range(8):
                        S.add("pe", (lambda e, pst=pst, hh=hh, half=half, qkb=qkb: e.transpose(
                            out=pst[:, hh * 128:(hh + 1) * 128],
                            in_=qkb[:, half * 1024 + hh * 128: half * 1024 + (hh + 1) * 128],
                            identity=ident_b[:, :])),
                              reads=[(qkbk, i) for i in range(6)] + ["ident_b"], writes=[("ps", bk)])
                    if half == 0:
                        S.add("dve", (lambda e, pst=pst, st=st: e.tensor_copy(
                            out=qT[:, :, st * 128:(st + 1) * 128], in_=pst.rearrange("p (a t) -> p a t", t=128))),
                              reads=[("ps", bk)], writes=bigrows(40, 8))
                    else:
                        S.add("dve", (lambda e, pst=pst, st=st: e.tensor_copy(
                            out=ob[:, :, st * 128:(st + 1) * 128], in_=pst.rearrange("p (a t) -> p a t", t=128))),
                              reads=[("ps", bk)], writes=[("kT", st)])
            S.add("pool", (lambda e: e.dma_start(out=xa_s[tile_idx], in_=x[:, :, :])),
                  reads=[("x", c) for c in range(KC)], writes=[("xa_s", tile_idx)], dsem="st_x")
            S.add("pool", (lambda e: e.dma_start(out=qT_s[tile_idx], in_=qT)),
                  reads=bigrows(40, 8), writes=[("qT_s", tile_idx)], dsem="st_q")
            S.add("pool", (lambda e: e.dma_start(
                out=kT_s[:, :, tile_idx * T:(tile_idx + 1) * T].rearrange("a p t -> p a t"), in_=ob[:, :, :])),
                  reads=[("kT", s_) for s_ in range(4)], writes=[("kT_s", tile_idx)], dsem="st_k")
            S.add("pool", (lambda e: e.dma_start(
                out=v_s[tile_idx * T:(tile_idx + 1) * T, :].rearrange("(s p) d -> p s d", p=128), in_=vbf)),
                  reads=bigrows(32, 8), writes=[("v_s", tile_idx)], dsem="st_v")

        def attention(S, ws, tile_idx, region, b):
            t0k, nkt, _ = geo.regions[region]
            nkeys = nkt * T
            key0 = t0k * T
            qT = big_bf(40, 8)
            S.add("sp", (lambda e: e.dma_start(out=qT, in_=qT_s[tile_idx])),
                  reads=[("qT_s", tile_idx)], writes=bigrows(40, 8), dsem="qt")
            KB = 2048 if nkeys >= 2048 else nkeys
            nkb = nkeys // KB
            SCB = (0, 1, 2, 3)
            lam_init = 0.8 - 0.6 * math.exp(-0.3 * 1)
            for hh in range(8):
                nchunks = nkeys // 128
                ci = 0
                for kb in range(nkb):
                    k_src = kT_s[hh, :, key0 + kb * KB: key0 + (kb + 1) * KB]
                    v_src = v_s[key0 + kb * KB: key0 + (kb + 1) * KB, hh * 128:(hh + 1) * 128] \
                        .rearrange("(c p) e -> p c e", p=128)
                    kt0 = (key0 + kb * KB) // T
                    kt1 = (key0 + (kb + 1) * KB - 1) // T
                    rds = [("kT_s", t_) for t_ in range(kt0, kt1 + 1)] + [("v_s", t_) for t_ in range(kt0, kt1 + 1)]
                    sl_t, sidx = ws.next(S, [(0, KB, k_src, None),
                                             (2048, KB, v_src, ("p (c e) -> p c e", dict(e=128)))], rds)
                    kv_reads = [("slot", sidx)]
                    for kcl in range(KB // 128):
                        kTc = sl_t[:, kcl * 128:(kcl + 1) * 128]
                        vch = sl_t[:, 2048 + kcl * 128: 2048 + (kcl + 1) * 128]
                        b0 = ps_next(SCB)
                        S.add("pe", (lambda e, b0=b0, kTc=kTc, hh=hh: e.matmul(
                            out=psum[b0][:, :], lhsT=kTc[0:64, :], rhs=qT[0:64, hh, :], start=True, stop=True)),
                              reads=kv_reads + bigrows(40 + hh, 1), writes=[("ps", b0)])
                        b1 = ps_next(SCB)
                        S.add("pe", (lambda e, b1=b1, kTc=kTc, hh=hh: e.matmul(
                            out=psum[b1][:, :], lhsT=kTc[64:128, :], rhs=qT[64:128, hh, :], start=True, stop=True)),
                              reads=kv_reads + bigrows(40 + hh, 1), writes=[("ps", b1)])
                        p0 = pT[(ci % 2) * 2]
                        p1 = pT[(ci % 2) * 2 + 1]
                        k0 = "pT%d" % ((ci % 2) * 2)
                        k1 = "pT%d" % ((ci % 2) * 2 + 1)
                        S.add("act", (lambda e, b0=b0, p0=p0: e.activation(out=p0[:, :], in_=psum[b0][:, :], func=AF.Exp)),
                              reads=[("ps", b0)], writes=[k0])
                        S.add("act", (lambda e, b1=b1, p1=p1: e.activation(out=p1[:, :], in_=psum[b1][:, :], func=AF.Exp)),
                              reads=[("ps", b1)], writes=[k1])
                        first = ci == 0
                        lastc = ci == nchunks - 1
                        mm(S, psum[4][:, :], vch, p0[:, :], first, lastc, kv_reads + [k0], 4)
                        mm(S, psum[5][:, :], ones_b[:, :], p0[:, :], first, lastc, [k0], 5)
                        mm(S, psum[6][:, :], vch, p1[:, :], first, lastc, kv_reads + [k1], 6)
                        mm(S, psum[7][:, :], ones_b[:, :], p1[:, :], first, lastc, [k1], 7)
                        ci += 1
                S.add("dve", (lambda e: e.reciprocal(out=tmpA[:, :], in_=psum[5][:, :])),
                      reads=[("ps", 5)], writes=["tmpA"])
                S.add("dve", (lambda e: e.reciprocal(out=tmpB[:, :], in_=psum[7][:, :])),
                      reads=[("ps", 7)], writes=["tmpB"])
                S.add("dve", (lambda e: e.tensor_tensor(out=tmpA[:, :], in0=psum[4][:, :], in1=tmpA[:, :], op=ALU.mult)),
                      reads=[("ps", 4), "tmpA"], writes=["tmpA"])
                S.add("dve", (lambda e: e.tensor_tensor(out=tmpB[:, :], in0=psum[6][:, :], in1=tmpB[:, :], op=ALU.mult)),
                      reads=[("ps", 6), "tmpB"], writes=["tmpB"])
                S.add("dve", (lambda e: e.scalar_tensor_tensor(out=tmpC[:, :], in0=tmpB[:, :], scalar=neglam[:, 0:1],
                                                               in1=tmpA[:, :], op0=ALU.mult, op1=ALU.add)),
                      reads=["tmpA", "tmpB", "neglam"], writes=["tmpC"])
                S.add("act", (lambda e: e.activation(out=rstd[:, :], in_=tmpC[:, :], func=AF.Square)),
                      reads=["tmpC"], writes=["rstd"])
                bk = ps_next(SCB)
                mm(S, psum[bk][:, :], ones_f[:, :], rstd[:, :], True, True, ["rstd"], bk)
                S.add("act", (lambda e, bk=bk: e.activation(out=tmpA[:, :], in_=psum[bk][:, :], func=AF.Sqrt,
                                                            bias=eps_t[:, 0:1], scale=1.0 / 128)),
                      reads=[("ps", bk), "eps_t"], writes=["tmpA"])
                S.add("dve", (lambda e: e.reciprocal(out=tmpB[:, :], in_=tmpA[:, :])),
                      reads=["tmpA"], writes=["tmpB"])
                S.add("dve", (lambda e, hh=hh: e.scalar_tensor_tensor(
                    out=ob[:, hh, :], in0=tmpC[:, :], scalar=sublng[:, 0:1], in1=tmpB[:, :],
                    op0=ALU.mult, op1=ALU.mult)),
                      reads=["tmpC", "tmpB", "sublng"], writes=[("ob", hh)])
            for db in range(4):
                sl_t, sidx = ws.next(S, [(0, 2048, wo_s[db], None)])
                wv = sl_t[:, 0:2048].rearrange("p (k f) -> p k f", f=256)
                for dcl in range(2):
                    dc = db * 2 + dcl
                    bk = ps_next(SCB)
                    for hh in range(8):
                        mm(S, psum[bk][:, :], wv[:, hh, dcl * 128:(dcl + 1) * 128], ob[:, hh, :],
                           hh == 0, hh == 7, [("slot", sidx), ("ob", hh)], bk)
                    g_ap = G_t[:, 1, 1, b, dc:dc + 1]
                    S.add("dve", (lambda e, bk=bk, dc=dc, g_ap=g_ap: e.scalar_tensor_tensor(
                        out=x[:, dc, :], in0=psum[bk][:, :], scalar=g_ap, in1=x[:, dc, :],
                        op0=ALU.mult, op1=ALU.add)),
                          reads=[("ps", bk), ("x", dc)], writes=[("x", dc)])

        def conv_in(S, ws, b, own_idx, region, tin, is_halo):
            norm_mod(S, 2, 1, b)
            bgt = big_f32(0, 16, T)
            gt = big_f32(16, 16, T)
            CB = (0, 1, 2, 3, 4, 5)
            for db in range(4):
                sl_t, sidx = ws.next(S, [(0, 6144, cin_s[db], None)])
                wv = sl_t[:, 0:6144].rearrange("p (k s f) -> p k s f", s=3, f=256)
                for dcl in range(2):
                    dc = db * 2 + dcl
                    bks = []
                    for sct in range(3):
                        bk = ps_next(CB)
                        bks.append(bk)
                        for kc in range(KC):
                            mm(S, psum[bk][:, :], wv[:, kc, sct, dcl * 128:(dcl + 1) * 128], h[:, kc, :],
                               kc == 0, kc == KC - 1, [("slot", sidx), ("h", kc)], bk)
                    S.add("act", (lambda e, dc=dc, bk=bks[0]: e.activation(out=bgt[:, dc, :], in_=psum[bk][:, :], func=AF.Copy)),
                          reads=[("ps", bks[0])], writes=bigrows(2 * dc, 2))
                    S.add("act", (lambda e, bk=bks[1]: e.activation(out=tmpA[:, :], in_=psum[bk][:, :], func=AF.Copy)),
                          reads=[("ps", bks[1])], writes=["tmpA"])
                    S.add("dve", (lambda e, dc=dc, bk=bks[2]: e.tensor_tensor(out=gt[:, dc, :], in0=psum[bk][:, :],
                                                                               in1=tmpA[:, :], op=ALU.mult)),
                          reads=[("ps", bks[2]), "tmpA"], writes=bigrows(16 + 2 * dc, 2))
            if is_halo:
                S.add("dve", (lambda e: e.tensor_scalar(out=hcol[:, :, 0:1], in0=gt[:, :, 255:256], scalar1=hmask[:, 0:1],
                                                        scalar2=None, op0=ALU.mult)),
                      reads=bigrows(16, 16) + ["hmask"], writes=["hcol0"])
                S.add("dve", (lambda e: e.tensor_scalar(out=hcol[:, :, 1:2], in0=gt[:, :, 0:1], scalar1=hmask[:, 1:2],
                                                        scalar2=None, op0=ALU.mult)),
                      reads=bigrows(16, 16) + ["hmask"], writes=["hcol1"])
                nq = geo.regions[2][2] * T
                S.add("pool", (lambda e: e.dma_start(out=g_s[2][:, :, 0:1], in_=hcol[:, :, 0:1], allow_slow_non_contiguous=True)),
                      reads=["hcol0"], writes=[("g_s", 2, "lo")], dsem="misc")
                S.add("pool", (lambda e: e.dma_start(out=g_s[2][:, :, nq + 1:nq + 2], in_=hcol[:, :, 1:2], allow_slow_non_contiguous=True)),
                      reads=["hcol1"], writes=[("g_s", 2, "hi")], dsem="misc")
                return
            S.add("pool", (lambda e: e.dma_start(out=xb_s[own_idx], in_=x[:, :, :])),
                  reads=[("x", c) for c in range(KC)], writes=[("xb_s", own_idx)], dsem="st_x")
            S.add("pool", (lambda e: e.dma_start(out=bg_s[own_idx], in_=bgt)),
                  reads=bigrows(0, 16), writes=[("bg_s", own_idx)], dsem="st_q")
            S.add("pool", (lambda e: e.dma_start(out=g_s[region][:, :, 1 + tin * T: 1 + (tin + 1) * T], in_=gt)),
                  reads=bigrows(16, 16), writes=[("g_s", region, tin)], dsem="st_k")

        def conv_mix(S, ws, b, own_idx, region, tin):
            gwin = big[:, 0:8224].bitcast(F32).rearrange("p (c t) -> p c t", t=514)
            bgt = big_f32(17, 16, T)
            mcv = big_bf(33, 8)
            nreg = geo.regions[region][2]
            deps = [("g_s", region, tin)]
            if tin > 0:
                deps.append(("g_s", region, tin - 1))
            else:
                deps.append(("g_s", region, "lo"))
            if tin < nreg - 1:
                deps.append(("g_s", region, tin + 1))
            else:
                deps.append(("g_s", region, "hi"))
            S.add("sp", (lambda e: e.dma_start(out=x[:, :, :], in_=xb_s[own_idx])),
                  reads=[("xb_s", own_idx)], writes=[("x", c) for c in range(KC)], dsem="x")
            S.add("sp", (lambda e: e.dma_start(out=gwin, in_=g_s[region][:, :, tin * T: tin * T + 514])),
                  reads=deps, writes=bigrows(0, 17), dsem="gw")
            S.add("sp", (lambda e: e.dma_start(out=bgt, in_=bg_s[own_idx])),
                  reads=[("bg_s", own_idx)], writes=bigrows(17, 16), dsem="bgl")
            for c in range(KC):
                w0 = smallT[:, C_CONVK + 0 * 8 + c: C_CONVK + 0 * 8 + c + 1]
                w1 = smallT[:, C_CONVK + 1 * 8 + c: C_CONVK + 1 * 8 + c + 1]
                w2 = smallT[:, C_CONVK + 2 * 8 + c: C_CONVK + 2 * 8 + c + 1]
                tt = tmpB if c % 2 == 0 else tmpC
                tk = "tmpB" if c % 2 == 0 else "tmpC"
                S.add("act", (lambda e, c=c, tt=tt, w0=w0: e.activation(out=tt[:, :], in_=gwin[:, c, 0:512],
                                                                        func=AF.Copy, scale=w0)),
                      reads=bigrows(0, 17), writes=[tk])
                S.add("dve", (lambda e, c=c, tt=tt, w1=w1: e.scalar_tensor_tensor(
                    out=tt[:, :], in0=gwin[:, c, 1:513], scalar=w1, in1=tt[:, :], op0=ALU.mult, op1=ALU.add)),
                      reads=bigrows(0, 17) + [tk], writes=[tk])
                S.add("dve", (lambda e, c=c, tt=tt, w2=w2: e.scalar_tensor_tensor(
                    out=tt[:, :], in0=gwin[:, c, 2:514], scalar=w2, in1=tt[:, :], op0=ALU.mult, op1=ALU.add)),
                      reads=bigrows(0, 17) + [tk], writes=[tk])
                S.add("pool", (lambda e, c=c, tt=tt: e.tensor_tensor(out=mcv[:, c, :], in0=tt[:, :], in1=bgt[:, c, :],
                                                                     op=ALU.mult)),
                      reads=[tk] + bigrows(17 + 2 * c, 2), writes=bigrows(33 + c, 1))
            OB_ = (0, 1, 2, 3)
            for db in range(4):
                sl_t, sidx = ws.next(S, [(0, 2048, cout_s[db], None)])
                wv = sl_t[:, 0:2048].rearrange("p (k f) -> p k f", f=256)
                for dcl in range(2):
                    dc = db * 2 + dcl
                    bk = ps_next(OB_)
                    for kc in range(KC):
                        mm(S, psum[bk][:, :], wv[:, kc, dcl * 128:(dcl + 1) * 128], mcv[:, kc, :],
                           kc == 0, kc == KC - 1, [("slot", sidx)] + bigrows(33 + kc, 1), bk)
                    g_ap = G_t[:, 2, 1, b, dc:dc + 1]
                    S.add("dve", (lambda e, bk=bk, dc=dc, g_ap=g_ap: e.scalar_tensor_tensor(
                        out=x[:, dc, :], in0=psum[bk][:, :], scalar=g_ap, in1=x[:, dc, :],
                        op0=ALU.mult, op1=ALU.add)),
                          reads=[("ps", bk), ("x", dc)], writes=[("x", dc)])

        def setup(S):
            S.add("pool", (lambda e: e.memset(ones_f[:, :], 1.0)), reads=(), writes=["ones_f"])
            S.add("pool", (lambda e: e.memset(ones_b[:, :], 1.0)), reads=(), writes=["ones_b"])
            S.add("pool", (lambda e: e.memset(zcol[:, :, :], 0.0)), reads=(), writes=["zcol"])
            S.add("pool", (lambda e: e.memset(eps_t[:, :], EPS)), reads=(), writes=["eps_t"])
            S.add("sp", (lambda e: e.dma_start(out=ident_f[:, :], in_=ident_in[:, :])), reads=(), writes=["ident_f"], dsem="misc")
            S.add("sp", (lambda e: e.dma_start(out=hmask[:, :], in_=hmask_in[:, :])), reads=(), writes=["hmask"], dsem="misc")
            S.add("sp", (lambda e: e.dma_start(out=stage[:, :, :], in_=smallp.rearrange("(a p) f -> p a f", p=128))),
                  reads=(), writes=[("stage", i) for i in range(5)], dsem="misc")
            S.add("dve", (lambda e: e.tensor_copy(out=ident_b[:, :], in_=ident_f[:, :])), reads=["ident_f"], writes=["ident_b"])
            for a in range(5):
                bk = ps_next()
                S.add("pe", (lambda e, a=a, bk=bk: e.transpose(out=psum[bk][:, 0:128], in_=stage[:, a, :], identity=ident_f[:, :])),
                      reads=[("stage", a), "ident_f"], writes=[("ps", bk)])
                S.add("dve", (lambda e, a=a, bk=bk: e.tensor_copy(out=smallT[:, a * 128:(a + 1) * 128], in_=psum[bk][:, 0:128])),
                      reads=[("ps", bk)], writes=["smallT"])
            def prep(dst, src):
                S.add("pool", (lambda e, dst=dst, src=src: e.dma_start(out=dst, in_=src)),
                      reads=(), writes=["wprep"], dsem="prep")
            for li in range(DEPTH):
                for fi in range(2):
                    for fb in range(11):
                        prep(wg_s[li, fi, fb], w_gate[li, fi][:, fb * 256:(fb + 1) * 256].rearrange("(k p) f -> p k f", p=128))
                        prep(wu_s[li, fi, fb], w_up[li, fi][:, fb * 256:(fb + 1) * 256].rearrange("(k p) f -> p k f", p=128))
                    for db in range(4):
                        prep(wd_s[li, fi, db], w_down[li, fi][:, db * 256:(db + 1) * 256].rearrange("(k p) f -> p k f", p=128))
            for j in range(2):
                for ub in range(12):
                    prep(sgu_s[j, ub], sg_w_in[j][:, ub * 256:(ub + 1) * 256].rearrange("(k p) f -> p k f", p=128))
                for vb in range(6):
                    prep(sgv_s[j, vb], sg_w_in[j][:, SGH + vb * 512:SGH + (vb + 1) * 512].rearrange("(k p) f -> p k f", p=128))
                for db in range(4):
                    prep(sgo_s[j, db], sg_w_out[j][:, db * 256:(db + 1) * 256].rearrange("(k p) f -> p k f", p=128))
            for cb in range(6):
                prep(qkv_s[cb], da_w_qkv[0][:, cb * 512:(cb + 1) * 512].rearrange("(k p) f -> p k f", p=128))
            for db in range(4):
                prep(wo_s[db], da_w_out[0][:, db * 256:(db + 1) * 256].rearrange("(k p) f -> p k f", p=128))
                prep(cout_s[db], conv_w_out[0][:, db * 256:(db + 1) * 256].rearrange("(k p) f -> p k f", p=128))
                for sct in range(3):
                    prep(cin_s[db][:, :, sct, :],
                         conv_w_in[0][:, sct * D + db * 256: sct * D + (db + 1) * 256].rearrange("(k p) f -> p k f", p=128))
            for bb in range(3):
                S.add("act", (lambda e, bb=bb: e.activation(out=cact[:, :, bb], in_=smallT[:, C_C + bb * 8: C_C + bb * 8 + 8],
                                                            func=AF.Silu)),
                      reads=["smallT"], writes=["cact"])
            for li in range(DEPTH):
                bk = ps_next()
                for nb in range(18):
                    sidx = nb % 2
                    blk = big[:, sidx * 8192:(sidx + 1) * 8192].bitcast(F32).rearrange("p (k f) -> p k f", f=512)
                    S.add("sp", (lambda e, blk=blk, li=li, nb=nb: e.dma_start(
                        out=blk, in_=ada_w[li][:, nb * 512:(nb + 1) * 512].rearrange("(k p) f -> p k f", p=128))),
                          reads=(), writes=[("adablk", sidx)], dsem="slot%d" % sidx)
                    for n4 in range(4):
                        n = nb * 4 + n4
                        for kc in range(KC):
                            S.add("pe", (lambda e, bk=bk, blk=blk, n=n, n4=n4, kc=kc: e.matmul(
                                out=psum[bk][:, n * 3:(n + 1) * 3], lhsT=blk[:, kc, n4 * 128:(n4 + 1) * 128],
                                rhs=cact[:, kc, :], start=(kc == 0), stop=(kc == KC - 1))),
                                  reads=[("adablk", sidx), "cact"], writes=[("ps", bk)])
                for bb in range(3):
                    S.add("dve", (lambda e, bk=bk, li=li, bb=bb: e.tensor_tensor(
                        out=modt[:, li, :, bb], in0=psum[bk][:, 0:216].rearrange("p (n b) -> p n b", b=3)[:, :, bb],
                        in1=smallT[:, C_ADAB + li * 72: C_ADAB + (li + 1) * 72], op=ALU.add)),
                          reads=[("ps", bk), "smallT"], writes=["modt"])
            for li in range(DEPTH):
                for sl in range(3):
                    for bb in range(3):
                        ng = smallT[:, C_NORMG + (li * 3 + sl) * 8: C_NORMG + (li * 3 + sl) * 8 + 8]
                        S.add("dve", (lambda e, li=li, sl=sl, bb=bb, ng=ng: e.scalar_tensor_tensor(
                            out=A_t[:, li, sl, bb, :], in0=modt[:, li, (3 * sl + 1) * 8:(3 * sl + 2) * 8, bb], scalar=1.0,
                            in1=ng, op0=ALU.add, op1=ALU.mult)),
                              reads=["modt", "smallT"], writes=["A_t"])
                        S.add("dve", (lambda e, li=li, sl=sl, bb=bb: e.tensor_copy(
                            out=B_t[:, li, sl, bb, :], in_=modt[:, li, (3 * sl) * 8:(3 * sl + 1) * 8, bb])),
                              reads=["modt"], writes=["B_t"])
                        S.add("dve", (lambda e, li=li, sl=sl, bb=bb: e.tensor_scalar(
                            out=G_t[:, li, sl, bb, :], in0=modt[:, li, (3 * sl + 2) * 8:(3 * sl + 3) * 8, bb],
                            scalar1=(1.0 if sl == 1 else 0.5), scalar2=None, op0=ALU.mult)),
                              reads=["modt"], writes=["G_t"])
            lam_init = 0.8 - 0.6 * math.exp(-0.3 * 1)
            S.add("sp", (lambda e: e.dma_start(out=lamrow[:, :], in_=da_lambda[0:1].rearrange("a r d -> a (r d)"))),
                  reads=(), writes=["lamrow"], dsem="misc")
            S.add("dve", (lambda e: e.tensor_tensor(out=lamrow[:, 0:64], in0=lamrow[:, 0:64], in1=lamrow[:, 64:128], op=ALU.mult)),
                  reads=["lamrow"], writes=["lamrow"])
            S.add("dve", (lambda e: e.tensor_tensor(out=lamrow[:, 128:192], in0=lamrow[:, 128:192], in1=lamrow[:, 192:256], op=ALU.mult)),
                  reads=["lamrow"], writes=["lamrow"])
            S.add("dve", (lambda e: e.reduce_sum(out=lamw[:, 0:1], in_=lamrow[:, 0:64], axis=AX.X)),
                  reads=["lamrow"], writes=["lamw"])
            S.add("dve", (lambda e: e.reduce_sum(out=lamw[:, 1:2], in_=lamrow[:, 128:192], axis=AX.X)),
                  reads=["lamrow"], writes=["lamw"])
            S.add("act", (lambda e: e.activation(out=lamw[:, 2:4], in_=lamw[:, 0:2], func=AF.Exp)),
                  reads=["lamw"], writes=["lamw"])
            S.add("dve", (lambda e: e.tensor_tensor(out=lamw[:, 4:5], in0=lamw[:, 3:4], in1=lamw[:, 2:3], op=ALU.subtract)),
                  reads=["lamw"], writes=["lamw"])
            S.add("dve", (lambda e: e.tensor_scalar(out=lamw[:, 5:6], in0=lamw[:, 4:5], scalar1=-lam_init, scalar2=None, op0=ALU.add)),
                  reads=["lamw"], writes=["lamw"])
            bk = ps_next()
            S.add("pe", (lambda e, bk=bk: e.matmul(out=psum[bk][:, 0:1], lhsT=ones_f[0:1, :], rhs=lamw[0:1, 5:6], start=True, stop=True)),
                  reads=["lamw", "ones_f"], writes=[("ps", bk)])
            S.add("dve", (lambda e, bk=bk: e.tensor_copy(out=neglam[:, :], in_=psum[bk][:, 0:1])),
                  reads=[("ps", bk)], writes=["neglam"])
            S.add("dve", (lambda e: e.tensor_scalar(out=sublng[:, :], in0=smallT[:, C_SUBLN:C_SUBLN + 1], scalar1=1.0 - lam_init,
                                                    scalar2=None, op0=ALU.mult)),
                  reads=["smallT"], writes=["sublng"])
            for r in range(2):
                n = geo.regions[r][2] * T
                S.add("pool", (lambda e, r=r: e.dma_start(out=g_s[r][:, :, 0:1], in_=zcol[:, :, :], allow_slow_non_contiguous=True)),
                      reads=["zcol"], writes=[("g_s", r, "lo")], dsem="misc")
                S.add("pool", (lambda e, r=r, n=n: e.dma_start(out=g_s[r][:, :, n + 1:n + 2], in_=zcol[:, :, :], allow_slow_non_contiguous=True)),
                      reads=["zcol"], writes=[("g_s", r, "hi")], dsem="misc")


        def final_out2(S, own_idx):
            fin_buf = big_f32(0, 16, T)
            norm_mod(S, 0, 0, 0, hout=fin_buf, final=True)
            otm = big_f32(16, 16, D)
            for st in range(4):
                for half in range(2):
                    bk = ps_next()
                    for c4 in range(4):
                        c = half * 4 + c4
                        S.add("pe", (lambda e, bk=bk, c=c, c4=c4, st=st: e.transpose(
                            out=psum[bk][:, c4 * 128:(c4 + 1) * 128], in_=fin_buf[:, c, st * 128:(st + 1) * 128],
                            identity=ident_f[:, :])),
                              reads=bigrows(2 * c, 2) + ["ident_f"], writes=[("ps", bk)])
                    rws = bigrows(16 + 4 * st + 2 * half, 2)
                    if half == 0:
                        S.add("act", (lambda e, bk=bk, st=st: e.activation(out=otm[:, st, 0:512], in_=psum[bk][:, :], func=AF.Copy)),
                              reads=[("ps", bk)], writes=rws)
                    else:
                        S.add("dve", (lambda e, bk=bk, st=st: e.tensor_copy(out=otm[:, st, 512:1024], in_=psum[bk][:, :])),
                              reads=[("ps", bk)], writes=rws)
            S.add("pool", (lambda e: e.dma_start(out=y_out[own_idx * T:(own_idx + 1) * T, :].rearrange("(s p) d -> p s d", p=128),
                                                 in_=otm)),
                  reads=bigrows(16, 16), writes=[("y", own_idx)], dsem="out")

        def tile_batch(tile_idx):
            if tile_idx < geo.ntP:
                return 0
            if tile_idx < 2 * geo.ntP:
                return 1
            return 2

        import os as _os
        STOP = int(_os.environ.get("KSTOP", "99"))

        def dump_x(S):
            S.add("pool", (lambda e: e.dma_start(out=dbg_out, in_=x[:, :, :])),
                  reads=[("x", c) for c in range(KC)], writes=["dbg"], dsem="out")

        def program(S, ws):
            _program(S, ws)
            if STOP != 99:
                dump_x(S)

        def _program(S, ws):
            setup(S)
            if STOP == 0:
                return
            gmlp_setup(S, 0)
            if STOP == 1:
                return
            S.barrier()
            for ti in range(geo.ntA):
                b = tile_batch(ti)
                load_x_tokens(S, ti)
                if STOP == 2:
                    return
                ffn(S, ws, 0, 0, b, 0)
                if STOP == 3:
                    return
                gmlp(S, ws, 0, 0, b)
                if STOP == 4:
                    return
                ffn(S, ws, 0, 1, b, 2)
                ffn(S, ws, 1, 0, b, 0)
                qkv_rotary(S, ws, ti, b)
                if STOP == 5:
                    return
            S.barrier()
            if STOP == 6:
                return
            btiles = [(ti, r, i, oi) for oi, (ti, r, i) in enumerate(geo.own)] + [(geo.halo_tile, 2, -1, -1)]
            for (ti, r, tin, oi) in btiles:
                b = r
                S.add("sp", (lambda e, ti=ti: e.dma_start(out=x[:, :, :], in_=xa_s[ti])),
                      reads=[("xa_s", ti)], writes=[("x", c) for c in range(KC)], dsem="x")
                if STOP == 10:
                    return
                attention(S, ws, ti, r, b)
                if STOP == 7:
                    return
                ffn(S, ws, 1, 1, b, 2)
                ffn(S, ws, 2, 0, b, 0)
                conv_in(S, ws, b, oi, r, tin, tin < 0)
                if STOP == 8:
                    return
            S.barrier()
            gmlp_setup(S, 1)
            S.barrier()
            for oi, (ti, r, tin) in enumerate(geo.own):
                b = r
                conv_mix(S, ws, b, oi, r, tin)
                if STOP == 9:
                    return
                ffn(S, ws, 2, 1, b, 2)
                ffn(S, ws, 3, 0, b, 0)
                gmlp(S, ws, 3, 1, b)
                ffn(S, ws, 3, 1, b, 2)
                final_out2(S, oi)
                if STOP == 11:
                    return

        rec = WStream(None)
        S0 = Sched()
        ps_rr[0] = 0
        program(S0, rec)
        ps_rr[0] = 0
        ws = WStream(rec.out)
        program(S, ws)
        S.emit(nc, block, sems, dsems)
    return nc


def _run(inputs, S_P, S_S):
    geo = Geo(S_P, S_S)
    f32 = np.float32
    xp = np.asarray(inputs["x_prompt"], f32)
    xs = np.asarray(inputs["x_sample"], f32)
    cp = np.asarray(inputs["c_prompt"], f32)
    cs = np.asarray(inputs["c_sample"], f32)
    assert xp.shape == (16, S_P, D) and xs.shape == (2, S_S, D)
    nc = build_program(geo)
    ident = np.eye(128, dtype=f32)
    shared = {k: np.ascontiguousarray(np.asarray(inputs[k], f32)) for k in
              ["ada_w", "ffn_w_gate", "ffn_w_up", "ffn_w_down", "sg_w_in", "sg_ln_g", "sg_ln_b", "sg_w_s", "sg_b_s",
               "sg_b_in", "sg_w_out", "da_w_qkv", "da_lambda", "da_w_out", "conv_w_in", "conv_w_out"]}
    in_maps = []
    orders = []
    for core in range(NCORES):
        sseq = core // 4
        rank = core % 4
        order = geo.sample_order(rank)
        orders.append(order)
        xs_perm = xs[sseq].reshape(S_S // 128, 128, D)[order].reshape(S_S, D)
        xA = np.concatenate([xp[2 * core], xp[2 * core + 1], xs_perm], axis=0)
        pos_s = (np.asarray(order)[:, None] * 128 + np.arange(128)[None, :]).reshape(-1)
        pos = np.concatenate([np.arange(S_P), np.arange(S_P), pos_s])
        rope = _rope_table(pos)
        small = np.zeros((640, 128), f32)
        small[0:288] = np.asarray(inputs["ada_b"], f32).reshape(288, 128)
        small[288:384] = np.asarray(inputs["norm_g"], f32).reshape(96, 128)
        small[384:480] = np.asarray(inputs["sg_b_in"], f32).reshape(96, 128)
        small[480:504] = np.asarray(inputs["conv_kernel"], f32).reshape(24, 128)
        small[504:512] = np.asarray(inputs["final_norm_g"], f32).reshape(8, 128)
        small[512:513] = np.asarray(inputs["da_subln_g"], f32).reshape(1, 128)
        small[513:521] = cp[2 * core].reshape(8, 128)
        small[521:529] = cp[2 * core + 1].reshape(8, 128)
        small[529:537] = cs[sseq].reshape(8, 128)
        hm = np.zeros((128, 2), f32)
        hm[:, 0] = 0.0 if rank == 0 else 1.0
        hm[:, 1] = 0.0 if rank == 3 else 1.0
        m = {"xA": np.ascontiguousarray(xA), "rope": rope, "smallp": small, "ident": ident, "hmask": hm}
        m.update(shared)
        in_maps.append(m)
    res = run_bass_kernel_spmd(nc, in_maps, core_ids=list(range(NCORES)))
    global _last_res
    _last_res = res
    yp = np.empty((16, S_P, D), f32)
    ys = np.empty((2, S_S, D), f32)
    for core in range(NCORES):
        y = np.asarray(res.results[core]["y"], f32)
        yp[2 * core] = y[0:S_P]
        yp[2 * core + 1] = y[S_P:2 * S_P]
        rank = core % 4
        ys[core // 4, rank * geo.Q:(rank + 1) * geo.Q] = y[2 * S_P:2 * S_P + geo.Q]
    return yp, ys


def kernel(**inputs):
    S_P = int(np.asarray(inputs["x_prompt"]).shape[1])
    S_S = int(np.asarray(inputs["x_sample"]).shape[1])
    return _run(inputs, S_P, S_S)
```

```python
import math
from contextlib import ExitStack

import numpy as np
import concourse.bass as bass
import concourse.mybir as mybir
from concourse.bass_utils import run_bass_kernel_spmd

F32 = mybir.dt.float32
BF16 = mybir.dt.bfloat16
AF = mybir.ActivationFunctionType
ALU = mybir.AluOpType
AX = mybir.AxisListType

D = 1024
KC = 8
FF = 2816
FC = 22
SGH = 3072
VC = 24
T = 512
NCORES = 8
EPS = 1e-6
DEPTH = 4
SLOT_ELEMS = 6144
NSLOTS = 4


class Sched:
    ENGS = ("pe", "act", "dve", "pool", "sp")

    def __init__(self):
        self.ops = []
        self.last_w = {}
        self.readers = {}
        self.pending_bar = {}

    def add(self, eng, fn, reads=(), writes=(), dsem=None):
        idx = len(self.ops)
        deps = set()
        lw = self.last_w
        rdrs = self.readers
        for r in reads:
            w = lw.get(r)
            if w is not None:
                deps.add(w)
        for w_ in writes:
            w = lw.get(w_)
            if w is not None:
                deps.add(w)
            rr = rdrs.get(w_)
            if rr:
                deps.update(rr)
        for r in reads:
            l = rdrs.get(r)
            if l is None:
                rdrs[r] = [idx]
            else:
                l.append(idx)
        for w_ in writes:
            lw[w_] = idx
            rdrs[w_] = []
        if dsem == "misc":
            w = lw.get("__misc_chain__")
            if w is not None:
                deps.add(w)
            lw["__misc_chain__"] = idx
        pb = self.pending_bar.pop(eng, None)
        if pb:
            deps.update(pb)
        self.ops.append([eng, fn, deps, dsem, False])
        return idx

    def barrier(self):
        last = {}
        for i, op in enumerate(self.ops):
            if op[3] is not None:
                last[("d", op[3])] = i
            else:
                last[("e", op[0])] = i
        s = set(last.values())
        for e in self.ENGS:
            self.pending_bar.setdefault(e, set()).update(s)
        self.last_w = {}
        self.readers = {}

    def emit(self, nc, block, sems, dsems):
        ops = self.ops
        for i, op in enumerate(ops):
            for j in op[2]:
                oj = ops[j]
                if oj[3] is None and oj[0] == "pe" and op[0] == "pe" and op[3] is None:
                    continue
                oj[4] = True
        last = {}
        for i, op in enumerate(ops):
            if op[3] is not None:
                last[("d", op[3])] = i
            else:
                last[("e", op[0])] = i
        for i in last.values():
            ops[i][4] = True
        cnt = {}
        val = [0] * len(ops)
        for i, op in enumerate(ops):
            if op[3] is not None:
                k = ("d", op[3])
                cnt[k] = cnt.get(k, 0) + 16
                val[i] = cnt[k]
            elif op[4]:
                k = ("e", op[0])
                cnt[k] = cnt.get(k, 0) + 1
                val[i] = cnt[k]
        per_eng = {e: [] for e in self.ENGS}
        for i, op in enumerate(ops):
            per_eng[op[0]].append(i)

        def semof(j):
            oj = ops[j]
            if oj[3] is not None:
                return ("d", oj[3]), dsems[oj[3]]
            return ("e", oj[0]), sems[oj[0]]

        def run_engine(ename, e):
            seen = {}
            for i in per_eng[ename]:
                op = ops[i]
                need = {}
                for j in op[2]:
                    oj = ops[j]
                    if oj[3] is None and oj[0] == "pe" and ename == "pe" and op[3] is None:
                        continue
                    k, sh = semof(j)
                    v = val[j]
                    if seen.get(k, 0) >= v:
                        continue
                    if k not in need or need[k][1] < v:
                        need[k] = (sh, v)
                for k, (sh, v) in need.items():
                    e.wait_ge(sh, v)
                    seen[k] = v
                ins = op[1](e)
                if op[3] is not None:
                    ins.then_inc(dsems[op[3]], 16)
                elif op[4]:
                    ins.then_inc(sems[ename], 1)
            if ename == "sp":
                for k, i in last.items():
                    _, sh = semof(i)
                    if seen.get(k, 0) < val[i]:
                        e.wait_ge(sh, val[i])

        @block.tensor
        def _(e):
            run_engine("pe", e)

        @block.scalar
        def _(e):
            run_engine("act", e)

        @block.vector
        def _(e):
            run_engine("dve", e)

        @block.gpsimd
        def _(e):
            run_engine("pool", e)

        @block.sync
        def _(e):
            run_engine("sp", e)


class Geo:
    def __init__(self, S_P, S_S):
        self.S_P = S_P
        self.S_S = S_S
        self.Q = S_S // 4
        assert S_P % T == 0 and self.Q % T == 0
        self.NA = 2 * S_P + S_S
        self.NOWN = 2 * S_P + self.Q
        self.ntA = self.NA // T
        self.ntP = S_P // T
        self.ntQ = self.Q // T
        self.regions = [
            (0, self.ntP, self.ntP),
            (self.ntP, self.ntP, self.ntP),
            (2 * self.ntP, S_S // T, self.ntQ),
        ]
        self.halo_tile = 2 * self.ntP + self.ntQ
        self.own = []
        for r, (t0, nk, no) in enumerate(self.regions):
            for i in range(no):
                self.own.append((t0 + i, r, i))

    def sample_order(self, rank):
        nch = self.S_S // 128
        qch = self.Q // 128
        own = list(range(rank * qch, (rank + 1) * qch))
        nxt = ((rank + 1) * qch) % nch
        prv = (rank * qch - 1) % nch
        rest = [c for c in range(nch) if c not in own and c != nxt and c != prv]
        return own + [nxt, prv] + rest


def _rope_table(positions):
    inv_freq = (500000.0 ** (-(np.arange(0, 16, 2, dtype=np.float32)) / np.float32(16))).astype(np.float32)
    ang = positions.astype(np.float32)[:, None] * inv_freq[None, :]
    ang = ang.astype(np.float32)
    return np.concatenate([np.cos(ang), np.sin(ang)], axis=1).astype(np.float32)


def build_program(geo):
    nc = bass.Bass("TRN2", target_bir_lowering=False)
    NA, NOWN = geo.NA, geo.NOWN
    ntA = geo.ntA

    def din(name, shape, dt=F32):
        return nc.dram_tensor(name, list(shape), dt, kind="ExternalInput").ap()

    def dscr(name, shape, dt):
        return nc.dram_tensor(name, list(shape), dt, kind="Internal").ap()

    xA = din("xA", [NA, D])
    rope = din("rope", [NA, 16])
    smallp = din("smallp", [640, 128])
    ident_in = din("ident", [128, 128])
    hmask_in = din("hmask", [128, 2])
    ada_w = din("ada_w", [DEPTH, D, 9 * D])
    w_gate = din("ffn_w_gate", [DEPTH, 2, D, FF])
    w_up = din("ffn_w_up", [DEPTH, 2, D, FF])
    w_down = din("ffn_w_down", [DEPTH, 2, FF, D])
    sg_w_in = din("sg_w_in", [2, D, 6144])
    sg_ln_g = din("sg_ln_g", [2, SGH])
    sg_ln_b = din("sg_ln_b", [2, SGH])
    sg_w_s = din("sg_w_s", [2, 8, 128, 128])
    sg_b_s = din("sg_b_s", [2, 8, 128])
    sg_b_in = din("sg_b_in", [2, 6144])
    sg_w_out = din("sg_w_out", [2, SGH, D])
    da_w_qkv = din("da_w_qkv", [1, D, 3 * D])
    da_lambda = din("da_lambda", [1, 4, 64])
    da_w_out = din("da_w_out", [1, D, D])
    conv_w_in = din("conv_w_in", [1, D, 3 * D])
    conv_w_out = din("conv_w_out", [1, D, D])
    y_out = nc.dram_tensor("y", [NOWN, D], F32, kind="ExternalOutput").ap()
    import os as _os2
    DBG = int(_os2.environ.get("KSTOP", "99")) != 99
    dbg_out = nc.dram_tensor("dbg", [128, KC, T], F32, kind="ExternalOutput").ap() if DBG else None

    wg_s = dscr("wg_s", [DEPTH, 2, 11, 128, KC, 256], BF16)
    wu_s = dscr("wu_s", [DEPTH, 2, 11, 128, KC, 256], BF16)
    wd_s = dscr("wd_s", [DEPTH, 2, 4, 128, FC, 256], BF16)
    sgu_s = dscr("sgu_s", [2, 12, 128, KC, 256], BF16)
    sgv_s = dscr("sgv_s", [2, 6, 128, KC, 512], BF16)
    sgo_s = dscr("sgo_s", [2, 4, 128, VC, 256], BF16)
    qkv_s = dscr("qkv_s", [6, 128, KC, 512], BF16)
    wo_s = dscr("wo_s", [4, 128, KC, 256], BF16)
    cin_s = dscr("cin_s", [4, 128, KC, 3, 256], BF16)
    cout_s = dscr("cout_s", [4, 128, KC, 256], BF16)
    xa_s = dscr("xa_s", [ntA, 128, KC, T], F32)
    qT_s = dscr("qT_s", [ntA, 128, KC, T], BF16)
    kT_s = dscr("kT_s", [8, 128, NA], BF16)
    v_s = dscr("v_s", [NA, D], BF16)
    nown_t = len(geo.own)
    xb_s = dscr("xb_s", [nown_t, 128, KC, T], F32)
    bg_s = dscr("bg_s", [nown_t, 128, KC, T], F32)
    g_s = [dscr(f"g_s{r}", [128, KC, geo.regions[r][2] * T + 2], F32) for r in range(3)]

    es = ExitStack()
    with es:
        def sb(name, shape, dt=F32):
            return es.enter_context(nc.sbuf_tensor("sb_" + name, list(shape), dt))

        sems = {e: es.enter_context(nc.semaphore("sem_" + e)) for e in Sched.ENGS}
        dsem_names = (["slot%d" % i for i in range(NSLOTS)] +
                      ["x", "qt", "st_x", "st_q", "st_k", "st_v", "misc", "big", "prep", "rope", "gw", "bgl", "out"])
        dsems = {n: es.enter_context(nc.semaphore("dsem_" + n)) for n in dsem_names}
        block = es.enter_context(nc.Block())

        S = Sched()
        psum = [es.enter_context(nc.psum_tensor("ps%d" % b, [128, 512], F32)) for b in range(8)]
        ps_rr = [0]

        def ps_next(banks=(0, 1, 2, 3, 4, 5, 6, 7)):
            b = banks[ps_rr[0] % len(banks)]
            ps_rr[0] += 1
            return b

        x = sb("x", [128, KC, T])
        h = sb("h", [128, KC, T], BF16)
        big = sb("big", [128, 48 * 512], BF16)
        slots = [sb("slot%d" % i, [128, SLOT_ELEMS], BF16) for i in range(NSLOTS)]
        rstd = sb("rstd", [128, T])
        tmpA = sb("tmpA", [128, T])
        tmpB = sb("tmpB", [128, T])
        tmpC = sb("tmpC", [128, T])
        ones_f = sb("ones_f", [128, 128])
        ones_b = sb("ones_b", [128, 128], BF16)
        ident_f = sb("ident_f", [128, 128])
        ident_b = sb("ident_b", [128, 128], BF16)
        smallT = sb("smallT", [128, 640])
        stage = sb("stage", [128, 5, 128])
        modt = sb("modt", [128, DEPTH, 72, 3])
        A_t = sb("A_t", [128, DEPTH, 3, 3, KC])
        B_t = sb("B_t", [128, DEPTH, 3, 3, KC])
        G_t = sb("G_t", [128, DEPTH, 3, 3, KC])
        cact = sb("cact", [128, KC, 3])
        neglam = sb("neglam", [128, 1])
        sublng = sb("sublng", [128, 1])
        hmask = sb("hmask", [128, 2])
        lamrow = sb("lamrow", [1, 256])
        lamw = sb("lamw", [1, 8])
        vg = sb("vg", [128, SGH])
        vn = [sb("vn%d" % i, [128, SGH], BF16) for i in range(2)]
        biasT = sb("biasT", [128, VC, 128])
        gsc = sb("gsc", [128, VC])
        wsT = sb("wsT", [128, 8, 128], BF16)
        bvrow = sb("bvrow", [1, SGH], BF16)
        bn6 = sb("bn6", [128, 6, 6])
        mv = sb("mv", [128, 4])
        ropet = sb("ropet", [128, 4, 16])
        pT = [sb("pT%d" % i, [128, T], BF16) for i in range(4)]
        spt_bufs = [sb("spt%d" % i, [128, T]) for i in range(2)]
        lnb_fm = sb("lnb_fm", [128, VC])
        ob = sb("ob", [128, KC, T], BF16)
        zcol = sb("zcol", [128, KC, 1])
        eps_t = sb("eps_t", [128, 1])
        hcol = sb("hcol", [128, KC, 2])

        C_ADAB, C_NORMG, C_BIN, C_CONVK, C_FING, C_SUBLN, C_C = 0, 288, 384, 480, 504, 512, 513

        def bigrows(r0, n):
            return [("big", r) for r in range(r0, r0 + n)]

        def big_bf(r0, n):
            return big[:, r0 * 512:(r0 + n) * 512].rearrange("p (f t) -> p f t", t=512)

        def big_f32(r0, nrows, inner):
            v = big[:, r0 * 512:(r0 + nrows) * 512].bitcast(F32)
            return v.rearrange("p (a b) -> p a b", b=inner)

        class WStream:
            def __init__(self, rec=None):
                self.rec = rec
                self.out = []
                self.i = 0
                self.issued = 0

            def _issue(self, S, idx):
                req, rds = self.rec[idx]
                s = idx % NSLOTS
                for (off, n, src, dshape) in req:
                    dst = slots[s][:, off:off + n]
                    if dshape is not None:
                        dst = dst.rearrange(dshape[0], **dshape[1])
                    S.add("sp", (lambda e, dst=dst, src=src: e.dma_start(out=dst, in_=src)),
                          reads=rds, writes=[("slot", s)], dsem="slot%d" % s)

            def next(self, S, req, rds=()):
                idx = self.i
                self.i += 1
                if self.rec is None:
                    self.out.append((req, list(rds)))
                    return slots[idx % NSLOTS], idx % NSLOTS
                while self.issued < len(self.rec) and self.issued <= idx + NSLOTS - 2:
                    self._issue(S, self.issued)
                    self.issued += 1
                return slots[idx % NSLOTS], idx % NSLOTS

        def mm(S, out, lhsT, rhs, start, stop, reads, bank):
            S.add("pe", (lambda e: e.matmul(out=out, lhsT=lhsT, rhs=rhs, start=start, stop=stop)),
                  reads=reads, writes=[("ps", bank)])

        def norm_mod(S, li, sl, b, hout=None, final=False):
            sq = big_f32(0, 16, T)
            for c in range(KC):
                S.add("act", (lambda e, c=c: e.activation(out=sq[:, c, :], in_=x[:, c, :], func=AF.Square)),
                      reads=[("x", c)], writes=bigrows(2 * c, 2))
            bk = ps_next()
            for c in range(KC):
                mm(S, psum[bk][:, :], ones_f[:, :], sq[:, c, :], c == 0, c == KC - 1,
                   bigrows(2 * c, 2), bk)
            S.add("act", (lambda e: e.activation(out=tmpA[:, :], in_=psum[bk][:, :], func=AF.Sqrt,
                                                 bias=eps_t[:, 0:1], scale=1.0 / D)),
                  reads=[("ps", bk), "eps_t"], writes=["tmpA"])
            S.add("dve", (lambda e: e.reciprocal(out=rstd[:, :], in_=tmpA[:, :])),
                  reads=["tmpA"], writes=["rstd"])
            for c in range(KC):
                if final:
                    a_ap = smallT[:, C_FING + c:C_FING + c + 1]
                    S.add("dve", (lambda e, c=c, a_ap=a_ap: e.scalar_tensor_tensor(
                        out=hout[:, c, :], in0=x[:, c, :], scalar=a_ap, in1=rstd[:, :],
                        op0=ALU.mult, op1=ALU.mult)),
                          reads=[("x", c), "rstd"], writes=bigrows(2 * c, 2))
                    continue
                tt = tmpB if c % 2 == 0 else tmpC
                tk = "tmpB" if c % 2 == 0 else "tmpC"
                a_ap = A_t[:, li, sl, b, c:c + 1]
                b_ap = B_t[:, li, sl, b, c:c + 1]
                S.add("dve", (lambda e, c=c, tt=tt, a_ap=a_ap: e.scalar_tensor_tensor(
                    out=tt[:, :], in0=x[:, c, :], scalar=a_ap, in1=rstd[:, :],
                    op0=ALU.mult, op1=ALU.mult)),
                      reads=[("x", c), "rstd"], writes=[tk])
                S.add("act", (lambda e, c=c, tt=tt, b_ap=b_ap: e.activation(
                    out=h[:, c, :], in_=tt[:, :], func=AF.Identity, bias=b_ap, scale=1.0)),
                      reads=[tk], writes=[("h", c)])

        def ffn(S, ws, li, fi, b, sl):
            norm_mod(S, li, sl, b)
            hid = big_bf(0, FC)
            HB = (0, 1, 2, 3)
            for fb in range(11):
                sl_t, sidx = ws.next(S, [(0, 2048, wg_s[li, fi, fb], ("p (k f) -> p k f", dict(f=256))),
                                         (2048, 2048, wu_s[li, fi, fb], ("p (k f) -> p k f", dict(f=256)))])
                wgv = sl_t[:, 0:2048].rearrange("p (k f) -> p k f", f=256)
                wuv = sl_t[:, 2048:4096].rearrange("p (k f) -> p k f", f=256)
                for fcl in range(2):
                    f = fb * 2 + fcl
                    bg = ps_next(HB)
                    for kc in range(KC):
                        mm(S, psum[bg][:, :], wgv[:, kc, fcl * 128:(fcl + 1) * 128], h[:, kc, :],
                           kc == 0, kc == KC - 1, [("slot", sidx), ("h", kc)], bg)
                    bu = ps_next(HB)
                    for kc in range(KC):
                        mm(S, psum[bu][:, :], wuv[:, kc, fcl * 128:(fcl + 1) * 128], h[:, kc, :],
                           kc == 0, kc == KC - 1, [("slot", sidx), ("h", kc)], bu)
                    tt = tmpB if f % 2 == 0 else tmpC
                    tk = "tmpB" if f % 2 == 0 else "tmpC"
                    S.add("act", (lambda e, bg=bg, tt=tt: e.activation(out=tt[:, :], in_=psum[bg][:, :], func=AF.Silu)),
                          reads=[("ps", bg)], writes=[tk])
                    S.add("dve", (lambda e, bu=bu, tt=tt, f=f: e.tensor_tensor(
                        out=hid[:, f, :], in0=psum[bu][:, :], in1=tt[:, :], op=ALU.mult)),
                          reads=[("ps", bu), tk], writes=bigrows(f, 1))
            DB = (4, 5, 6, 7)
            for db in range(4):
                sl_t, sidx = ws.next(S, [(0, FC * 256, wd_s[li, fi, db], ("p (k f) -> p k f", dict(f=256)))])
                wdv = sl_t[:, 0:FC * 256].rearrange("p (k f) -> p k f", f=256)
                for dcl in range(2):
                    dc = db * 2 + dcl
                    bk = ps_next(DB)
                    for f in range(FC):
                        mm(S, psum[bk][:, :], wdv[:, f, dcl * 128:(dcl + 1) * 128], hid[:, f, :],
                           f == 0, f == FC - 1, [("slot", sidx)] + bigrows(f, 1), bk)
                    g_ap = G_t[:, li, sl, b, dc:dc + 1]
                    S.add("dve", (lambda e, bk=bk, dc=dc, g_ap=g_ap: e.scalar_tensor_tensor(
                        out=x[:, dc, :], in0=psum[bk][:, :], scalar=g_ap, in1=x[:, dc, :],
                        op0=ALU.mult, op1=ALU.add)),
                          reads=[("ps", bk), ("x", dc)], writes=[("x", dc)])

        def gmlp_setup(S, j):
            S.add("sp", (lambda e: e.dma_start(out=stage[0:24, 0, :], in_=sg_ln_g[j].rearrange("(a f) -> a f", f=128))),
                  reads=(), writes=[("stage", 0)], dsem="misc")
            S.add("sp", (lambda e: e.dma_start(out=stage[0:24, 1, :], in_=sg_ln_b[j].rearrange("(a f) -> a f", f=128))),
                  reads=(), writes=[("stage", 1)], dsem="misc")
            bk = ps_next()
            S.add("pe", (lambda e, bk=bk: e.transpose(out=psum[bk][:, 0:24], in_=stage[0:24, 0, :], identity=ident_f[0:24, 0:24])),
                  reads=[("stage", 0), "ident_f"], writes=[("ps", bk)])
            S.add("dve", (lambda e, bk=bk: e.tensor_copy(out=gsc[:, :], in_=psum[bk][:, 0:24])),
                  reads=[("ps", bk)], writes=["gsc"])
            bk2 = ps_next()
            S.add("pe", (lambda e, bk2=bk2: e.transpose(out=psum[bk2][:, 0:24], in_=stage[0:24, 1, :], identity=ident_f[0:24, 0:24])),
                  reads=[("stage", 1), "ident_f"], writes=[("ps", bk2)])
            S.add("dve", (lambda e, bk2=bk2: e.tensor_copy(out=lnb_fm[:, :], in_=psum[bk2][:, 0:24])),
                  reads=[("ps", bk2)], writes=["lnb_fm"])
            S.add("sp", (lambda e: e.dma_start(out=vg[:, 0:1024],
                                               in_=sg_b_s[j].rearrange("g q -> (g q)").partition_broadcast(128))),
                  reads=(), writes=[("vg", 0), ("vg", 1)], dsem="misc")
            for g in range(8):
                sidx = 2 + g % 2
                S.add("sp", (lambda e, g=g, sidx=sidx: e.dma_start(out=stage[:, sidx, :], in_=sg_w_s[j, g])),
                      reads=(), writes=[("stage", sidx)], dsem="misc")
                bk = ps_next()
                S.add("pe", (lambda e, sidx=sidx, bk=bk: e.transpose(out=psum[bk][:, 0:128], in_=stage[:, sidx, :],
                                                                      identity=ident_f[:, :])),
                      reads=[("stage", sidx), "ident_f"], writes=[("ps", bk)])
                S.add("act", (lambda e, bk=bk: e.activation(out=stage[:, 4, :], in_=psum[bk][:, 0:128], func=AF.Copy)),
                      reads=[("ps", bk)], writes=[("stage", 4)])
                S.add("dve", (lambda e, g=g: e.tensor_copy(out=wsT[:, g, :], in_=stage[:, 4, :])),
                      reads=[("stage", 4)], writes=["wsT"])
                bk3 = ps_next()
                S.add("pe", (lambda e, bk3=bk3: e.matmul(out=psum[bk3][:, 0:128], lhsT=ones_f[:, :], rhs=stage[:, 4, :],
                                                          start=True, stop=True)),
                      reads=[("stage", 4), "ones_f"], writes=[("ps", bk3)])
                for dcl in range(3):
                    vc = g * 3 + dcl
                    S.add("dve", (lambda e, g=g, vc=vc, bk3=bk3: e.scalar_tensor_tensor(
                        out=biasT[:, vc, :], in0=psum[bk3][:, 0:128], scalar=lnb_fm[:, vc:vc + 1],
                        in1=vg[:, g * 128:(g + 1) * 128], op0=ALU.mult, op1=ALU.add)),
                          reads=[("ps", bk3), "lnb_fm", ("vg", 0), ("vg", 1)], writes=[("biasT", vc)])
            S.add("sp", (lambda e: e.dma_start(out=vg[0:1, :], in_=sg_b_in[j:j + 1, SGH:2 * SGH])),
                  reads=[("biasT", v_) for v_ in range(VC)], writes=[("vg", i_) for i_ in range(6)], dsem="misc")
            S.add("dve", (lambda e: e.tensor_copy(out=bvrow[:, :], in_=vg[0:1, :])),
                  reads=[("vg", i_) for i_ in range(6)], writes=["bvrow"])

        def gmlp(S, ws, li, j, b):
            norm_mod(S, li, 1, b)
            u = big_bf(0, VC)
            m = big_bf(24, VC)
            UB = (0, 1)
            for ub in range(12):
                sl_t, sidx = ws.next(S, [(0, 2048, sgu_s[j, ub], None)])
                wv = sl_t[:, 0:2048].rearrange("p (k f) -> p k f", f=256)
                for fcl in range(2):
                    fcx = ub * 2 + fcl
                    bk = ps_next(UB)
                    for kc in range(KC):
                        mm(S, psum[bk][:, :], wv[:, kc, fcl * 128:(fcl + 1) * 128], h[:, kc, :],
                           kc == 0, kc == KC - 1, [("slot", sidx), ("h", kc)], bk)
                    bias_ap = smallT[:, C_BIN + j * 48 + fcx:C_BIN + j * 48 + fcx + 1]
                    S.add("act", (lambda e, bk=bk, fcx=fcx, bias_ap=bias_ap: e.activation(
                        out=u[:, fcx, :], in_=psum[bk][:, :], func=AF.Gelu, bias=bias_ap, scale=1.0)),
                          reads=[("ps", bk)], writes=bigrows(fcx, 1))
            VB = (2, 3, 4, 5)
            SB_ = (6, 7)
            for st in range(4):
                vnb = vn[st % 2]
                vnk = "vn%d" % (st % 2)
                for vb in range(6):
                    sl_t, sidx = ws.next(S, [(0, 4096, sgv_s[j, vb], None)])
                    wv = sl_t[:, 0:4096].rearrange("p (k f) -> p k f", f=512)
                    bk = ps_next(VB)
                    for kc in range(KC):
                        mm(S, psum[bk][:, :], h[:, kc, st * 128:(st + 1) * 128], wv[:, kc, :],
                           kc == 0, False, [("slot", sidx), ("h", kc)], bk)
                    mm(S, psum[bk][:, :], ones_b[0:1, :], bvrow[0:1, vb * 512:(vb + 1) * 512],
                       False, True, ["bvrow"], bk)
                    S.add("act", (lambda e, bk=bk, vb=vb: e.activation(
                        out=vg[:, vb * 512:(vb + 1) * 512], in_=psum[bk][:, :], func=AF.Gelu)),
                          reads=[("ps", bk)], writes=[("vg", vb)])
                    S.add("dve", (lambda e, vb=vb: e.bn_stats(out=bn6[:, vb, :], in_=vg[:, vb * 512:(vb + 1) * 512])),
                          reads=[("vg", vb)], writes=[("bn6", vb)])
                S.add("dve", (lambda e: e.bn_aggr(out=mv[:, 0:2], in_=bn6[:, :, :])),
                      reads=[("bn6", i) for i in range(6)], writes=["mv"])
                S.add("act", (lambda e: e.activation(out=mv[:, 2:3], in_=mv[:, 1:2], func=AF.Sqrt,
                                                     bias=eps_t[:, 0:1], scale=1.0)),
                      reads=["mv", "eps_t"], writes=["mv2"])
                S.add("dve", (lambda e: e.reciprocal(out=mv[:, 2:3], in_=mv[:, 2:3])),
                      reads=["mv2"], writes=["mv2"])
                S.add("dve", (lambda e: e.scalar_tensor_tensor(out=mv[:, 3:4], in0=mv[:, 0:1], scalar=-1.0,
                                                               in1=mv[:, 2:3], op0=ALU.mult, op1=ALU.mult)),
                      reads=["mv", "mv2"], writes=["mv3"])
                for vb in range(6):
                    S.add("act", (lambda e, vb=vb, vnb=vnb: e.activation(
                        out=vnb[:, vb * 512:(vb + 1) * 512], in_=vg[:, vb * 512:(vb + 1) * 512],
                        func=AF.Identity, bias=mv[:, 3:4], scale=mv[:, 2:3])),
                          reads=[("vg", vb), "mv2", "mv3"], writes=[(vnk, vb)])
                for grp in range(6):
                    bk = ps_next(SB_)
                    for i4 in range(4):
                        vc = grp * 4 + i4
                        g = vc // 3
                        mm(S, psum[bk][:, i4 * 128:(i4 + 1) * 128], vnb[:, vc * 128:(vc + 1) * 128], wsT[:, g, :],
                           True, True, [(vnk, vc // 4), "wsT"], bk)
                    sbi = (st * 6 + grp) % 2
                    spt = spt_bufs[sbi]
                    for i4 in range(4):
                        vc = grp * 4 + i4
                        S.add("dve", (lambda e, bk=bk, i4=i4, vc=vc, spt=spt: e.scalar_tensor_tensor(
                            out=spt[:, i4 * 128:(i4 + 1) * 128], in0=psum[bk][:, i4 * 128:(i4 + 1) * 128],
                            scalar=gsc[:, vc:vc + 1], in1=biasT[:, vc, :], op0=ALU.mult, op1=ALU.add)),
                              reads=[("ps", bk), ("biasT", vc), "gsc"], writes=[("spt", sbi, i4)])
                    S.add("pool", (lambda e, grp=grp, st=st, spt=spt: e.tensor_tensor(
                        out=m[:, grp * 4:(grp + 1) * 4, st * 128:(st + 1) * 128],
                        in0=spt[:, :].rearrange("p (a q) -> p a q", q=128),
                        in1=u[:, grp * 4:(grp + 1) * 4, st * 128:(st + 1) * 128], op=ALU.mult)),
                          reads=[("spt", sbi, i) for i in range(4)] + bigrows(grp * 4, 4),
                          writes=bigrows(24 + grp * 4, 4))
            OB = (0, 1, 2, 3)
            for db in range(4):
                sl_t, sidx = ws.next(S, [(0, VC * 256, sgo_s[j, db], None)])
                wv = sl_t[:, 0:VC * 256].rearrange("p (k f) -> p k f", f=256)
                for dcl in range(2):
                    dc = db * 2 + dcl
                    bk = ps_next(OB)
                    for vc in range(VC):
                        mm(S, psum[bk][:, :], wv[:, vc, dcl * 128:(dcl + 1) * 128], m[:, vc, :],
                           vc == 0, vc == VC - 1, [("slot", sidx)] + bigrows(24 + vc, 1), bk)
                    g_ap = G_t[:, li, 1, b, dc:dc + 1]
                    S.add("dve", (lambda e, bk=bk, dc=dc, g_ap=g_ap: e.scalar_tensor_tensor(
                        out=x[:, dc, :], in0=psum[bk][:, :], scalar=g_ap, in1=x[:, dc, :],
                        op0=ALU.mult, op1=ALU.add)),
                          reads=[("ps", bk), ("x", dc)], writes=[("x", dc)])

        def load_x_tokens(S, tile_idx):
            xin = big_f32(0, 16, D)
            S.add("sp", (lambda e: e.dma_start(out=xin, in_=xA[tile_idx * T:(tile_idx + 1) * T, :]
                                               .rearrange("(s p) d -> p s d", p=128))),
                  reads=(), writes=bigrows(0, 16), dsem="big")
            for c in range(KC):
                bk = ps_next()
                for st in range(4):
                    S.add("pe", (lambda e, c=c, st=st, bk=bk: e.transpose(
                        out=psum[bk][:, st * 128:(st + 1) * 128], in_=xin[:, st, c * 128:(c + 1) * 128],
                        identity=ident_f[:, :])),
                          reads=bigrows(4 * st, 4) + ["ident_f"], writes=[("ps", bk)])
                eng = "act" if c % 2 == 0 else "dve"
                if eng == "act":
                    S.add("act", (lambda e, c=c, bk=bk: e.activation(out=x[:, c, :], in_=psum[bk][:, :], func=AF.Copy)),
                          reads=[("ps", bk)], writes=[("x", c)])
                else:
                    S.add("dve", (lambda e, c=c, bk=bk: e.tensor_copy(out=x[:, c, :], in_=psum[bk][:, :])),
                          reads=[("ps", bk)], writes=[("x", c)])

        def qkv_rotary(S, ws, tile_idx, b):
            norm_mod(S, 1, 1, b)
            qk = big_f32(0, 32, 2048)
            vbf = big_bf(32, 8).rearrange("p a t -> p (a t)").rearrange("p (s d) -> p s d", d=D)
            qT = big_bf(40, 8)
            S.add("sp", (lambda e: e.dma_start(out=ropet[:, :, :], in_=rope[tile_idx * T:(tile_idx + 1) * T, :]
                                               .rearrange("(s p) c -> p s c", p=128))),
                  reads=(), writes=["ropet"], dsem="rope")
            QB = (0, 1, 2, 3)
            for st in range(4):
                for cb in range(6):
                    sl_t, sidx = ws.next(S, [(0, 4096, qkv_s[cb], None)])
                    wv = sl_t[:, 0:4096].rearrange("p (k f) -> p k f", f=512)
                    bk = ps_next(QB)
                    for kc in range(KC):
                        mm(S, psum[bk][:, :], h[:, kc, st * 128:(st + 1) * 128], wv[:, kc, :],
                           kc == 0, kc == KC - 1, [("slot", sidx), ("h", kc)], bk)
                    if cb < 2:
                        S.add("act", (lambda e, bk=bk, st=st, cb=cb: e.activation(
                            out=qk[:, st, cb * 512:(cb + 1) * 512], in_=psum[bk][:, :], func=AF.Copy, scale=0.125)),
                              reads=[("ps", bk)], writes=bigrows(8 * st + 2 * cb, 2))
                    elif cb < 4:
                        S.add("dve", (lambda e, bk=bk, st=st, cb=cb: e.tensor_copy(
                            out=qk[:, st, cb * 512:(cb + 1) * 512], in_=psum[bk][:, :])),
                              reads=[("ps", bk)], writes=bigrows(8 * st + 2 * cb, 2))
                    else:
                        S.add("act", (lambda e, bk=bk, st=st, cb=cb: e.activation(
                            out=vbf[:, st, (cb - 4) * 512:(cb - 3) * 512], in_=psum[bk][:, :], func=AF.Copy)),
                              reads=[("ps", bk)], writes=bigrows(32 + 2 * st + (cb - 4), 1))
                blk = qk[:, st, :].rearrange("p (a d) -> p a d", d=64)
                x1 = blk[:, :, 0:8]
                x2 = blk[:, :, 8:16]
                cosb = ropet[:, st, 0:8].unsqueeze(1).broadcast_to([128, 32, 8])
                sinb = ropet[:, st, 8:16].unsqueeze(1).broadcast_to([128, 32, 8])
                t1 = tmpA[:, 0:256].rearrange("p (a d) -> p a d", d=8)
                t2 = tmpA[:, 256:512].rearrange("p (a d) -> p a d", d=8)
                t3 = tmpB[:, 0:256].rearrange("p (a d) -> p a d", d=8)
                t4 = tmpB[:, 256:512].rearrange("p (a d) -> p a d", d=8)
                rows = bigrows(8 * st, 8)
                S.add("dve", (lambda e, x1=x1, cosb=cosb, t1=t1: e.tensor_tensor(out=t1, in0=x1, in1=cosb, op=ALU.mult)),
                      reads=rows + ["ropet"], writes=["t1"])
                S.add("pool", (lambda e, x2=x2, sinb=sinb, t2=t2: e.tensor_tensor(out=t2, in0=x2, in1=sinb, op=ALU.mult)),
                      reads=rows + ["ropet"], writes=["t2"])
                S.add("dve", (lambda e, x2=x2, cosb=cosb, t3=t3: e.tensor_tensor(out=t3, in0=x2, in1=cosb, op=ALU.mult)),
                      reads=rows + ["ropet"], writes=["t3"])
                S.add("pool", (lambda e, x1=x1, sinb=sinb, t4=t4: e.tensor_tensor(out=t4, in0=x1, in1=sinb, op=ALU.mult)),
                      reads=rows + ["ropet"], writes=["t4"])
                S.add("dve", (lambda e, x1=x1, t1=t1, t2=t2: e.tensor_tensor(out=x1, in0=t1, in1=t2, op=ALU.subtract)),
                      reads=["t1", "t2"], writes=rows)
                S.add("pool", (lambda e, x2=x2, t3=t3, t4=t4: e.tensor_tensor(out=x2, in0=t3, in1=t4, op=ALU.add)),
                      reads=["t3", "t4"], writes=rows)
                qkb = vn[st % 2][:, 0:2048]
                qkbk = "vn%d" % (st % 2)
                S.add("act", (lambda e, st=st, qkb=qkb: e.activation(out=qkb, in_=qk[:, st, :], func=AF.Copy)),
                      reads=rows, writes=[(qkbk, i) for i in range(6)])
                for half in range(2):
                    bk = ps_next((4, 5, 6, 7))
                    pst = psum[bk][:, :].bitcast(BF16)
                    for hh in range(8):
                        S.add("pe", (lambda e, pst=pst, hh=hh, half=half, qkb=qkb: e.transpose(
                            out=pst[:, hh * 128:(hh + 1) * 128],
                            in_=qkb[:, half * 1024 + hh * 128: half * 1024 + (hh + 1) * 128],
                            identity=ident_b[:, :])),
                              reads=[(qkbk, i) for i in range(6)] + ["ident_b"], writes=[("ps", bk)])
                    if half == 0:
                        S.add("dve", (lambda e, pst=pst, st=st: e.tensor_copy(
                            out=qT[:, :, st * 128:(st + 1) * 128], in_=pst.rearrange("p (a t) -> p a t", t=128))),
                              reads=[("ps", bk)], writes=bigrows(40, 8))
                    else:
                        S.add("dve", (lambda e, pst=pst, st=st: e.tensor_copy(
                            out=ob[:, :, st * 128:(st + 1) * 128], in_=pst.rearrange("p (a t) -> p a t", t=128))),
                              reads=[("ps", bk)], writes=[("kT", st)])
            S.add("pool", (lambda e: e.dma_start(out=xa_s[tile_idx], in_=x[:, :, :])),
                  reads=[("x", c) for c in range(KC)], writes=[("xa_s", tile_idx)], dsem="st_x")
            S.add("pool", (lambda e: e.dma_start(out=qT_s[tile_idx], in_=qT)),
                  reads=bigrows(40, 8), writes=[("qT_s", tile_idx)], dsem="st_q")
            S.add("pool", (lambda e: e.dma_start(
                out=kT_s[:, :, tile_idx * T:(tile_idx + 1) * T].rearrange("a p t -> p a t"), in_=ob[:, :, :])),
                  reads=[("kT", s_) for s_ in range(4)], writes=[("kT_s", tile_idx)], dsem="st_k")
            S.add("pool", (lambda e: e.dma_start(
                out=v_s[tile_idx * T:(tile_idx + 1) * T, :].rearrange("(s p) d -> p s d", p=128), in_=vbf)),
                  reads=bigrows(32, 8), writes=[("v_s", tile_idx)], dsem="st_v")

        def attention(S, ws, tile_idx, region, b):
            t0k, nkt, _ = geo.regions[region]
            nkeys = nkt * T
            key0 = t0k * T
            qT = big_bf(40, 8)
            S.add("sp", (lambda e: e.dma_start(out=qT, in_=qT_s[tile_idx])),
                  reads=[("qT_s", tile_idx)], writes=bigrows(40, 8), dsem="qt")
            KB = 2048 if nkeys >= 2048 else nkeys
            nkb = nkeys // KB
            SCB = (0, 1, 2, 3)
            lam_init = 0.8 - 0.6 * math.exp(-0.3 * 1)
            for hh in range(8):
                nchunks = nkeys // 128
                cpb = KB // 128

                state = {}

                def get_chunk(ci, hh=hh, state=state):
                    kb = ci // cpb
                    if state.get("kb") != kb:
                        k_src = kT_s[hh, :, key0 + kb * KB: key0 + (kb + 1) * KB]
                        v_src = v_s[key0 + kb * KB: key0 + (kb + 1) * KB, hh * 128:(hh + 1) * 128] \
                            .rearrange("(c p) e -> p c e", p=128)
                        kt0 = (key0 + kb * KB) // T
                        kt1 = (key0 + (kb + 1) * KB - 1) // T
                        rds = [("kT_s", t_) for t_ in range(kt0, kt1 + 1)] + [("v_s", t_) for t_ in range(kt0, kt1 + 1)]
                        sl_t, sidx = ws.next(S, [(0, KB, k_src, None),
                                                 (2048, KB, v_src, ("p (c e) -> p c e", dict(e=128)))], rds)
                        state["kb"] = kb
                        state["sl"] = (sl_t, sidx)
                    sl_t, sidx = state["sl"]
                    kcl = ci % cpb
                    return (sl_t[:, kcl * 128:(kcl + 1) * 128],
                            sl_t[:, 2048 + kcl * 128: 2048 + (kcl + 1) * 128], [("slot", sidx)])

                def emit_scores(ci, hh=hh):
                    kTc, vch, kv_reads = get_chunk(ci)
                    b0 = SCB[(2 * ci) % 4]
                    b1 = SCB[(2 * ci + 1) % 4]
                    S.add("pe", (lambda e, b0=b0, kTc=kTc, hh=hh: e.matmul(
                        out=psum[b0][:, :], lhsT=kTc[0:64, :], rhs=qT[0:64, hh, :], start=True, stop=True)),
                          reads=kv_reads + bigrows(40 + hh, 1), writes=[("ps", b0)])
                    S.add("pe", (lambda e, b1=b1, kTc=kTc, hh=hh: e.matmul(
                        out=psum[b1][:, :], lhsT=kTc[64:128, :], rhs=qT[64:128, hh, :], start=True, stop=True)),
                          reads=kv_reads + bigrows(40 + hh, 1), writes=[("ps", b1)])
                    p0 = pT[(ci % 2) * 2]
                    p1 = pT[(ci % 2) * 2 + 1]
                    k0 = "pT%d" % ((ci % 2) * 2)
                    k1 = "pT%d" % ((ci % 2) * 2 + 1)
                    S.add("act", (lambda e, b0=b0, p0=p0: e.activation(out=p0[:, :], in_=psum[b0][:, :], func=AF.Exp)),
                          reads=[("ps", b0)], writes=[k0])
                    S.add("act", (lambda e, b1=b1, p1=p1: e.activation(out=p1[:, :], in_=psum[b1][:, :], func=AF.Exp)),
                          reads=[("ps", b1)], writes=[k1])
                    return (vch, kv_reads, p0, p1, k0, k1)

                pend = emit_scores(0)
                for ci in range(nchunks):
                    cur = pend
                    if ci + 1 < nchunks:
                        pend = emit_scores(ci + 1)
                    vch, kv_reads, p0, p1, k0, k1 = cur
                    first = ci == 0
                    lastc = ci == nchunks - 1
                    mm(S, psum[4][:, :], vch, p0[:, :], first, lastc, kv_reads + [k0], 4)
                    mm(S, psum[5][:, :], ones_b[:, :], p0[:, :], first, lastc, [k0], 5)
                    mm(S, psum[6][:, :], vch, p1[:, :], first, lastc, kv_reads + [k1], 6)
                    mm(S, psum[7][:, :], ones_b[:, :], p1[:, :], first, lastc, [k1], 7)
                S.add("dve", (lambda e: e.reciprocal(out=tmpA[:, :], in_=psum[5][:, :])),
                      reads=[("ps", 5)], writes=["tmpA"])
                S.add("dve", (lambda e: e.reciprocal(out=tmpB[:, :], in_=psum[7][:, :])),
                      reads=[("ps", 7)], writes=["tmpB"])
                S.add("dve", (lambda e: e.tensor_tensor(out=tmpA[:, :], in0=psum[4][:, :], in1=tmpA[:, :], op=ALU.mult)),
                      reads=[("ps", 4), "tmpA"], writes=["tmpA"])
                S.add("dve", (lambda e: e.tensor_tensor(out=tmpB[:, :], in0=psum[6][:, :], in1=tmpB[:, :], op=ALU.mult)),
                      reads=[("ps", 6), "tmpB"], writes=["tmpB"])
                S.add("dve", (lambda e: e.scalar_tensor_tensor(out=tmpC[:, :], in0=tmpB[:, :], scalar=neglam[:, 0:1],
                                                               in1=tmpA[:, :], op0=ALU.mult, op1=ALU.add)),
                      reads=["tmpA", "tmpB", "neglam"], writes=["tmpC"])
                S.add("act", (lambda e: e.activation(out=rstd[:, :], in_=tmpC[:, :], func=AF.Square)),
                      reads=["tmpC"], writes=["rstd"])
                bk = ps_next(SCB)
                mm(S, psum[bk][:, :], ones_f[:, :], rstd[:, :], True, True, ["rstd"], bk)
                S.add("act", (lambda e, bk=bk: e.activation(out=tmpA[:, :], in_=psum[bk][:, :], func=AF.Sqrt,
                                                            bias=eps_t[:, 0:1], scale=1.0 / 128)),
                      reads=[("ps", bk), "eps_t"], writes=["tmpA"])
                S.add("dve", (lambda e: e.reciprocal(out=tmpB[:, :], in_=tmpA[:, :])),
                      reads=["tmpA"], writes=["tmpB"])
                S.add("dve", (lambda e, hh=hh: e.scalar_tensor_tensor(
                    out=ob[:, hh, :], in0=tmpC[:, :], scalar=sublng[:, 0:1], in1=tmpB[:, :],
                    op0=ALU.mult, op1=ALU.mult)),
                      reads=["tmpC", "tmpB", "sublng"], writes=[("ob", hh)])
            for db in range(4):
                sl_t, sidx = ws.next(S, [(0, 2048, wo_s[db], None)])
                wv = sl_t[:, 0:2048].rearrange("p (k f) -> p k f", f=256)
                for dcl in range(2):
                    dc = db * 2 + dcl
                    bk = ps_next(SCB)
                    for hh in range(8):
                        mm(S, psum[bk][:, :], wv[:, hh, dcl * 128:(dcl + 1) * 128], ob[:, hh, :],
                           hh == 0, hh == 7, [("slot", sidx), ("ob", hh)], bk)
                    g_ap = G_t[:, 1, 1, b, dc:dc + 1]
                    S.add("dve", (lambda e, bk=bk, dc=dc, g_ap=g_ap: e.scalar_tensor_tensor(
                        out=x[:, dc, :], in0=psum[bk][:, :], scalar=g_ap, in1=x[:, dc, :],
                        op0=ALU.mult, op1=ALU.add)),
                          reads=[("ps", bk), ("x", dc)], writes=[("x", dc)])

        def conv_in(S, ws, b, own_idx, region, tin, is_halo):
            norm_mod(S, 2, 1, b)
            bgt = big_f32(0, 16, T)
            gt = big_f32(16, 16, T)
            CB = (0, 1, 2, 3, 4, 5)
            for db in range(4):
                sl_t, sidx = ws.next(S, [(0, 6144, cin_s[db], None)])
                wv = sl_t[:, 0:6144].rearrange("p (k s f) -> p k s f", s=3, f=256)
                for dcl in range(2):
                    dc = db * 2 + dcl
                    bks = []
                    for sct in range(3):
                        bk = ps_next(CB)
                        bks.append(bk)
                        for kc in range(KC):
                            mm(S, psum[bk][:, :], wv[:, kc, sct, dcl * 128:(dcl + 1) * 128], h[:, kc, :],
                               kc == 0, kc == KC - 1, [("slot", sidx), ("h", kc)], bk)
                    S.add("act", (lambda e, dc=dc, bk=bks[0]: e.activation(out=bgt[:, dc, :], in_=psum[bk][:, :], func=AF.Copy)),
                          reads=[("ps", bks[0])], writes=bigrows(2 * dc, 2))
                    S.add("act", (lambda e, bk=bks[1]: e.activation(out=tmpA[:, :], in_=psum[bk][:, :], func=AF.Copy)),
                          reads=[("ps", bks[1])], writes=["tmpA"])
                    S.add("dve", (lambda e, dc=dc, bk=bks[2]: e.tensor_tensor(out=gt[:, dc, :], in0=psum[bk][:, :],
                                                                               in1=tmpA[:, :], op=ALU.mult)),
                          reads=[("ps", bks[2]), "tmpA"], writes=bigrows(16 + 2 * dc, 2))
            if is_halo:
                S.add("dve", (lambda e: e.tensor_scalar(out=hcol[:, :, 0:1], in0=gt[:, :, 255:256], scalar1=hmask[:, 0:1],
                                                        scalar2=None, op0=ALU.mult)),
                      reads=bigrows(16, 16) + ["hmask"], writes=["hcol0"])
                S.add("dve", (lambda e: e.tensor_scalar(out=hcol[:, :, 1:2], in0=gt[:, :, 0:1], scalar1=hmask[:, 1:2],
                                                        scalar2=None, op0=ALU.mult)),
                      reads=bigrows(16, 16) + ["hmask"], writes=["hcol1"])
                nq = geo.regions[2][2] * T
                S.add("pool", (lambda e: e.dma_start(out=g_s[2][:, :, 0:1], in_=hcol[:, :, 0:1], allow_slow_non_contiguous=True)),
                      reads=["hcol0"], writes=[("g_s", 2, "lo")], dsem="misc")
                S.add("pool", (lambda e: e.dma_start(out=g_s[2][:, :, nq + 1:nq + 2], in_=hcol[:, :, 1:2], allow_slow_non_contiguous=True)),
                      reads=["hcol1"], writes=[("g_s", 2, "hi")], dsem="misc")
                return
            S.add("pool", (lambda e: e.dma_start(out=xb_s[own_idx], in_=x[:, :, :])),
                  reads=[("x", c) for c in range(KC)], writes=[("xb_s", own_idx)], dsem="st_x")
            S.add("pool", (lambda e: e.dma_start(out=bg_s[own_idx], in_=bgt)),
                  reads=bigrows(0, 16), writes=[("bg_s", own_idx)], dsem="st_q")
            S.add("pool", (lambda e: e.dma_start(out=g_s[region][:, :, 1 + tin * T: 1 + (tin + 1) * T], in_=gt)),
                  reads=bigrows(16, 16), writes=[("g_s", region, tin)], dsem="st_k")

        def conv_mix(S, ws, b, own_idx, region, tin):
            gwin = big[:, 0:8224].bitcast(F32).rearrange("p (c t) -> p c t", t=514)
            bgt = big_f32(17, 16, T)
            mcv = big_bf(33, 8)
            nreg = geo.regions[region][2]
            deps = [("g_s", region, tin)]
            if tin > 0:
                deps.append(("g_s", region, tin - 1))
            else:
                deps.append(("g_s", region, "lo"))
            if tin < nreg - 1:
                deps.append(("g_s", region, tin + 1))
            else:
                deps.append(("g_s", region, "hi"))
            S.add("sp", (lambda e: e.dma_start(out=x[:, :, :], in_=xb_s[own_idx])),
                  reads=[("xb_s", own_idx)], writes=[("x", c) for c in range(KC)], dsem="x")
            S.add("sp", (lambda e: e.dma_start(out=gwin, in_=g_s[region][:, :, tin * T: tin * T + 514])),
                  reads=deps, writes=bigrows(0, 17), dsem="gw")
            S.add("sp", (lambda e: e.dma_start(out=bgt, in_=bg_s[own_idx])),
                  reads=[("bg_s", own_idx)], writes=bigrows(17, 16), dsem="bgl")
            for c in range(KC):
                w0 = smallT[:, C_CONVK + 0 * 8 + c: C_CONVK + 0 * 8 + c + 1]
                w1 = smallT[:, C_CONVK + 1 * 8 + c: C_CONVK + 1 * 8 + c + 1]
                w2 = smallT[:, C_CONVK + 2 * 8 + c: C_CONVK + 2 * 8 + c + 1]
                tt = tmpB if c % 2 == 0 else tmpC
                tk = "tmpB" if c % 2 == 0 else "tmpC"
                S.add("act", (lambda e, c=c, tt=tt, w0=w0: e.activation(out=tt[:, :], in_=gwin[:, c, 0:512],
                                                                        func=AF.Copy, scale=w0)),
                      reads=bigrows(0, 17), writes=[tk])
                S.add("dve", (lambda e, c=c, tt=tt, w1=w1: e.scalar_tensor_tensor(
                    out=tt[:, :], in0=gwin[:, c, 1:513], scalar=w1, in1=tt[:, :], op0=ALU.mult, op1=ALU.add)),
                      reads=bigrows(0, 17) + [tk], writes=[tk])
                S.add("dve", (lambda e, c=c, tt=tt, w2=w2: e.scalar_tensor_tensor(
                    out=tt[:, :], in0=gwin[:, c, 2:514], scalar=w2, in1=tt[:, :], op0=ALU.mult, op1=ALU.add)),
                      reads=bigrows(0, 17) + [tk], writes=[tk])
                S.add("pool", (lambda e, c=c, tt=tt: e.tensor_tensor(out=mcv[:, c, :], in0=tt[:, :], in1=bgt[:, c, :],
                                                                     op=ALU.mult)),
                      reads=[tk] + bigrows(17 + 2 * c, 2), writes=bigrows(33 + c, 1))
            OB_ = (0, 1, 2, 3)
            for db in range(4):
                sl_t, sidx = ws.next(S, [(0, 2048, cout_s[db], None)])
                wv = sl_t[:, 0:2048].rearrange("p (k f) -> p k f", f=256)
                for dcl in range(2):
                    dc = db * 2 + dcl
                    bk = ps_next(OB_)
                    for kc in range(KC):
                        mm(S, psum[bk][:, :], wv[:, kc, dcl * 128:(dcl + 1) * 128], mcv[:, kc, :],
                           kc == 0, kc == KC - 1, [("slot", sidx)] + bigrows(33 + kc, 1), bk)
                    g_ap = G_t[:, 2, 1, b, dc:dc + 1]
                    S.add("dve", (lambda e, bk=bk, dc=dc, g_ap=g_ap: e.scalar_tensor_tensor(
                        out=x[:, dc, :], in0=psum[bk][:, :], scalar=g_ap, in1=x[:, dc, :],
                        op0=ALU.mult, op1=ALU.add)),
                          reads=[("ps", bk), ("x", dc)], writes=[("x", dc)])

        def setup(S):
            S.add("pool", (lambda e: e.memset(ones_f[:, :], 1.0)), reads=(), writes=["ones_f"])
            S.add("pool", (lambda e: e.memset(ones_b[:, :], 1.0)), reads=(), writes=["ones_b"])
            S.add("pool", (lambda e: e.memset(zcol[:, :, :], 0.0)), reads=(), writes=["zcol"])
            S.add("pool", (lambda e: e.memset(eps_t[:, :], EPS)), reads=(), writes=["eps_t"])
            S.add("sp", (lambda e: e.dma_start(out=ident_f[:, :], in_=ident_in[:, :])), reads=(), writes=["ident_f"], dsem="misc")
            S.add("sp", (lambda e: e.dma_start(out=hmask[:, :], in_=hmask_in[:, :])), reads=(), writes=["hmask"], dsem="misc")
            S.add("sp", (lambda e: e.dma_start(out=stage[:, :, :], in_=smallp.rearrange("(a p) f -> p a f", p=128))),
                  reads=(), writes=[("stage", i) for i in range(5)], dsem="misc")
            S.add("dve", (lambda e: e.tensor_copy(out=ident_b[:, :], in_=ident_f[:, :])), reads=["ident_f"], writes=["ident_b"])
            for a in range(5):
                bk = ps_next()
                S.add("pe", (lambda e, a=a, bk=bk: e.transpose(out=psum[bk][:, 0:128], in_=stage[:, a, :], identity=ident_f[:, :])),
                      reads=[("stage", a), "ident_f"], writes=[("ps", bk)])
                S.add("dve", (lambda e, a=a, bk=bk: e.tensor_copy(out=smallT[:, a * 128:(a + 1) * 128], in_=psum[bk][:, 0:128])),
                      reads=[("ps", bk)], writes=["smallT"])
            def prep(dst, src):
                S.add("pool", (lambda e, dst=dst, src=src: e.dma_start(out=dst, in_=src)),
                      reads=(), writes=["wprep"], dsem="prep")
            for li in range(DEPTH):
                for fi in range(2):
                    for fb in range(11):
                        prep(wg_s[li, fi, fb], w_gate[li, fi][:, fb * 256:(fb + 1) * 256].rearrange("(k p) f -> p k f", p=128))
                        prep(wu_s[li, fi, fb], w_up[li, fi][:, fb * 256:(fb + 1) * 256].rearrange("(k p) f -> p k f", p=128))
                    for db in range(4):
                        prep(wd_s[li, fi, db], w_down[li, fi][:, db * 256:(db + 1) * 256].rearrange("(k p) f -> p k f", p=128))
            for j in range(2):
                for ub in range(12):
                    prep(sgu_s[j, ub], sg_w_in[j][:, ub * 256:(ub + 1) * 256].rearrange("(k p) f -> p k f", p=128))
                for vb in range(6):
                    prep(sgv_s[j, vb], sg_w_in[j][:, SGH + vb * 512:SGH + (vb + 1) * 512].rearrange("(k p) f -> p k f", p=128))
                for db in range(4):
                    prep(sgo_s[j, db], sg_w_out[j][:, db * 256:(db + 1) * 256].rearrange("(k p) f -> p k f", p=128))
            for cb in range(6):
                prep(qkv_s[cb], da_w_qkv[0][:, cb * 512:(cb + 1) * 512].rearrange("(k p) f -> p k f", p=128))
            for db in range(4):
                prep(wo_s[db], da_w_out[0][:, db * 256:(db + 1) * 256].rearrange("(k p) f -> p k f", p=128))
                prep(cout_s[db], conv_w_out[0][:, db * 256:(db + 1) * 256].rearrange("(k p) f -> p k f", p=128))
                for sct in range(3):
                    prep(cin_s[db][:, :, sct, :],
                         conv_w_in[0][:, sct * D + db * 256: sct * D + (db + 1) * 256].rearrange("(k p) f -> p k f", p=128))
            for bb in range(3):
                S.add("act", (lambda e, bb=bb: e.activation(out=cact[:, :, bb], in_=smallT[:, C_C + bb * 8: C_C + bb * 8 + 8],
                                                            func=AF.Silu)),
                      reads=["smallT"], writes=["cact"])
            for li in range(DEPTH):
                bk = ps_next()
                for nb in range(18):
                    sidx = nb % 2
                    blk = big[:, sidx * 8192:(sidx + 1) * 8192].bitcast(F32).rearrange("p (k f) -> p k f", f=512)
                    S.add("sp", (lambda e, blk=blk, li=li, nb=nb: e.dma_start(
                        out=blk, in_=ada_w[li][:, nb * 512:(nb + 1) * 512].rearrange("(k p) f -> p k f", p=128))),
                          reads=(), writes=[("adablk", sidx)], dsem="slot%d" % sidx)
                    for n4 in range(4):
                        n = nb * 4 + n4
                        for kc in range(KC):
                            S.add("pe", (lambda e, bk=bk, blk=blk, n=n, n4=n4, kc=kc: e.matmul(
                                out=psum[bk][:, n * 3:(n + 1) * 3], lhsT=blk[:, kc, n4 * 128:(n4 + 1) * 128],
                                rhs=cact[:, kc, :], start=(kc == 0), stop=(kc == KC - 1))),
                                  reads=[("adablk", sidx), "cact"], writes=[("ps", bk)])
                for bb in range(3):
                    S.add("dve", (lambda e, bk=bk, li=li, bb=bb: e.tensor_tensor(
                        out=modt[:, li, :, bb], in0=psum[bk][:, 0:216].rearrange("p (n b) -> p n b", b=3)[:, :, bb],
                        in1=smallT[:, C_ADAB + li * 72: C_ADAB + (li + 1) * 72], op=ALU.add)),
                          reads=[("ps", bk), "smallT"], writes=["modt"])
            for li in range(DEPTH):
                for sl in range(3):
                    for bb in range(3):
                        ng = smallT[:, C_NORMG + (li * 3 + sl) * 8: C_NORMG + (li * 3 + sl) * 8 + 8]
                        S.add("dve", (lambda e, li=li, sl=sl, bb=bb, ng=ng: e.scalar_tensor_tensor(
                            out=A_t[:, li, sl, bb, :], in0=modt[:, li, (3 * sl + 1) * 8:(3 * sl + 2) * 8, bb], scalar=1.0,
                            in1=ng, op0=ALU.add, op1=ALU.mult)),
                              reads=["modt", "smallT"], writes=["A_t"])
                        S.add("dve", (lambda e, li=li, sl=sl, bb=bb: e.tensor_copy(
                            out=B_t[:, li, sl, bb, :], in_=modt[:, li, (3 * sl) * 8:(3 * sl + 1) * 8, bb])),
                              reads=["modt"], writes=["B_t"])
                        S.add("dve", (lambda e, li=li, sl=sl, bb=bb: e.tensor_scalar(
                            out=G_t[:, li, sl, bb, :], in0=modt[:, li, (3 * sl + 2) * 8:(3 * sl + 3) * 8, bb],
                            scalar1=(1.0 if sl == 1 else 0.5), scalar2=None, op0=ALU.mult)),
                              reads=["modt"], writes=["G_t"])
            lam_init = 0.8 - 0.6 * math.exp(-0.3 * 1)
            S.add("sp", (lambda e: e.dma_start(out=lamrow[:, :], in_=da_lambda[0:1].rearrange("a r d -> a (r d)"))),
                  reads=(), writes=["lamrow"], dsem="misc")
            S.add("dve", (lambda e: e.tensor_tensor(out=lamrow[:, 0:64], in0=lamrow[:, 0:64], in1=lamrow[:, 64:128], op=ALU.mult)),
                  reads=["lamrow"], writes=["lamrow"])
            S.add("dve", (lambda e: e.tensor_tensor(out=lamrow[:, 128:192], in0=lamrow[:, 128:192], in1=lamrow[:, 192:256], op=ALU.mult)),
                  reads=["lamrow"], writes=["lamrow"])
            S.add("dve", (lambda e: e.reduce_sum(out=lamw[:, 0:1], in_=lamrow[:, 0:64], axis=AX.X)),
                  reads=["lamrow"], writes=["lamw"])
            S.add("dve", (lambda e: e.reduce_sum(out=lamw[:, 1:2], in_=lamrow[:, 128:192], axis=AX.X)),
                  reads=["lamrow"], writes=["lamw"])
            S.add("act", (lambda e: e.activation(out=lamw[:, 2:4], in_=lamw[:, 0:2], func=AF.Exp)),
                  reads=["lamw"], writes=["lamw"])
            S.add("dve", (lambda e: e.tensor_tensor(out=lamw[:, 4:5], in0=lamw[:, 3:4], in1=lamw[:, 2:3], op=ALU.subtract)),
                  reads=["lamw"], writes=["lamw"])
            S.add("dve", (lambda e: e.tensor_scalar(out=lamw[:, 5:6], in0=lamw[:, 4:5], scalar1=-lam_init, scalar2=None, op0=ALU.add)),
                  reads=["lamw"], writes=["lamw"])
            bk = ps_next()
            S.add("pe", (lambda e, bk=bk: e.matmul(out=psum[bk][:, 0:1], lhsT=ones_f[0:1, :], rhs=lamw[0:1, 5:6], start=True, stop=True)),
                  reads=["lamw", "ones_f"], writes=[("ps", bk)])
            S.add("dve", (lambda e, bk=bk: e.tensor_copy(out=neglam[:, :], in_=psum[bk][:, 0:1])),
                  reads=[("ps", bk)], writes=["neglam"])
            S.add("dve", (lambda e: e.tensor_scalar(out=sublng[:, :], in0=smallT[:, C_SUBLN:C_SUBLN + 1], scalar1=1.0 - lam_init,
                                                    scalar2=None, op0=ALU.mult)),
                  reads=["smallT"], writes=["sublng"])
            for r in range(2):
                n = geo.regions[r][2] * T
                S.add("pool", (lambda e, r=r: e.dma_start(out=g_s[r][:, :, 0:1], in_=zcol[:, :, :], allow_slow_non_contiguous=True)),
                      reads=["zcol"], writes=[("g_s", r, "lo")], dsem="misc")
                S.add("pool", (lambda e, r=r, n=n: e.dma_start(out=g_s[r][:, :, n + 1:n + 2], in_=zcol[:, :, :], allow_slow_non_contiguous=True)),
                      reads=["zcol"], writes=[("g_s", r, "hi")], dsem="misc")


        def final_out2(S, own_idx):
            fin_buf = big_f32(0, 16, T)
            norm_mod(S, 0, 0, 0, hout=fin_buf, final=True)
            otm = big_f32(16, 16, D)
            for st in range(4):
                for half in range(2):
                    bk = ps_next()
                    for c4 in range(4):
                        c = half * 4 + c4
                        S.add("pe", (lambda e, bk=bk, c=c, c4=c4, st=st: e.transpose(
                            out=psum[bk][:, c4 * 128:(c4 + 1) * 128], in_=fin_buf[:, c, st * 128:(st + 1) * 128],
                            identity=ident_f[:, :])),
                              reads=bigrows(2 * c, 2) + ["ident_f"], writes=[("ps", bk)])
                    rws = bigrows(16 + 4 * st + 2 * half, 2)
                    if half == 0:
                        S.add("act", (lambda e, bk=bk, st=st: e.activation(out=otm[:, st, 0:512], in_=psum[bk][:, :], func=AF.Copy)),
                              reads=[("ps", bk)], writes=rws)
                    else:
                        S.add("dve", (lambda e, bk=bk, st=st: e.tensor_copy(out=otm[:, st, 512:1024], in_=psum[bk][:, :])),
                              reads=[("ps", bk)], writes=rws)
            S.add("pool", (lambda e: e.dma_start(out=y_out[own_idx * T:(own_idx + 1) * T, :].rearrange("(s p) d -> p s d", p=128),
                                                 in_=otm)),
                  reads=bigrows(16, 16), writes=[("y", own_idx)], dsem="out")

        def tile_batch(tile_idx):
            if tile_idx < geo.ntP:
                return 0
            if tile_idx < 2 * geo.ntP:
                return 1
            return 2

        import os as _os
        STOP = int(_os.environ.get("KSTOP", "99"))

        def dump_x(S):
            S.add("pool", (lambda e: e.dma_start(out=dbg_out, in_=x[:, :, :])),
                  reads=[("x", c) for c in range(KC)], writes=["dbg"], dsem="out")

        def program(S, ws):
            _program(S, ws)
            if STOP != 99:
                dump_x(S)

        def _program(S, ws):
            setup(S)
            if STOP == 0:
                return
            gmlp_setup(S, 0)
            if STOP == 1:
                return
            S.barrier()
            for ti in range(geo.ntA):
                b = tile_batch(ti)
                load_x_tokens(S, ti)
                if STOP == 2:
                    return
                ffn(S, ws, 0, 0, b, 0)
                if STOP == 3:
                    return
                gmlp(S, ws, 0, 0, b)
                if STOP == 4:
                    return
                ffn(S, ws, 0, 1, b, 2)
                ffn(S, ws, 1, 0, b, 0)
                qkv_rotary(S, ws, ti, b)
                if STOP == 5:
                    return
            S.barrier()
            if STOP == 6:
                return
            btiles = [(ti, r, i, oi) for oi, (ti, r, i) in enumerate(geo.own)] + [(geo.halo_tile, 2, -1, -1)]
            for (ti, r, tin, oi) in btiles:
                b = r
                S.add("sp", (lambda e, ti=ti: e.dma_start(out=x[:, :, :], in_=xa_s[ti])),
                      reads=[("xa_s", ti)], writes=[("x", c) for c in range(KC)], dsem="x")
                if STOP == 10:
                    return
                attention(S, ws, ti, r, b)
                if STOP == 7:
                    return
                ffn(S, ws, 1, 1, b, 2)
                ffn(S, ws, 2, 0, b, 0)
                conv_in(S, ws, b, oi, r, tin, tin < 0)
                if STOP == 8:
                    return
            S.barrier()
            gmlp_setup(S, 1)
            S.barrier()
            for oi, (ti, r, tin) in enumerate(geo.own):
                b = r
                conv_mix(S, ws, b, oi, r, tin)
                if STOP == 9:
                    return
                ffn(S, ws, 2, 1, b, 2)
                ffn(S, ws, 3, 0, b, 0)
                gmlp(S, ws, 3, 1, b)
                ffn(S, ws, 3, 1, b, 2)
                final_out2(S, oi)
                if STOP == 11:
                    return

        rec = WStream(None)
        S0 = Sched()
        ps_rr[0] = 0
        program(S0, rec)
        ps_rr[0] = 0
        ws = WStream(rec.out)
        program(S, ws)
        S.emit(nc, block, sems, dsems)
    return nc


def _run(inputs, S_P, S_S):
    geo = Geo(S_P, S_S)
    f32 = np.float32
    xp = np.asarray(inputs["x_prompt"], f32)
    xs = np.asarray(inputs["x_sample"], f32)
    cp = np.asarray(inputs["c_prompt"], f32)
    cs = np.asarray(inputs["c_sample"], f32)
    assert xp.shape == (16, S_P, D) and xs.shape == (2, S_S, D)
    nc = build_program(geo)
    ident = np.eye(128, dtype=f32)
    shared = {k: np.ascontiguousarray(np.asarray(inputs[k], f32)) for k in
              ["ada_w", "ffn_w_gate", "ffn_w_up", "ffn_w_down", "sg_w_in", "sg_ln_g", "sg_ln_b", "sg_w_s", "sg_b_s",
               "sg_b_in", "sg_w_out", "da_w_qkv", "da_lambda", "da_w_out", "conv_w_in", "conv_w_out"]}
    in_maps = []
    orders = []
    for core in range(NCORES):
        sseq = core // 4
        rank = core % 4
        order = geo.sample_order(rank)
        orders.append(order)
        xs_perm = xs[sseq].reshape(S_S // 128, 128, D)[order].reshape(S_S, D)
        xA = np.concatenate([xp[2 * core], xp[2 * core + 1], xs_perm], axis=0)
        pos_s = (np.asarray(order)[:, None] * 128 + np.arange(128)[None, :]).reshape(-1)
        pos = np.concatenate([np.arange(S_P), np.arange(S_P), pos_s])
        rope = _rope_table(pos)
        small = np.zeros((640, 128), f32)
        small[0:288] = np.asarray(inputs["ada_b"], f32).reshape(288, 128)
        small[288:384] = np.asarray(inputs["norm_g"], f32).reshape(96, 128)
        small[384:480] = np.asarray(inputs["sg_b_in"], f32).reshape(96, 128)
        small[480:504] = np.asarray(inputs["conv_kernel"], f32).reshape(24, 128)
        small[504:512] = np.asarray(inputs["final_norm_g"], f32).reshape(8, 128)
        small[512:513] = np.asarray(inputs["da_subln_g"], f32).reshape(1, 128)
        small[513:521] = cp[2 * core].reshape(8, 128)
        small[521:529] = cp[2 * core + 1].reshape(8, 128)
        small[529:537] = cs[sseq].reshape(8, 128)
        hm = np.zeros((128, 2), f32)
        hm[:, 0] = 0.0 if rank == 0 else 1.0
        hm[:, 1] = 0.0 if rank == 3 else 1.0
        m = {"xA": np.ascontiguousarray(xA), "rope": rope, "smallp": small, "ident": ident, "hmask": hm}
        m.update(shared)
        in_maps.append(m)
    res = run_bass_kernel_spmd(nc, in_maps, core_ids=list(range(NCORES)))
    global _last_res
    _last_res = res
    yp = np.empty((16, S_P, D), f32)
    ys = np.empty((2, S_S, D), f32)
    for core in range(NCORES):
        y = np.asarray(res.results[core]["y"], f32)
        yp[2 * core] = y[0:S_P]
        yp[2 * core + 1] = y[S_P:2 * S_P]
        rank = core % 4
        ys[core // 4, rank * geo.Q:(rank + 1) * geo.Q] = y[2 * S_P:2 * S_P + geo.Q]
    return yp, ys


def kernel(**inputs):
    S_P = int(np.asarray(inputs["x_prompt"]).shape[1])
    S_S = int(np.asarray(inputs["x_sample"]).shape[1])
    return _run(inputs, S_P, S_S)
```

```python
import math
from contextlib import ExitStack

import numpy as np
import concourse.bass as bass
import concourse.mybir as mybir
from concourse.bass_utils import run_bass_kernel_spmd

F32 = mybir.dt.float32
BF16 = mybir.dt.bfloat16
AF = mybir.ActivationFunctionType
ALU = mybir.AluOpType
AX = mybir.AxisListType

D = 1024
KC = 8
FF = 2816
FC = 22
SGH = 3072
VC = 24
T = 512
NCORES = 8
EPS = 1e-6
DEPTH = 4
SLOT_ELEMS = 6144
NSLOTS = 4


class Sched:
    ENGS = ("pe", "act", "dve", "pool", "sp")

    def __init__(self):
        self.ops = []
        self.last_w = {}
        self.readers = {}
        self.pending_bar = {}

    def add(self, eng, fn, reads=(), writes=(), dsem=None):
        idx = len(self.ops)
        deps = set()
        lw = self.last_w
        rdrs = self.readers
        for r in reads:
            w = lw.get(r)
            if w is not None:
                deps.add(w)
        for w_ in writes:
            w = lw.get(w_)
            if w is not None:
                deps.add(w)
            rr = rdrs.get(w_)
            if rr:
                deps.update(rr)
        for r in reads:
            l = rdrs.get(r)
            if l is None:
                rdrs[r] = [idx]
            else:
                l.append(idx)
        for w_ in writes:
            lw[w_] = idx
            rdrs[w_] = []
        if dsem == "misc":
            w = lw.get("__misc_chain__")
            if w is not None:
                deps.add(w)
            lw["__misc_chain__"] = idx
        pb = self.pending_bar.pop(eng, None)
        if pb:
            deps.update(pb)
        self.ops.append([eng, fn, deps, dsem, False])
        return idx

    def barrier(self):
        last = {}
        for i, op in enumerate(self.ops):
            if op[3] is not None:
                last[("d", op[3])] = i
            else:
                last[("e", op[0])] = i
        s = set(last.values())
        for e in self.ENGS:
            self.pending_bar.setdefault(e, set()).update(s)
        self.last_w = {}
        self.readers = {}

    def emit(self, nc, block, sems, dsems):
        ops = self.ops
        for i, op in enumerate(ops):
            for j in op[2]:
                oj = ops[j]
                if oj[3] is None and oj[0] == "pe" and op[0] == "pe" and op[3] is None:
                    continue
                oj[4] = True
        last = {}
        for i, op in enumerate(ops):
            if op[3] is not None:
                last[("d", op[3])] = i
            else:
                last[("e", op[0])] = i
        for i in last.values():
            ops[i][4] = True
        cnt = {}
        val = [0] * len(ops)
        for i, op in enumerate(ops):
            if op[3] is not None:
                k = ("d", op[3])
                cnt[k] = cnt.get(k, 0) + 16
                val[i] = cnt[k]
            elif op[4]:
                k = ("e", op[0])
                cnt[k] = cnt.get(k, 0) + 1
                val[i] = cnt[k]
        per_eng = {e: [] for e in self.ENGS}
        for i, op in enumerate(ops):
            per_eng[op[0]].append(i)

        def semof(j):
            oj = ops[j]
            if oj[3] is not None:
                return ("d", oj[3]), dsems[oj[3]]
            return ("e", oj[0]), sems[oj[0]]

        def run_engine(ename, e):
            seen = {}
            for i in per_eng[ename]:
                op = ops[i]
                need = {}
                for j in op[2]:
                    oj = ops[j]
                    if oj[3] is None and oj[0] == "pe" and ename == "pe" and op[3] is None:
                        continue
                    k, sh = semof(j)
                    v = val[j]
                    if seen.get(k, 0) >= v:
                        continue
                    if k not in need or need[k][1] < v:
                        need[k] = (sh, v)
                for k, (sh, v) in need.items():
                    e.wait_ge(sh, v)
                    seen[k] = v
                ins = op[1](e)
                if op[3] is not None:
                    ins.then_inc(dsems[op[3]], 16)
                elif op[4]:
                    ins.then_inc(sems[ename], 1)
            if ename == "sp":
                for k, i in last.items():
                    _, sh = semof(i)
                    if seen.get(k, 0) < val[i]:
                        e.wait_ge(sh, val[i])

        @block.tensor
        def _(e):
            run_engine("pe", e)

        @block.scalar
        def _(e):
            run_engine("act", e)

        @block.vector
        def _(e):
            run_engine("dve", e)

        @block.gpsimd
        def _(e):
            run_engine("pool", e)

        @block.sync
        def _(e):
            run_engine("sp", e)


class Geo:
    def __init__(self, S_P, S_S):
        self.S_P = S_P
        self.S_S = S_S
        self.Q = S_S // 4
        assert S_P % T == 0 and self.Q % T == 0
        self.NA = 2 * S_P + S_S
        self.NOWN = 2 * S_P + self.Q
        self.ntA = self.NA // T
        self.ntP = S_P // T
        self.ntQ = self.Q // T
        self.regions = [
            (0, self.ntP, self.ntP),
            (self.ntP, self.ntP, self.ntP),
            (2 * self.ntP, S_S // T, self.ntQ),
        ]
        self.halo_tile = 2 * self.ntP + self.ntQ
        self.own = []
        for r, (t0, nk, no) in enumerate(self.regions):
            for i in range(no):
                self.own.append((t0 + i, r, i))

    def sample_order(self, rank):
        nch = self.S_S // 128
        qch = self.Q // 128
        own = list(range(rank * qch, (rank + 1) * qch))
        nxt = ((rank + 1) * qch) % nch
        prv = (rank * qch - 1) % nch
        rest = [c for c in range(nch) if c not in own and c != nxt and c != prv]
        return own + [nxt, prv] + rest


def _rope_table(positions):
    inv_freq = (500000.0 ** (-(np.arange(0, 16, 2, dtype=np.float32)) / np.float32(16))).astype(np.float32)
    ang = positions.astype(np.float32)[:, None] * inv_freq[None, :]
    ang = ang.astype(np.float32)
    return np.concatenate([np.cos(ang), np.sin(ang)], axis=1).astype(np.float32)


def build_program(geo):
    nc = bass.Bass("TRN2", target_bir_lowering=False)
    NA, NOWN = geo.NA, geo.NOWN
    ntA = geo.ntA

    def din(name, shape, dt=F32):
        return nc.dram_tensor(name, list(shape), dt, kind="ExternalInput").ap()

    def dscr(name, shape, dt):
        return nc.dram_tensor(name, list(shape), dt, kind="Internal").ap()

    xA = din("xA", [NA, D])
    rope = din("rope", [NA, 16])
    smallp = din("smallp", [640, 128])
    ident_in = din("ident", [128, 128])
    hmask_in = din("hmask", [128, 2])
    ada_w = din("ada_w", [DEPTH, D, 9 * D])
    w_gate = din("ffn_w_gate", [DEPTH, 2, D, FF])
    w_up = din("ffn_w_up", [DEPTH, 2, D, FF])
    w_down = din("ffn_w_down", [DEPTH, 2, FF, D])
    sg_w_in = din("sg_w_in", [2, D, 6144])
    sg_ln_g = din("sg_ln_g", [2, SGH])
    sg_ln_b = din("sg_ln_b", [2, SGH])
    sg_w_s = din("sg_w_s", [2, 8, 128, 128])
    sg_b_s = din("sg_b_s", [2, 8, 128])
    sg_b_in = din("sg_b_in", [2, 6144])
    sg_w_out = din("sg_w_out", [2, SGH, D])
    da_w_qkv = din("da_w_qkv", [1, D, 3 * D])
    da_lambda = din("da_lambda", [1, 4, 64])
    da_w_out = din("da_w_out", [1, D, D])
    conv_w_in = din("conv_w_in", [1, D, 3 * D])
    conv_w_out = din("conv_w_out", [1, D, D])
    y_out = nc.dram_tensor("y", [NOWN, D], F32, kind="ExternalOutput").ap()
    import os as _os2
    DBG = int(_os2.environ.get("KSTOP", "99")) != 99
    dbg_out = nc.dram_tensor("dbg", [128, KC, T], F32, kind="ExternalOutput").ap() if DBG else None

    wg_s = dscr("wg_s", [DEPTH, 2, 11, 128, KC, 256], BF16)
    wu_s = dscr("wu_s", [DEPTH, 2, 11, 128, KC, 256], BF16)
    wd_s = dscr("wd_s", [DEPTH, 2, 4, 128, FC, 256], BF16)
    sgu_s = dscr("sgu_s", [2, 12, 128, KC, 256], BF16)
    sgv_s = dscr("sgv_s", [2, 6, 128, KC, 512], BF16)
    sgo_s = dscr("sgo_s", [2, 4, 128, VC, 256], BF16)
    qkv_s = dscr("qkv_s", [6, 128, KC, 512], BF16)
    wo_s = dscr("wo_s", [4, 128, KC, 256], BF16)
    cin_s = dscr("cin_s", [4, 128, KC, 3, 256], BF16)
    cout_s = dscr("cout_s", [4, 128, KC, 256], BF16)
    xa_s = dscr("xa_s", [ntA, 128, KC, T], F32)
    qT_s = dscr("qT_s", [ntA, 128, KC, T], BF16)
    kT_s = dscr("kT_s", [8, 128, NA], BF16)
    v_s = dscr("v_s", [NA, D], BF16)
    nown_t = len(geo.own)
    xb_s = dscr("xb_s", [nown_t, 128, KC, T], F32)
    bg_s = dscr("bg_s", [nown_t, 128, KC, T], F32)
    g_s = [dscr(f"g_s{r}", [128, KC, geo.regions[r][2] * T + 2], F32) for r in range(3)]

    es = ExitStack()
    with es:
        def sb(name, shape, dt=F32):
            return es.enter_context(nc.sbuf_tensor("sb_" + name, list(shape), dt))

        sems = {e: es.enter_context(nc.semaphore("sem_" + e)) for e in Sched.ENGS}
        dsem_names = (["slot%d" % i for i in range(NSLOTS)] +
                      ["x", "qt", "st_x", "st_q", "st_k", "st_v", "misc", "big", "prep", "rope", "gw", "bgl", "out"])
        dsems = {n: es.enter_context(nc.semaphore("dsem_" + n)) for n in dsem_names}
        block = es.enter_context(nc.Block())

        S = Sched()
        psum = [es.enter_context(nc.psum_tensor("ps%d" % b, [128, 512], F32)) for b in range(8)]
        ps_rr = [0]

        def ps_next(banks=(0, 1, 2, 3, 4, 5, 6, 7)):
            b = banks[ps_rr[0] % len(banks)]
            ps_rr[0] += 1
            return b

        x = sb("x", [128, KC, T])
        h = sb("h", [128, KC, T], BF16)
        big = sb("big", [128, 48 * 512], BF16)
        slots = [sb("slot%d" % i, [128, SLOT_ELEMS], BF16) for i in range(NSLOTS)]
        rstd = sb("rstd", [128, T])
        tmpA = sb("tmpA", [128, T])
        tmpB = sb("tmpB", [128, T])
        tmpC = sb("tmpC", [128, T])
        ones_f = sb("ones_f", [128, 128])
        ones_b = sb("ones_b", [128, 128], BF16)
        ident_f = sb("ident_f", [128, 128])
        ident_b = sb("ident_b", [128, 128], BF16)
        smallT = sb("smallT", [128, 640])
        stage = sb("stage", [128, 5, 128])
        modt = sb("modt", [128, DEPTH, 72, 3])
        A_t = sb("A_t", [128, DEPTH, 3, 3, KC])
        B_t = sb("B_t", [128, DEPTH, 3, 3, KC])
        G_t = sb("G_t", [128, DEPTH, 3, 3, KC])
        cact = sb("cact", [128, KC, 3])
        neglam = sb("neglam", [128, 1])
        sublng = sb("sublng", [128, 1])
        hmask = sb("hmask", [128, 2])
        lamrow = sb("lamrow", [1, 256])
        lamw = sb("lamw", [1, 8])
        vg = sb("vg", [128, SGH])
        vn = [sb("vn%d" % i, [128, SGH], BF16) for i in range(2)]
        biasT = sb("biasT", [128, VC, 128])
        gsc = sb("gsc", [128, VC])
        wsT = sb("wsT", [128, 8, 128], BF16)
        bvrow = sb("bvrow", [1, SGH], BF16)
        bn6 = sb("bn6", [128, 6, 6])
        mv = sb("mv", [128, 4])
        ropet = sb("ropet", [128, 4, 16])
        pT = [sb("pT%d" % i, [128, T], BF16) for i in range(4)]
        spt_bufs = [sb("spt%d" % i, [128, T]) for i in range(2)]
        sqb = [sb("sqb%d" % i, [128, T]) for i in range(2)]
        pre_state = {"bank": None}
        lnb_fm = sb("lnb_fm", [128, VC])
        ob = sb("ob", [128, KC, T], BF16)
        zcol = sb("zcol", [128, KC, 1])
        eps_t = sb("eps_t", [128, 1])
        hcol = sb("hcol", [128, KC, 2])

        C_ADAB, C_NORMG, C_BIN, C_CONVK, C_FING, C_SUBLN, C_C = 0, 288, 384, 480, 504, 512, 513

        def bigrows(r0, n):
            return [("big", r) for r in range(r0, r0 + n)]

        def big_bf(r0, n):
            return big[:, r0 * 512:(r0 + n) * 512].rearrange("p (f t) -> p f t", t=512)

        def big_f32(r0, nrows, inner):
            v = big[:, r0 * 512:(r0 + nrows) * 512].bitcast(F32)
            return v.rearrange("p (a b) -> p a b", b=inner)

        class WStream:
            def __init__(self, rec=None):
                self.rec = rec
                self.out = []
                self.i = 0
                self.issued = 0

            def _issue(self, S, idx):
                req, rds = self.rec[idx]
                s = idx % NSLOTS
                for (off, n, src, dshape) in req:
                    dst = slots[s][:, off:off + n]
                    if dshape is not None:
                        dst = dst.rearrange(dshape[0], **dshape[1])
                    S.add("sp", (lambda e, dst=dst, src=src: e.dma_start(out=dst, in_=src)),
                          reads=rds, writes=[("slot", s)], dsem="slot%d" % s)

            def next(self, S, req, rds=()):
                idx = self.i
                self.i += 1
                if self.rec is None:
                    self.out.append((req, list(rds)))
                    return slots[idx % NSLOTS], idx % NSLOTS
                while self.issued < len(self.rec) and self.issued <= idx + NSLOTS - 2:
                    self._issue(S, self.issued)
                    self.issued += 1
                return slots[idx % NSLOTS], idx % NSLOTS

        def mm(S, out, lhsT, rhs, start, stop, reads, bank):
            S.add("pe", (lambda e: e.matmul(out=out, lhsT=lhsT, rhs=rhs, start=start, stop=stop)),
                  reads=reads, writes=[("ps", bank)])

        def stats_mm(S, c, sbank):
            mm(S, psum[sbank][:, :], ones_f[:, :], sqb[c % 2][:, :], c == 0, c == KC - 1, [("sqb", c % 2)], sbank)

        def resid(S, bk, dc, g_ap, sbank):
            S.add("dve", (lambda e, bk=bk, dc=dc, g_ap=g_ap: e.scalar_tensor_tensor(
                out=x[:, dc, :], in0=psum[bk][:, :], scalar=g_ap, in1=x[:, dc, :],
                op0=ALU.mult, op1=ALU.add)),
                  reads=[("ps", bk), ("x", dc)], writes=[("x", dc)])
            S.add("act", (lambda e, dc=dc: e.activation(out=sqb[dc % 2][:, :], in_=x[:, dc, :], func=AF.Square)),
                  reads=[("x", dc)], writes=[("sqb", dc % 2)])

        def norm_mod(S, li, sl, b, hout=None, final=False):
            if pre_state["bank"] is not None:
                bk = pre_state["bank"]
                pre_state["bank"] = None
            else:
                sq = big_f32(0, 16, T)
                for c in range(KC):
                    S.add("act", (lambda e, c=c: e.activation(out=sq[:, c, :], in_=x[:, c, :], func=AF.Square)),
                          reads=[("x", c)], writes=bigrows(2 * c, 2))
                bk = ps_next()
                for c in range(KC):
                    mm(S, psum[bk][:, :], ones_f[:, :], sq[:, c, :], c == 0, c == KC - 1,
                       bigrows(2 * c, 2), bk)
            S.add("act", (lambda e: e.activation(out=tmpA[:, :], in_=psum[bk][:, :], func=AF.Sqrt,
                                                 bias=eps_t[:, 0:1], scale=1.0 / D)),
                  reads=[("ps", bk), "eps_t"], writes=["tmpA"])
            S.add("dve", (lambda e: e.reciprocal(out=rstd[:, :], in_=tmpA[:, :])),
                  reads=["tmpA"], writes=["rstd"])
            for c in range(KC):
                if final:
                    a_ap = smallT[:, C_FING + c:C_FING + c + 1]
                    S.add("dve", (lambda e, c=c, a_ap=a_ap: e.scalar_tensor_tensor(
                        out=hout[:, c, :], in0=x[:, c, :], scalar=a_ap, in1=rstd[:, :],
                        op0=ALU.mult, op1=ALU.mult)),
                          reads=[("x", c), "rstd"], writes=bigrows(2 * c, 2))
                    continue
                tt = tmpB if c % 2 == 0 else tmpC
                tk = "tmpB" if c % 2 == 0 else "tmpC"
                a_ap = A_t[:, li, sl, b, c:c + 1]
                b_ap = B_t[:, li, sl, b, c:c + 1]
                S.add("dve", (lambda e, c=c, tt=tt, a_ap=a_ap: e.scalar_tensor_tensor(
                    out=tt[:, :], in0=x[:, c, :], scalar=a_ap, in1=rstd[:, :],
                    op0=ALU.mult, op1=ALU.mult)),
                      reads=[("x", c), "rstd"], writes=[tk])
                S.add("act", (lambda e, c=c, tt=tt, b_ap=b_ap: e.activation(
                    out=h[:, c, :], in_=tt[:, :], func=AF.Identity, bias=b_ap, scale=1.0)),
                      reads=[tk], writes=[("h", c)])

        def ffn(S, ws, li, fi, b, sl):
            norm_mod(S, li, sl, b)
            hid = big_bf(0, FC)
            HB = (0, 1, 2, 3)
            for fb in range(11):
                sl_t, sidx = ws.next(S, [(0, 2048, wg_s[li, fi, fb], ("p (k f) -> p k f", dict(f=256))),
                                         (2048, 2048, wu_s[li, fi, fb], ("p (k f) -> p k f", dict(f=256)))])
                wgv = sl_t[:, 0:2048].rearrange("p (k f) -> p k f", f=256)
                wuv = sl_t[:, 2048:4096].rearrange("p (k f) -> p k f", f=256)
                pre_banks = None
                if fb == 0:
                    pre_banks = [ps_next(HB) for _ in range(4)]
                    for kc in range(KC):
                        for gi, (wv_, fcl_) in enumerate(((wgv, 0), (wuv, 0), (wgv, 1), (wuv, 1))):
                            mm(S, psum[pre_banks[gi]][:, :], wv_[:, kc, fcl_ * 128:(fcl_ + 1) * 128], h[:, kc, :],
                               kc == 0, kc == KC - 1, [("slot", sidx), ("h", kc)], pre_banks[gi])
                for fcl in range(2):
                    f = fb * 2 + fcl
                    if pre_banks is not None:
                        bg = pre_banks[2 * fcl]
                        bu = pre_banks[2 * fcl + 1]
                    else:
                        bg = ps_next(HB)
                        for kc in range(KC):
                            mm(S, psum[bg][:, :], wgv[:, kc, fcl * 128:(fcl + 1) * 128], h[:, kc, :],
                               kc == 0, kc == KC - 1, [("slot", sidx), ("h", kc)], bg)
                        bu = ps_next(HB)
                        for kc in range(KC):
                            mm(S, psum[bu][:, :], wuv[:, kc, fcl * 128:(fcl + 1) * 128], h[:, kc, :],
                               kc == 0, kc == KC - 1, [("slot", sidx), ("h", kc)], bu)
                    tt = tmpB if f % 2 == 0 else tmpC
                    tk = "tmpB" if f % 2 == 0 else "tmpC"
                    S.add("act", (lambda e, bg=bg, tt=tt: e.activation(out=tt[:, :], in_=psum[bg][:, :], func=AF.Silu)),
                          reads=[("ps", bg)], writes=[tk])
                    S.add("dve", (lambda e, bu=bu, tt=tt, f=f: e.tensor_tensor(
                        out=hid[:, f, :], in0=psum[bu][:, :], in1=tt[:, :], op=ALU.mult)),
                          reads=[("ps", bu), tk], writes=bigrows(f, 1))
            DB = (4, 5, 6, 7)
            for db in range(4):
                sl_t, sidx = ws.next(S, [(0, FC * 256, wd_s[li, fi, db], ("p (k f) -> p k f", dict(f=256)))])
                wdv = sl_t[:, 0:FC * 256].rearrange("p (k f) -> p k f", f=256)
                for dcl in range(2):
                    dc = db * 2 + dcl
                    bk = ps_next(DB)
                    for f in range(FC):
                        mm(S, psum[bk][:, :], wdv[:, f, dcl * 128:(dcl + 1) * 128], hid[:, f, :],
                           f == 0, f == FC - 1, [("slot", sidx)] + bigrows(f, 1), bk)
                    if dc >= 1:
                        stats_mm(S, dc - 1, 0)
                    g_ap = G_t[:, li, sl, b, dc:dc + 1]
                    resid(S, bk, dc, g_ap, 0)
            stats_mm(S, KC - 1, 0)
            pre_state["bank"] = 0

        def gmlp_setup(S, j):
            S.add("sp", (lambda e: e.dma_start(out=stage[0:24, 0, :], in_=sg_ln_g[j].rearrange("(a f) -> a f", f=128))),
                  reads=(), writes=[("stage", 0)], dsem="misc")
            S.add("sp", (lambda e: e.dma_start(out=stage[0:24, 1, :], in_=sg_ln_b[j].rearrange("(a f) -> a f", f=128))),
                  reads=(), writes=[("stage", 1)], dsem="misc")
            bk = ps_next()
            S.add("pe", (lambda e, bk=bk: e.transpose(out=psum[bk][:, 0:24], in_=stage[0:24, 0, :], identity=ident_f[0:24, 0:24])),
                  reads=[("stage", 0), "ident_f"], writes=[("ps", bk)])
            S.add("dve", (lambda e, bk=bk: e.tensor_copy(out=gsc[:, :], in_=psum[bk][:, 0:24])),
                  reads=[("ps", bk)], writes=["gsc"])
            bk2 = ps_next()
            S.add("pe", (lambda e, bk2=bk2: e.transpose(out=psum[bk2][:, 0:24], in_=stage[0:24, 1, :], identity=ident_f[0:24, 0:24])),
                  reads=[("stage", 1), "ident_f"], writes=[("ps", bk2)])
            S.add("dve", (lambda e, bk2=bk2: e.tensor_copy(out=lnb_fm[:, :], in_=psum[bk2][:, 0:24])),
                  reads=[("ps", bk2)], writes=["lnb_fm"])
            S.add("sp", (lambda e: e.dma_start(out=vg[:, 0:1024],
                                               in_=sg_b_s[j].rearrange("g q -> (g q)").partition_broadcast(128))),
                  reads=(), writes=[("vg", 0), ("vg", 1)], dsem="misc")
            for g in range(8):
                sidx = 2 + g % 2
                S.add("sp", (lambda e, g=g, sidx=sidx: e.dma_start(out=stage[:, sidx, :], in_=sg_w_s[j, g])),
                      reads=(), writes=[("stage", sidx)], dsem="misc")
                bk = ps_next()
                S.add("pe", (lambda e, sidx=sidx, bk=bk: e.transpose(out=psum[bk][:, 0:128], in_=stage[:, sidx, :],
                                                                      identity=ident_f[:, :])),
                      reads=[("stage", sidx), "ident_f"], writes=[("ps", bk)])
                S.add("act", (lambda e, bk=bk: e.activation(out=stage[:, 4, :], in_=psum[bk][:, 0:128], func=AF.Copy)),
                      reads=[("ps", bk)], writes=[("stage", 4)])
                S.add("dve", (lambda e, g=g: e.tensor_copy(out=wsT[:, g, :], in_=stage[:, 4, :])),
                      reads=[("stage", 4)], writes=["wsT"])
                bk3 = ps_next()
                S.add("pe", (lambda e, bk3=bk3: e.matmul(out=psum[bk3][:, 0:128], lhsT=ones_f[:, :], rhs=stage[:, 4, :],
                                                          start=True, stop=True)),
                      reads=[("stage", 4), "ones_f"], writes=[("ps", bk3)])
                for dcl in range(3):
                    vc = g * 3 + dcl
                    S.add("dve", (lambda e, g=g, vc=vc, bk3=bk3: e.scalar_tensor_tensor(
                        out=biasT[:, vc, :], in0=psum[bk3][:, 0:128], scalar=lnb_fm[:, vc:vc + 1],
                        in1=vg[:, g * 128:(g + 1) * 128], op0=ALU.mult, op1=ALU.add)),
                          reads=[("ps", bk3), "lnb_fm", ("vg", 0), ("vg", 1)], writes=[("biasT", vc)])
            S.add("sp", (lambda e: e.dma_start(out=vg[0:1, :], in_=sg_b_in[j:j + 1, SGH:2 * SGH])),
                  reads=[("biasT", v_) for v_ in range(VC)], writes=[("vg", i_) for i_ in range(6)], dsem="misc")
            S.add("dve", (lambda e: e.tensor_copy(out=bvrow[:, :], in_=vg[0:1, :])),
                  reads=[("vg", i_) for i_ in range(6)], writes=["bvrow"])

        def gmlp(S, ws, li, j, b):
            norm_mod(S, li, 1, b)
            u = big_bf(0, VC)
            m = big_bf(24, VC)
            UB = (0, 1)
            for ub in range(12):
                sl_t, sidx = ws.next(S, [(0, 2048, sgu_s[j, ub], None)])
                wv = sl_t[:, 0:2048].rearrange("p (k f) -> p k f", f=256)
                for fcl in range(2):
                    fcx = ub * 2 + fcl
                    bk = ps_next(UB)
                    for kc in range(KC):
                        mm(S, psum[bk][:, :], wv[:, kc, fcl * 128:(fcl + 1) * 128], h[:, kc, :],
                           kc == 0, kc == KC - 1, [("slot", sidx), ("h", kc)], bk)
                    bias_ap = smallT[:, C_BIN + j * 48 + fcx:C_BIN + j * 48 + fcx + 1]
                    S.add("act", (lambda e, bk=bk, fcx=fcx, bias_ap=bias_ap: e.activation(
                        out=u[:, fcx, :], in_=psum[bk][:, :], func=AF.Gelu, bias=bias_ap, scale=1.0)),
                          reads=[("ps", bk)], writes=bigrows(fcx, 1))
            VB = (2, 3, 4, 5)
            SB_ = (6, 7)
            def vpart(st):
                vnb = vn[st % 2]
                vnk = "vn%d" % (st % 2)
                for vb in range(6):
                    sl_t, sidx = ws.next(S, [(0, 4096, sgv_s[j, vb], None)])
                    wv = sl_t[:, 0:4096].rearrange("p (k f) -> p k f", f=512)
                    bk = ps_next(VB)
                    for kc in range(KC):
                        mm(S, psum[bk][:, :], h[:, kc, st * 128:(st + 1) * 128], wv[:, kc, :],
                           kc == 0, False, [("slot", sidx), ("h", kc)], bk)
                    mm(S, psum[bk][:, :], ones_b[0:1, :], bvrow[0:1, vb * 512:(vb + 1) * 512],
                       False, True, ["bvrow"], bk)
                    S.add("act", (lambda e, bk=bk, vb=vb: e.activation(
                        out=vg[:, vb * 512:(vb + 1) * 512], in_=psum[bk][:, :], func=AF.Gelu)),
                          reads=[("ps", bk)], writes=[("vg", vb)])
                    S.add("dve", (lambda e, vb=vb: e.bn_stats(out=bn6[:, vb, :], in_=vg[:, vb * 512:(vb + 1) * 512])),
                          reads=[("vg", vb)], writes=[("bn6", vb)])

            def lnpart(st):
                vnb = vn[st % 2]
                vnk = "vn%d" % (st % 2)
                S.add("dve", (lambda e: e.bn_aggr(out=mv[:, 0:2], in_=bn6[:, :, :])),
                      reads=[("bn6", i) for i in range(6)], writes=["mv"])
                S.add("act", (lambda e: e.activation(out=mv[:, 2:3], in_=mv[:, 1:2], func=AF.Sqrt,
                                                     bias=eps_t[:, 0:1], scale=1.0)),
                      reads=["mv", "eps_t"], writes=["mv2"])
                S.add("dve", (lambda e: e.reciprocal(out=mv[:, 2:3], in_=mv[:, 2:3])),
                      reads=["mv2"], writes=["mv2"])
                S.add("dve", (lambda e: e.scalar_tensor_tensor(out=mv[:, 3:4], in0=mv[:, 0:1], scalar=-1.0,
                                                               in1=mv[:, 2:3], op0=ALU.mult, op1=ALU.mult)),
                      reads=["mv", "mv2"], writes=["mv3"])
                for vb in range(6):
                    S.add("act", (lambda e, vb=vb, vnb=vnb: e.activation(
                        out=vnb[:, vb * 512:(vb + 1) * 512], in_=vg[:, vb * 512:(vb + 1) * 512],
                        func=AF.Identity, bias=mv[:, 3:4], scale=mv[:, 2:3])),
                          reads=[("vg", vb), "mv2", "mv3"], writes=[(vnk, vb)])

            def sppart(st):
                vnb = vn[st % 2]
                vnk = "vn%d" % (st % 2)
                for grp in range(6):
                    bk = ps_next(SB_)
                    for i4 in range(4):
                        vc = grp * 4 + i4
                        g = vc // 3
                        mm(S, psum[bk][:, i4 * 128:(i4 + 1) * 128], vnb[:, vc * 128:(vc + 1) * 128], wsT[:, g, :],
                           True, True, [(vnk, vc // 4), "wsT"], bk)
                    sbi = (st * 6 + grp) % 2
                    spt = spt_bufs[sbi]
                    for i4 in range(4):
                        vc = grp * 4 + i4
                        S.add("dve", (lambda e, bk=bk, i4=i4, vc=vc, spt=spt: e.scalar_tensor_tensor(
                            out=spt[:, i4 * 128:(i4 + 1) * 128], in0=psum[bk][:, i4 * 128:(i4 + 1) * 128],
                            scalar=gsc[:, vc:vc + 1], in1=biasT[:, vc, :], op0=ALU.mult, op1=ALU.add)),
                              reads=[("ps", bk), ("biasT", vc), "gsc"], writes=[("spt", sbi, i4)])
                    S.add("pool", (lambda e, grp=grp, st=st, spt=spt: e.tensor_tensor(
                        out=m[:, grp * 4:(grp + 1) * 4, st * 128:(st + 1) * 128],
                        in0=spt[:, :].rearrange("p (a q) -> p a q", q=128),
                        in1=u[:, grp * 4:(grp + 1) * 4, st * 128:(st + 1) * 128], op=ALU.mult)),
                          reads=[("spt", sbi, i) for i in range(4)] + bigrows(grp * 4, 4),
                          writes=bigrows(24 + grp * 4, 4))

            vpart(0)
            lnpart(0)
            for st in range(4):
                if st + 1 < 4:
                    vpart(st + 1)
                sppart(st)
                if st + 1 < 4:
                    lnpart(st + 1)
            OB = (0, 1, 2, 3)
            for db in range(4):
                sl_t, sidx = ws.next(S, [(0, VC * 256, sgo_s[j, db], None)])
                wv = sl_t[:, 0:VC * 256].rearrange("p (k f) -> p k f", f=256)
                for dcl in range(2):
                    dc = db * 2 + dcl
                    bk = ps_next(OB)
                    for vc in range(VC):
                        mm(S, psum[bk][:, :], wv[:, vc, dcl * 128:(dcl + 1) * 128], m[:, vc, :],
                           vc == 0, vc == VC - 1, [("slot", sidx)] + bigrows(24 + vc, 1), bk)
                    if dc >= 1:
                        stats_mm(S, dc - 1, 4)
                    g_ap = G_t[:, li, 1, b, dc:dc + 1]
                    resid(S, bk, dc, g_ap, 4)
            stats_mm(S, KC - 1, 4)
            pre_state["bank"] = 4

        def load_x_tokens(S, tile_idx):
            xin = big_f32(0, 16, D)
            S.add("sp", (lambda e: e.dma_start(out=xin, in_=xA[tile_idx * T:(tile_idx + 1) * T, :]
                                               .rearrange("(s p) d -> p s d", p=128))),
                  reads=(), writes=bigrows(0, 16), dsem="big")
            for c in range(KC):
                bk = ps_next()
                for st in range(4):
                    S.add("pe", (lambda e, c=c, st=st, bk=bk: e.transpose(
                        out=psum[bk][:, st * 128:(st + 1) * 128], in_=xin[:, st, c * 128:(c + 1) * 128],
                        identity=ident_f[:, :])),
                          reads=bigrows(4 * st, 4) + ["ident_f"], writes=[("ps", bk)])
                eng = "act" if c % 2 == 0 else "dve"
                if eng == "act":
                    S.add("act", (lambda e, c=c, bk=bk: e.activation(out=x[:, c, :], in_=psum[bk][:, :], func=AF.Copy)),
                          reads=[("ps", bk)], writes=[("x", c)])
                else:
                    S.add("dve", (lambda e, c=c, bk=bk: e.tensor_copy(out=x[:, c, :], in_=psum[bk][:, :])),
                          reads=[("ps", bk)], writes=[("x", c)])

        def qkv_rotary(S, ws, tile_idx, b):
            norm_mod(S, 1, 1, b)
            qk = big_f32(0, 32, 2048)
            vbf = big_bf(32, 8).rearrange("p a t -> p (a t)").rearrange("p (s d) -> p s d", d=D)
            qT = big_bf(40, 8)
            S.add("sp", (lambda e: e.dma_start(out=ropet[:, :, :], in_=rope[tile_idx * T:(tile_idx + 1) * T, :]
                                               .rearrange("(s p) c -> p s c", p=128))),
                  reads=(), writes=["ropet"], dsem="rope")
            QB = (0, 1, 2, 3)
            for st in range(4):
                for cb in range(6):
                    sl_t, sidx = ws.next(S, [(0, 4096, qkv_s[cb], None)])
                    wv = sl_t[:, 0:4096].rearrange("p (k f) -> p k f", f=512)
                    bk = ps_next(QB)
                    for kc in range(KC):
                        mm(S, psum[bk][:, :], h[:, kc, st * 128:(st + 1) * 128], wv[:, kc, :],
                           kc == 0, kc == KC - 1, [("slot", sidx), ("h", kc)], bk)
                    if cb < 2:
                        S.add("act", (lambda e, bk=bk, st=st, cb=cb: e.activation(
                            out=qk[:, st, cb * 512:(cb + 1) * 512], in_=psum[bk][:, :], func=AF.Copy, scale=0.125)),
                              reads=[("ps", bk)], writes=bigrows(8 * st + 2 * cb, 2))
                    elif cb < 4:
                        S.add("dve", (lambda e, bk=bk, st=st, cb=cb: e.tensor_copy(
                            out=qk[:, st, cb * 512:(cb + 1) * 512], in_=psum[bk][:, :])),
                              reads=[("ps", bk)], writes=bigrows(8 * st + 2 * cb, 2))
                    else:
                        S.add("act", (lambda e, bk=bk, st=st, cb=cb: e.activation(
                            out=vbf[:, st, (cb - 4) * 512:(cb - 3) * 512], in_=psum[bk][:, :], func=AF.Copy)),
                              reads=[("ps", bk)], writes=bigrows(32 + 2 * st + (cb - 4), 1))
                blk = qk[:, st, :].rearrange("p (a d) -> p a d", d=64)
                x1 = blk[:, :, 0:8]
                x2 = blk[:, :, 8:16]
                cosb = ropet[:, st, 0:8].unsqueeze(1).broadcast_to([128, 32, 8])
                sinb = ropet[:, st, 8:16].unsqueeze(1).broadcast_to([128, 32, 8])
                t1 = tmpA[:, 0:256].rearrange("p (a d) -> p a d", d=8)
                t2 = tmpA[:, 256:512].rearrange("p (a d) -> p a d", d=8)
                t3 = tmpB[:, 0:256].rearrange("p (a d) -> p a d", d=8)
                t4 = tmpB[:, 256:512].rearrange("p (a d) -> p a d", d=8)
                rows = bigrows(8 * st, 8)
                S.add("dve", (lambda e, x1=x1, cosb=cosb, t1=t1: e.tensor_tensor(out=t1, in0=x1, in1=cosb, op=ALU.mult)),
                      reads=rows + ["ropet"], writes=["t1"])
                S.add("pool", (lambda e, x2=x2, sinb=sinb, t2=t2: e.tensor_tensor(out=t2, in0=x2, in1=sinb, op=ALU.mult)),
                      reads=rows + ["ropet"], writes=["t2"])
                S.add("dve", (lambda e, x2=x2, cosb=cosb, t3=t3: e.tensor_tensor(out=t3, in0=x2, in1=cosb, op=ALU.mult)),
                      reads=rows + ["ropet"], writes=["t3"])
                S.add("pool", (lambda e, x1=x1, sinb=sinb, t4=t4: e.tensor_tensor(out=t4, in0=x1, in1=sinb, op=ALU.mult)),
                      reads=rows + ["ropet"], writes=["t4"])
                S.add("dve", (lambda e, x1=x1, t1=t1, t2=t2: e.tensor_tensor(out=x1, in0=t1, in1=t2, op=ALU.subtract)),
                      reads=["t1", "t2"], writes=rows)
                S.add("pool", (lambda e, x2=x2, t3=t3, t4=t4: e.tensor_tensor(out=x2, in0=t3, in1=t4, op=ALU.add)),
                      reads=["t3", "t4"], writes=rows)
                qkb = vn[st % 2][:, 0:2048]
                qkbk = "vn%d" % (st % 2)
                S.add("act", (lambda e, st=st, qkb=qkb: e.activation(out=qkb, in_=qk[:, st, :], func=AF.Copy)),
                      reads=rows, writes=[(qkbk, i) for i in range(6)])
                for half in range(2):
                    bk = ps_next((4, 5, 6, 7))
                    pst = psum[bk][:, :].bitcast(BF16)
                    for hh in range(8):
                        S.add("pe", (lambda e, pst=pst, hh=hh, half=half, qkb=qkb: e.transpose(
                            out=pst[:, hh * 128:(hh + 1) * 128],
                            in_=qkb[:, half * 1024 + hh * 128: half * 1024 + (hh + 1) * 128],
                            identity=ident_b[:, :])),
                              reads=[(qkbk, i) for i in range(6)] + ["ident_b"], writes=[("ps", bk)])
                    if half == 0:
                        S.add("dve", (lambda e, pst=pst, st=st: e.tensor_copy(
                            out=qT[:, :, st * 128:(st + 1) * 128], in_=pst.rearrange("p (a t) -> p a t", t=128))),
                              reads=[("ps", bk)], writes=bigrows(40, 8))
                    else:
                        S.add("dve", (lambda e, pst=pst, st=st: e.tensor_copy(
                            out=ob[:, :, st * 128:(st + 1) * 128], in_=pst.rearrange("p (a t) -> p a t", t=128))),
                              reads=[("ps", bk)], writes=[("kT", st)])
            S.add("pool", (lambda e: e.dma_start(out=xa_s[tile_idx], in_=x[:, :, :])),
                  reads=[("x", c) for c in range(KC)], writes=[("xa_s", tile_idx)], dsem="st_x")
            S.add("pool", (lambda e: e.dma_start(out=qT_s[tile_idx], in_=qT)),
                  reads=bigrows(40, 8), writes=[("qT_s", tile_idx)], dsem="st_q")
            S.add("pool", (lambda e: e.dma_start(
                out=kT_s[:, :, tile_idx * T:(tile_idx + 1) * T].rearrange("a p t -> p a t"), in_=ob[:, :, :])),
                  reads=[("kT", s_) for s_ in range(4)], writes=[("kT_s", tile_idx)], dsem="st_k")
            S.add("pool", (lambda e: e.dma_start(
                out=v_s[tile_idx * T:(tile_idx + 1) * T, :].rearrange("(s p) d -> p s d", p=128), in_=vbf)),
                  reads=bigrows(32, 8), writes=[("v_s", tile_idx)], dsem="st_v")

        def attention(S, ws, tile_idx, region, b):
            t0k, nkt, _ = geo.regions[region]
            nkeys = nkt * T
            key0 = t0k * T
            qT = big_bf(40, 8)
            S.add("sp", (lambda e: e.dma_start(out=qT, in_=qT_s[tile_idx])),
                  reads=[("qT_s", tile_idx)], writes=bigrows(40, 8), dsem="qt")
            KB = 2048 if nkeys >= 2048 else nkeys
            nkb = nkeys // KB
            SCB = (0, 1, 2, 3)
            lam_init = 0.8 - 0.6 * math.exp(-0.3 * 1)
            for hh in range(8):
                nchunks = nkeys // 128
                cpb = KB // 128

                state = {}

                def get_chunk(ci, hh=hh, state=state):
                    kb = ci // cpb
                    if state.get("kb") != kb:
                        k_src = kT_s[hh, :, key0 + kb * KB: key0 + (kb + 1) * KB]
                        v_src = v_s[key0 + kb * KB: key0 + (kb + 1) * KB, hh * 128:(hh + 1) * 128] \
                            .rearrange("(c p) e -> p c e", p=128)
                        kt0 = (key0 + kb * KB) // T
                        kt1 = (key0 + (kb + 1) * KB - 1) // T
                        rds = [("kT_s", t_) for t_ in range(kt0, kt1 + 1)] + [("v_s", t_) for t_ in range(kt0, kt1 + 1)]
                        sl_t, sidx = ws.next(S, [(0, KB, k_src, None),
                                                 (2048, KB, v_src, ("p (c e) -> p c e", dict(e=128)))], rds)
                        state["kb"] = kb
                        state["sl"] = (sl_t, sidx)
                    sl_t, sidx = state["sl"]
                    kcl = ci % cpb
                    return (sl_t[:, kcl * 128:(kcl + 1) * 128],
                            sl_t[:, 2048 + kcl * 128: 2048 + (kcl + 1) * 128], [("slot", sidx)])

                def emit_scores(ci, hh=hh):
                    kTc, vch, kv_reads = get_chunk(ci)
                    b0 = SCB[(2 * ci) % 4]
                    b1 = SCB[(2 * ci + 1) % 4]
                    S.add("pe", (lambda e, b0=b0, kTc=kTc, hh=hh: e.matmul(
                        out=psum[b0][:, :], lhsT=kTc[0:64, :], rhs=qT[0:64, hh, :], start=True, stop=True)),
                          reads=kv_reads + bigrows(40 + hh, 1), writes=[("ps", b0)])
                    S.add("pe", (lambda e, b1=b1, kTc=kTc, hh=hh: e.matmul(
                        out=psum[b1][:, :], lhsT=kTc[64:128, :], rhs=qT[64:128, hh, :], start=True, stop=True)),
                          reads=kv_reads + bigrows(40 + hh, 1), writes=[("ps", b1)])
                    p0 = pT[(ci % 2) * 2]
                    p1 = pT[(ci % 2) * 2 + 1]
                    k0 = "pT%d" % ((ci % 2) * 2)
                    k1 = "pT%d" % ((ci % 2) * 2 + 1)
                    S.add("act", (lambda e, b0=b0, p0=p0: e.activation(out=p0[:, :], in_=psum[b0][:, :], func=AF.Exp)),
                          reads=[("ps", b0)], writes=[k0])
                    S.add("act", (lambda e, b1=b1, p1=p1: e.activation(out=p1[:, :], in_=psum[b1][:, :], func=AF.Exp)),
                          reads=[("ps", b1)], writes=[k1])
                    return (vch, kv_reads, p0, p1, k0, k1)

                pend = emit_scores(0)
                for ci in range(nchunks):
                    cur = pend
                    if ci + 1 < nchunks:
                        pend = emit_scores(ci + 1)
                    vch, kv_reads, p0, p1, k0, k1 = cur
                    first = ci == 0
                    lastc = ci == nchunks - 1
                    mm(S, psum[4][:, :], vch, p0[:, :], first, lastc, kv_reads + [k0], 4)
                    mm(S, psum[5][:, :], ones_b[:, :], p0[:, :], first, lastc, [k0], 5)
                    mm(S, psum[6][:, :], vch, p1[:, :], first, lastc, kv_reads + [k1], 6)
                    mm(S, psum[7][:, :], ones_b[:, :], p1[:, :], first, lastc, [k1], 7)
                S.add("dve", (lambda e: e.reciprocal(out=tmpA[:, :], in_=psum[5][:, :])),
                      reads=[("ps", 5)], writes=["tmpA"])
                S.add("dve", (lambda e: e.reciprocal(out=tmpB[:, :], in_=psum[7][:, :])),
                      reads=[("ps", 7)], writes=["tmpB"])
                S.add("dve", (lambda e: e.tensor_tensor(out=tmpA[:, :], in0=psum[4][:, :], in1=tmpA[:, :], op=ALU.mult)),
                      reads=[("ps", 4), "tmpA"], writes=["tmpA"])
                S.add("dve", (lambda e: e.tensor_tensor(out=tmpB[:, :], in0=psum[6][:, :], in1=tmpB[:, :], op=ALU.mult)),
                      reads=[("ps", 6), "tmpB"], writes=["tmpB"])
                S.add("dve", (lambda e: e.scalar_tensor_tensor(out=tmpC[:, :], in0=tmpB[:, :], scalar=neglam[:, 0:1],
                                                               in1=tmpA[:, :], op0=ALU.mult, op1=ALU.add)),
                      reads=["tmpA", "tmpB", "neglam"], writes=["tmpC"])
                S.add("act", (lambda e: e.activation(out=rstd[:, :], in_=tmpC[:, :], func=AF.Square)),
                      reads=["tmpC"], writes=["rstd"])
                bk = ps_next(SCB)
                mm(S, psum[bk][:, :], ones_f[:, :], rstd[:, :], True, True, ["rstd"], bk)
                S.add("act", (lambda e, bk=bk: e.activation(out=tmpA[:, :], in_=psum[bk][:, :], func=AF.Sqrt,
                                                            bias=eps_t[:, 0:1], scale=1.0 / 128)),
                      reads=[("ps", bk), "eps_t"], writes=["tmpA"])
                S.add("dve", (lambda e: e.reciprocal(out=tmpB[:, :], in_=tmpA[:, :])),
                      reads=["tmpA"], writes=["tmpB"])
                S.add("dve", (lambda e, hh=hh: e.scalar_tensor_tensor(
                    out=ob[:, hh, :], in0=tmpC[:, :], scalar=sublng[:, 0:1], in1=tmpB[:, :],
                    op0=ALU.mult, op1=ALU.mult)),
                      reads=["tmpC", "tmpB", "sublng"], writes=[("ob", hh)])
            for db in range(4):
                sl_t, sidx = ws.next(S, [(0, 2048, wo_s[db], None)])
                wv = sl_t[:, 0:2048].rearrange("p (k f) -> p k f", f=256)
                for dcl in range(2):
                    dc = db * 2 + dcl
                    bk = ps_next(SCB)
                    for hh in range(8):
                        mm(S, psum[bk][:, :], wv[:, hh, dcl * 128:(dcl + 1) * 128], ob[:, hh, :],
                           hh == 0, hh == 7, [("slot", sidx), ("ob", hh)], bk)
                    if dc >= 1:
                        stats_mm(S, dc - 1, 4)
                    g_ap = G_t[:, 1, 1, b, dc:dc + 1]
                    resid(S, bk, dc, g_ap, 4)
            stats_mm(S, KC - 1, 4)
            pre_state["bank"] = 4

        def conv_in(S, ws, b, own_idx, region, tin, is_halo):
            norm_mod(S, 2, 1, b)
            bgt = big_f32(0, 16, T)
            gt = big_f32(16, 16, T)
            CB = (0, 1, 2, 3, 4, 5)
            for db in range(4):
                sl_t, sidx = ws.next(S, [(0, 6144, cin_s[db], None)])
                wv = sl_t[:, 0:6144].rearrange("p (k s f) -> p k s f", s=3, f=256)
                for dcl in range(2):
                    dc = db * 2 + dcl
                    bks = []
                    for sct in range(3):
                        bk = ps_next(CB)
                        bks.append(bk)
                        for kc in range(KC):
                            mm(S, psum[bk][:, :], wv[:, kc, sct, dcl * 128:(dcl + 1) * 128], h[:, kc, :],
                               kc == 0, kc == KC - 1, [("slot", sidx), ("h", kc)], bk)
                    S.add("act", (lambda e, dc=dc, bk=bks[0]: e.activation(out=bgt[:, dc, :], in_=psum[bk][:, :], func=AF.Copy)),
                          reads=[("ps", bks[0])], writes=bigrows(2 * dc, 2))
                    S.add("act", (lambda e, bk=bks[1]: e.activation(out=tmpA[:, :], in_=psum[bk][:, :], func=AF.Copy)),
                          reads=[("ps", bks[1])], writes=["tmpA"])
                    S.add("dve", (lambda e, dc=dc, bk=bks[2]: e.tensor_tensor(out=gt[:, dc, :], in0=psum[bk][:, :],
                                                                               in1=tmpA[:, :], op=ALU.mult)),
                          reads=[("ps", bks[2]), "tmpA"], writes=bigrows(16 + 2 * dc, 2))
            if is_halo:
                S.add("dve", (lambda e: e.tensor_scalar(out=hcol[:, :, 0:1], in0=gt[:, :, 255:256], scalar1=hmask[:, 0:1],
                                                        scalar2=None, op0=ALU.mult)),
                      reads=bigrows(16, 16) + ["hmask"], writes=["hcol0"])
                S.add("dve", (lambda e: e.tensor_scalar(out=hcol[:, :, 1:2], in0=gt[:, :, 0:1], scalar1=hmask[:, 1:2],
                                                        scalar2=None, op0=ALU.mult)),
                      reads=bigrows(16, 16) + ["hmask"], writes=["hcol1"])
                nq = geo.regions[2][2] * T
                S.add("pool", (lambda e: e.dma_start(out=g_s[2][:, :, 0:1], in_=hcol[:, :, 0:1], allow_slow_non_contiguous=True)),
                      reads=["hcol0"], writes=[("g_s", 2, "lo")], dsem="misc")
                S.add("pool", (lambda e: e.dma_start(out=g_s[2][:, :, nq + 1:nq + 2], in_=hcol[:, :, 1:2], allow_slow_non_contiguous=True)),
                      reads=["hcol1"], writes=[("g_s", 2, "hi")], dsem="misc")
                return
            S.add("pool", (lambda e: e.dma_start(out=xb_s[own_idx], in_=x[:, :, :])),
                  reads=[("x", c) for c in range(KC)], writes=[("xb_s", own_idx)], dsem="st_x")
            S.add("pool", (lambda e: e.dma_start(out=bg_s[own_idx], in_=bgt)),
                  reads=bigrows(0, 16), writes=[("bg_s", own_idx)], dsem="st_q")
            S.add("pool", (lambda e: e.dma_start(out=g_s[region][:, :, 1 + tin * T: 1 + (tin + 1) * T], in_=gt)),
                  reads=bigrows(16, 16), writes=[("g_s", region, tin)], dsem="st_k")

        def conv_mix(S, ws, b, own_idx, region, tin):
            gwin = big[:, 0:8224].bitcast(F32).rearrange("p (c t) -> p c t", t=514)
            bgt = big_f32(17, 16, T)
            mcv = big_bf(33, 8)
            nreg = geo.regions[region][2]
            deps = [("g_s", region, tin)]
            if tin > 0:
                deps.append(("g_s", region, tin - 1))
            else:
                deps.append(("g_s", region, "lo"))
            if tin < nreg - 1:
                deps.append(("g_s", region, tin + 1))
            else:
                deps.append(("g_s", region, "hi"))
            S.add("sp", (lambda e: e.dma_start(out=x[:, :, :], in_=xb_s[own_idx])),
                  reads=[("xb_s", own_idx)], writes=[("x", c) for c in range(KC)], dsem="x")
            S.add("sp", (lambda e: e.dma_start(out=gwin, in_=g_s[region][:, :, tin * T: tin * T + 514])),
                  reads=deps, writes=bigrows(0, 17), dsem="gw")
            S.add("sp", (lambda e: e.dma_start(out=bgt, in_=bg_s[own_idx])),
                  reads=[("bg_s", own_idx)], writes=bigrows(17, 16), dsem="bgl")
            for c in range(KC):
                w0 = smallT[:, C_CONVK + 0 * 8 + c: C_CONVK + 0 * 8 + c + 1]
                w1 = smallT[:, C_CONVK + 1 * 8 + c: C_CONVK + 1 * 8 + c + 1]
                w2 = smallT[:, C_CONVK + 2 * 8 + c: C_CONVK + 2 * 8 + c + 1]
                tt = tmpB if c % 2 == 0 else tmpC
                tk = "tmpB" if c % 2 == 0 else "tmpC"
                S.add("act", (lambda e, c=c, tt=tt, w0=w0: e.activation(out=tt[:, :], in_=gwin[:, c, 0:512],
                                                                        func=AF.Copy, scale=w0)),
                      reads=bigrows(0, 17), writes=[tk])
                S.add("dve", (lambda e, c=c, tt=tt, w1=w1: e.scalar_tensor_tensor(
                    out=tt[:, :], in0=gwin[:, c, 1:513], scalar=w1, in1=tt[:, :], op0=ALU.mult, op1=ALU.add)),
                      reads=bigrows(0, 17) + [tk], writes=[tk])
                S.add("dve", (lambda e, c=c, tt=tt, w2=w2: e.scalar_tensor_tensor(
                    out=tt[:, :], in0=gwin[:, c, 2:514], scalar=w2, in1=tt[:, :], op0=ALU.mult, op1=ALU.add)),
                      reads=bigrows(0, 17) + [tk], writes=[tk])
                S.add("pool", (lambda e, c=c, tt=tt: e.tensor_tensor(out=mcv[:, c, :], in0=tt[:, :], in1=bgt[:, c, :],
                                                                     op=ALU.mult)),
                      reads=[tk] + bigrows(17 + 2 * c, 2), writes=bigrows(33 + c, 1))
            OB_ = (0, 1, 2, 3)
            for db in range(4):
                sl_t, sidx = ws.next(S, [(0, 2048, cout_s[db], None)])
                wv = sl_t[:, 0:2048].rearrange("p (k f) -> p k f", f=256)
                for dcl in range(2):
                    dc = db * 2 + dcl
                    bk = ps_next(OB_)
                    for kc in range(KC):
                        mm(S, psum[bk][:, :], wv[:, kc, dcl * 128:(dcl + 1) * 128], mcv[:, kc, :],
                           kc == 0, kc == KC - 1, [("slot", sidx)] + bigrows(33 + kc, 1), bk)
                    if dc >= 1:
                        stats_mm(S, dc - 1, 4)
                    g_ap = G_t[:, 2, 1, b, dc:dc + 1]
                    resid(S, bk, dc, g_ap, 4)
            stats_mm(S, KC - 1, 4)
            pre_state["bank"] = 4

        def setup(S):
            S.add("pool", (lambda e: e.memset(ones_f[:, :], 1.0)), reads=(), writes=["ones_f"])
            S.add("pool", (lambda e: e.memset(ones_b[:, :], 1.0)), reads=(), writes=["ones_b"])
            S.add("pool", (lambda e: e.memset(zcol[:, :, :], 0.0)), reads=(), writes=["zcol"])
            S.add("pool", (lambda e: e.memset(eps_t[:, :], EPS)), reads=(), writes=["eps_t"])
            S.add("sp", (lambda e: e.dma_start(out=ident_f[:, :], in_=ident_in[:, :])), reads=(), writes=["ident_f"], dsem="misc")
            S.add("sp", (lambda e: e.dma_start(out=hmask[:, :], in_=hmask_in[:, :])), reads=(), writes=["hmask"], dsem="misc")
            S.add("sp", (lambda e: e.dma_start(out=stage[:, :, :], in_=smallp.rearrange("(a p) f -> p a f", p=128))),
                  reads=(), writes=[("stage", i) for i in range(5)], dsem="misc")
            S.add("dve", (lambda e: e.tensor_copy(out=ident_b[:, :], in_=ident_f[:, :])), reads=["ident_f"], writes=["ident_b"])
            for a in range(5):
                bk = ps_next()
                S.add("pe", (lambda e, a=a, bk=bk: e.transpose(out=psum[bk][:, 0:128], in_=stage[:, a, :], identity=ident_f[:, :])),
                      reads=[("stage", a), "ident_f"], writes=[("ps", bk)])
                S.add("dve", (lambda e, a=a, bk=bk: e.tensor_copy(out=smallT[:, a * 128:(a + 1) * 128], in_=psum[bk][:, 0:128])),
                      reads=[("ps", bk)], writes=["smallT"])
            def prep(dst, src):
                S.add("pool", (lambda e, dst=dst, src=src: e.dma_start(out=dst, in_=src)),
                      reads=(), writes=["wprep"], dsem="prep")
            for li in range(DEPTH):
                for fi in range(2):
                    for fb in range(11):
                        prep(wg_s[li, fi, fb], w_gate[li, fi][:, fb * 256:(fb + 1) * 256].rearrange("(k p) f -> p k f", p=128))
                        prep(wu_s[li, fi, fb], w_up[li, fi][:, fb * 256:(fb + 1) * 256].rearrange("(k p) f -> p k f", p=128))
                    for db in range(4):
                        prep(wd_s[li, fi, db], w_down[li, fi][:, db * 256:(db + 1) * 256].rearrange("(k p) f -> p k f", p=128))
            for j in range(2):
                for ub in range(12):
                    prep(sgu_s[j, ub], sg_w_in[j][:, ub * 256:(ub + 1) * 256].rearrange("(k p) f -> p k f", p=128))
                for vb in range(6):
                    prep(sgv_s[j, vb], sg_w_in[j][:, SGH + vb * 512:SGH + (vb + 1) * 512].rearrange("(k p) f -> p k f", p=128))
                for db in range(4):
                    prep(sgo_s[j, db], sg_w_out[j][:, db * 256:(db + 1) * 256].rearrange("(k p) f -> p k f", p=128))
            for cb in range(6):
                prep(qkv_s[cb], da_w_qkv[0][:, cb * 512:(cb + 1) * 512].rearrange("(k p) f -> p k f", p=128))
            for db in range(4):
                prep(wo_s[db], da_w_out[0][:, db * 256:(db + 1) * 256].rearrange("(k p) f -> p k f", p=128))
                prep(cout_s[db], conv_w_out[0][:, db * 256:(db + 1) * 256].rearrange("(k p) f -> p k f", p=128))
                for sct in range(3):
                    prep(cin_s[db][:, :, sct, :],
                         conv_w_in[0][:, sct * D + db * 256: sct * D + (db + 1) * 256].rearrange("(k p) f -> p k f", p=128))
            for bb in range(3):
                S.add("act", (lambda e, bb=bb: e.activation(out=cact[:, :, bb], in_=smallT[:, C_C + bb * 8: C_C + bb * 8 + 8],
                                                            func=AF.Silu)),
                      reads=["smallT"], writes=["cact"])
            for li in range(DEPTH):
                bk = ps_next()
                for nb in range(18):
                    sidx = nb % 2
                    blk = big[:, sidx * 8192:(sidx + 1) * 8192].bitcast(F32).rearrange("p (k f) -> p k f", f=512)
                    S.add("sp", (lambda e, blk=blk, li=li, nb=nb: e.dma_start(
                        out=blk, in_=ada_w[li][:, nb * 512:(nb + 1) * 512].rearrange("(k p) f -> p k f", p=128))),
                          reads=(), writes=[("adablk", sidx)], dsem="slot%d" % sidx)
                    for n4 in range(4):
                        n = nb * 4 + n4
                        for kc in range(KC):
                            S.add("pe", (lambda e, bk=bk, blk=blk, n=n, n4=n4, kc=kc: e.matmul(
                                out=psum[bk][:, n * 3:(n + 1) * 3], lhsT=blk[:, kc, n4 * 128:(n4 + 1) * 128],
                                rhs=cact[:, kc, :], start=(kc == 0), stop=(kc == KC - 1))),
                                  reads=[("adablk", sidx), "cact"], writes=[("ps", bk)])
                for bb in range(3):
                    S.add("dve", (lambda e, bk=bk, li=li, bb=bb: e.tensor_tensor(
                        out=modt[:, li, :, bb], in0=psum[bk][:, 0:216].rearrange("p (n b) -> p n b", b=3)[:, :, bb],
                        in1=smallT[:, C_ADAB + li * 72: C_ADAB + (li + 1) * 72], op=ALU.add)),
                          reads=[("ps", bk), "smallT"], writes=["modt"])
            for li in range(DEPTH):
                for sl in range(3):
                    for bb in range(3):
                        ng = smallT[:, C_NORMG + (li * 3 + sl) * 8: C_NORMG + (li * 3 + sl) * 8 + 8]
                        S.add("dve", (lambda e, li=li, sl=sl, bb=bb, ng=ng: e.scalar_tensor_tensor(
                            out=A_t[:, li, sl, bb, :], in0=modt[:, li, (3 * sl + 1) * 8:(3 * sl + 2) * 8, bb], scalar=1.0,
                            in1=ng, op0=ALU.add, op1=ALU.mult)),
                              reads=["modt", "smallT"], writes=["A_t"])
                        S.add("dve", (lambda e, li=li, sl=sl, bb=bb: e.tensor_copy(
                            out=B_t[:, li, sl, bb, :], in_=modt[:, li, (3 * sl) * 8:(3 * sl + 1) * 8, bb])),
                              reads=["modt"], writes=["B_t"])
                        S.add("dve", (lambda e, li=li, sl=sl, bb=bb: e.tensor_scalar(
                            out=G_t[:, li, sl, bb, :], in0=modt[:, li, (3 * sl + 2) * 8:(3 * sl + 3) * 8, bb],
                            scalar1=(1.0 if sl == 1 else 0.5), scalar2=None, op0=ALU.mult)),
                              reads=["modt"], writes=["G_t"])
            lam_init = 0.8 - 0.6 * math.exp(-0.3 * 1)
            S.add("sp", (lambda e: e.dma_start(out=lamrow[:, :], in_=da_lambda[0:1].rearrange("a r d -> a (r d)"))),
                  reads=(), writes=["lamrow"], dsem="misc")
            S.add("dve", (lambda e: e.tensor_tensor(out=lamrow[:, 0:64], in0=lamrow[:, 0:64], in1=lamrow[:, 64:128], op=ALU.mult)),
                  reads=["lamrow"], writes=["lamrow"])
            S.add("dve", (lambda e: e.tensor_tensor(out=lamrow[:, 128:192], in0=lamrow[:, 128:192], in1=lamrow[:, 192:256], op=ALU.mult)),
                  reads=["lamrow"], writes=["lamrow"])
            S.add("dve", (lambda e: e.reduce_sum(out=lamw[:, 0:1], in_=lamrow[:, 0:64], axis=AX.X)),
                  reads=["lamrow"], writes=["lamw"])
            S.add("dve", (lambda e: e.reduce_sum(out=lamw[:, 1:2], in_=lamrow[:, 128:192], axis=AX.X)),
                  reads=["lamrow"], writes=["lamw"])
            S.add("act", (lambda e: e.activation(out=lamw[:, 2:4], in_=lamw[:, 0:2], func=AF.Exp)),
                  reads=["lamw"], writes=["lamw"])
            S.add("dve", (lambda e: e.tensor_tensor(out=lamw[:, 4:5], in0=lamw[:, 3:4], in1=lamw[:, 2:3], op=ALU.subtract)),
                  reads=["lamw"], writes=["lamw"])
            S.add("dve", (lambda e: e.tensor_scalar(out=lamw[:, 5:6], in0=lamw[:, 4:5], scalar1=-lam_init, scalar2=None, op0=ALU.add)),
                  reads=["lamw"], writes=["lamw"])
            bk = ps_next()
            S.add("pe", (lambda e, bk=bk: e.matmul(out=psum[bk][:, 0:1], lhsT=ones_f[0:1, :], rhs=lamw[0:1, 5:6], start=True, stop=True)),
                  reads=["lamw", "ones_f"], writes=[("ps", bk)])
            S.add("dve", (lambda e, bk=bk: e.tensor_copy(out=neglam[:, :], in_=psum[bk][:, 0:1])),
                  reads=[("ps", bk)], writes=["neglam"])
            S.add("dve", (lambda e: e.tensor_scalar(out=sublng[:, :], in0=smallT[:, C_SUBLN:C_SUBLN + 1], scalar1=1.0 - lam_init,
                                                    scalar2=None, op0=ALU.mult)),
                  reads=["smallT"], writes=["sublng"])
            for r in range(2):
                n = geo.regions[r][2] * T
                S.add("pool", (lambda e, r=r: e.dma_start(out=g_s[r][:, :, 0:1], in_=zcol[:, :, :], allow_slow_non_contiguous=True)),
                      reads=["zcol"], writes=[("g_s", r, "lo")], dsem="misc")
                S.add("pool", (lambda e, r=r, n=n: e.dma_start(out=g_s[r][:, :, n + 1:n + 2], in_=zcol[:, :, :], allow_slow_non_contiguous=True)),
                      reads=["zcol"], writes=[("g_s", r, "hi")], dsem="misc")


        def final_out2(S, own_idx):
            fin_buf = big_f32(0, 16, T)
            norm_mod(S, 0, 0, 0, hout=fin_buf, final=True)
            otm = big_f32(16, 16, D)
            for st in range(4):
                for half in range(2):
                    bk = ps_next()
                    for c4 in range(4):
                        c = half * 4 + c4
                        S.add("pe", (lambda e, bk=bk, c=c, c4=c4, st=st: e.transpose(
                            out=psum[bk][:, c4 * 128:(c4 + 1) * 128], in_=fin_buf[:, c, st * 128:(st + 1) * 128],
                            identity=ident_f[:, :])),
                              reads=bigrows(2 * c, 2) + ["ident_f"], writes=[("ps", bk)])
                    rws = bigrows(16 + 4 * st + 2 * half, 2)
                    if half == 0:
                        S.add("act", (lambda e, bk=bk, st=st: e.activation(out=otm[:, st, 0:512], in_=psum[bk][:, :], func=AF.Copy)),
                              reads=[("ps", bk)], writes=rws)
                    else:
                        S.add("dve", (lambda e, bk=bk, st=st: e.tensor_copy(out=otm[:, st, 512:1024], in_=psum[bk][:, :])),
                              reads=[("ps", bk)], writes=rws)
            S.add("pool", (lambda e: e.dma_start(out=y_out[own_idx * T:(own_idx + 1) * T, :].rearrange("(s p) d -> p s d", p=128),
                                                 in_=otm)),
                  reads=bigrows(16, 16), writes=[("y", own_idx)], dsem="out")

        def tile_batch(tile_idx):
            if tile_idx < geo.ntP:
                return 0
            if tile_idx < 2 * geo.ntP:
                return 1
            return 2

        import os as _os
        STOP = int(_os.environ.get("KSTOP", "99"))

        def dump_x(S):
            S.add("pool", (lambda e: e.dma_start(out=dbg_out, in_=x[:, :, :])),
                  reads=[("x", c) for c in range(KC)], writes=["dbg"], dsem="out")

        def program(S, ws):
            _program(S, ws)
            if STOP != 99:
                dump_x(S)

        def _program(S, ws):
            setup(S)
            if STOP == 0:
                return
            gmlp_setup(S, 0)
            if STOP == 1:
                return
            S.barrier()
            for ti in range(geo.ntA):
                b = tile_batch(ti)
                load_x_tokens(S, ti)
                if STOP == 2:
                    return
                ffn(S, ws, 0, 0, b, 0)
                if STOP == 3:
                    return
                gmlp(S, ws, 0, 0, b)
                if STOP == 4:
                    return
                ffn(S, ws, 0, 1, b, 2)
                ffn(S, ws, 1, 0, b, 0)
                qkv_rotary(S, ws, ti, b)
                if STOP == 5:
                    return
            S.barrier()
            if STOP == 6:
                return
            btiles = [(ti, r, i, oi) for oi, (ti, r, i) in enumerate(geo.own)] + [(geo.halo_tile, 2, -1, -1)]
            for (ti, r, tin, oi) in btiles:
                b = r
                S.add("sp", (lambda e, ti=ti: e.dma_start(out=x[:, :, :], in_=xa_s[ti])),
                      reads=[("xa_s", ti)], writes=[("x", c) for c in range(KC)], dsem="x")
                if STOP == 10:
                    return
                attention(S, ws, ti, r, b)
                if STOP == 7:
                    return
                ffn(S, ws, 1, 1, b, 2)
                ffn(S, ws, 2, 0, b, 0)
                conv_in(S, ws, b, oi, r, tin, tin < 0)
                if STOP == 8:
                    return
            S.barrier()
            gmlp_setup(S, 1)
            S.barrier()
            for oi, (ti, r, tin) in enumerate(geo.own):
                b = r
                conv_mix(S, ws, b, oi, r, tin)
                if STOP == 9:
                    return
                ffn(S, ws, 2, 1, b, 2)
                ffn(S, ws, 3, 0, b, 0)
                gmlp(S, ws, 3, 1, b)
                ffn(S, ws, 3, 1, b, 2)
                final_out2(S, oi)
                if STOP == 11:
                    return

        rec = WStream(None)
        S0 = Sched()
        ps_rr[0] = 0
        pre_state["bank"] = None
        program(S0, rec)
        ps_rr[0] = 0
        pre_state["bank"] = None
        ws = WStream(rec.out)
        program(S, ws)
        S.emit(nc, block, sems, dsems)
    return nc


def _run(inputs, S_P, S_S):
    geo = Geo(S_P, S_S)
    f32 = np.float32
    xp = np.asarray(inputs["x_prompt"], f32)
    xs = np.asarray(inputs["x_sample"], f32)
    cp = np.asarray(inputs["c_prompt"], f32)
    cs = np.asarray(inputs["c_sample"], f32)
    assert xp.shape == (16, S_P, D) and xs.shape == (2, S_S, D)
    nc = build_program(geo)
    ident = np.eye(128, dtype=f32)
    shared = {k: np.ascontiguousarray(np.asarray(inputs[k], f32)) for k in
              ["ada_w", "ffn_w_gate", "ffn_w_up", "ffn_w_down", "sg_w_in", "sg_ln_g", "sg_ln_b", "sg_w_s", "sg_b_s",
               "sg_b_in", "sg_w_out", "da_w_qkv", "da_lambda", "da_w_out", "conv_w_in", "conv_w_out"]}
    in_maps = []
    orders = []
    for core in range(NCORES):
        sseq = core // 4
        rank = core % 4
        order = geo.sample_order(rank)
        orders.append(order)
        xs_perm = xs[sseq].reshape(S_S // 128, 128, D)[order].reshape(S_S, D)
        xA = np.concatenate([xp[2 * core], xp[2 * core + 1], xs_perm], axis=0)
        pos_s = (np.asarray(order)[:, None] * 128 + np.arange(128)[None, :]).reshape(-1)
        pos = np.concatenate([np.arange(S_P), np.arange(S_P), pos_s])
        rope = _rope_table(pos)
        small = np.zeros((640, 128), f32)
        small[0:288] = np.asarray(inputs["ada_b"], f32).reshape(288, 128)
        small[288:384] = np.asarray(inputs["norm_g"], f32).reshape(96, 128)
        small[384:480] = np.asarray(inputs["sg_b_in"], f32).reshape(96, 128)
        small[480:504] = np.asarray(inputs["conv_kernel"], f32).reshape(24, 128)
        small[504:512] = np.asarray(inputs["final_norm_g"], f32).reshape(8, 128)
        small[512:513] = np.asarray(inputs["da_subln_g"], f32).reshape(1, 128)
        small[513:521] = cp[2 * core].reshape(8, 128)
        small[521:529] = cp[2 * core + 1].reshape(8, 128)
        small[529:537] = cs[sseq].reshape(8, 128)
        hm = np.zeros((128, 2), f32)
        hm[:, 0] = 0.0 if rank == 0 else 1.0
        hm[:, 1] = 0.0 if rank == 3 else 1.0
        m = {"xA": np.ascontiguousarray(xA), "rope": rope, "smallp": small, "ident": ident, "hmask": hm}
        m.update(shared)
        in_maps.append(m)
    res = run_bass_kernel_spmd(nc, in_maps, core_ids=list(range(NCORES)))
    global _last_res
    _last_res = res
    yp = np.empty((16, S_P, D), f32)
    ys = np.empty((2, S_S, D), f32)
    for core in range(NCORES):
        y = np.asarray(res.results[core]["y"], f32)
        yp[2 * core] = y[0:S_P]
        yp[2 * core + 1] = y[S_P:2 * S_P]
        rank = core % 4
        ys[core // 4, rank * geo.Q:(rank + 1) * geo.Q] = y[2 * S_P:2 * S_P + geo.Q]
    return yp, ys


def kernel(**inputs):
    S_P = int(np.asarray(inputs["x_prompt"]).shape[1])
    S_S = int(np.asarray(inputs["x_sample"]).shape[1])
    return _run(inputs, S_P, S_S)
```

```python
import math
from contextlib import ExitStack

import numpy as np
import concourse.bass as bass
import concourse.mybir as mybir
from concourse.bass_utils import run_bass_kernel_spmd

F32 = mybir.dt.float32
BF16 = mybir.dt.bfloat16
AF = mybir.ActivationFunctionType
ALU = mybir.AluOpType
AX = mybir.AxisListType

D = 1024
KC = 8
FF = 2816
FC = 22
SGH = 3072
VC = 24
T = 512
NCORES = 8
EPS = 1e-6
DEPTH = 4
SLOT_ELEMS = 6144
NSLOTS = 4


class Sched:
    ENGS = ("pe", "act", "dve", "pool", "sp")

    def __init__(self):
        self.ops = []
        self.last_w = {}
        self.readers = {}
        self.pending_bar = {}

    def add(self, eng, fn, reads=(), writes=(), dsem=None):
        idx = len(self.ops)
        deps = set()
        lw = self.last_w
        rdrs = self.readers
        for r in reads:
            w = lw.get(r)
            if w is not None:
                deps.add(w)
        for w_ in writes:
            w = lw.get(w_)
            if w is not None:
                deps.add(w)
            rr = rdrs.get(w_)
            if rr:
                deps.update(rr)
        for r in reads:
            l = rdrs.get(r)
            if l is None:
                rdrs[r] = [idx]
            else:
                l.append(idx)
        for w_ in writes:
            lw[w_] = idx
            rdrs[w_] = []
        if dsem == "misc":
            w = lw.get("__misc_chain__")
            if w is not None:
                deps.add(w)
            lw["__misc_chain__"] = idx
        pb = self.pending_bar.pop(eng, None)
        if pb:
            deps.update(pb)
        self.ops.append([eng, fn, deps, dsem, False])
        return idx

    def barrier(self):
        last = {}
        for i, op in enumerate(self.ops):
            if op[3] is not None:
                last[("d", op[3])] = i
            else:
                last[("e", op[0])] = i
        s = set(last.values())
        for e in self.ENGS:
            self.pending_bar.setdefault(e, set()).update(s)
        self.last_w = {}
        self.readers = {}

    def emit(self, nc, block, sems, dsems):
        ops = self.ops
        for i, op in enumerate(ops):
            for j in op[2]:
                oj = ops[j]
                if oj[3] is None and oj[0] == "pe" and op[0] == "pe" and op[3] is None:
                    continue
                oj[4] = True
        last = {}
        for i, op in enumerate(ops):
            if op[3] is not None:
                last[("d", op[3])] = i
            else:
                last[("e", op[0])] = i
        for i in last.values():
            ops[i][4] = True
        cnt = {}
        val = [0] * len(ops)
        for i, op in enumerate(ops):
            if op[3] is not None:
                k = ("d", op[3])
                cnt[k] = cnt.get(k, 0) + 16
                val[i] = cnt[k]
            elif op[4]:
                k = ("e", op[0])
                cnt[k] = cnt.get(k, 0) + 1
                val[i] = cnt[k]
        per_eng = {e: [] for e in self.ENGS}
        for i, op in enumerate(ops):
            per_eng[op[0]].append(i)

        def semof(j):
            oj = ops[j]
            if oj[3] is not None:
                return ("d", oj[3]), dsems[oj[3]]
            return ("e", oj[0]), sems[oj[0]]

        def run_engine(ename, e):
            seen = {}
            for i in per_eng[ename]:
                op = ops[i]
                need = {}
                for j in op[2]:
                    oj = ops[j]
                    if oj[3] is None and oj[0] == "pe" and ename == "pe" and op[3] is None:
                        continue
                    k, sh = semof(j)
                    v = val[j]
                    if seen.get(k, 0) >= v:
                        continue
                    if k not in need or need[k][1] < v:
                        need[k] = (sh, v)
                for k, (sh, v) in need.items():
                    e.wait_ge(sh, v)
                    seen[k] = v
                ins = op[1](e)
                if op[3] is not None:
                    ins.then_inc(dsems[op[3]], 16)
                elif op[4]:
                    ins.then_inc(sems[ename], 1)
            if ename == "sp":
                for k, i in last.items():
                    _, sh = semof(i)
                    if seen.get(k, 0) < val[i]:
                        e.wait_ge(sh, val[i])

        @block.tensor
        def _(e):
            run_engine("pe", e)

        @block.scalar
        def _(e):
            run_engine("act", e)

        @block.vector
        def _(e):
            run_engine("dve", e)

        @block.gpsimd
        def _(e):
            run_engine("pool", e)

        @block.sync
        def _(e):
            run_engine("sp", e)


class Geo:
    def __init__(self, S_P, S_S):
        self.S_P = S_P
        self.S_S = S_S
        self.Q = S_S // 4
        assert S_P % T == 0 and self.Q % T == 0
        self.NA = 2 * S_P + S_S
        self.NOWN = 2 * S_P + self.Q
        self.ntA = self.NA // T
        self.ntP = S_P // T
        self.ntQ = self.Q // T
        self.regions = [
            (0, self.ntP, self.ntP),
            (self.ntP, self.ntP, self.ntP),
            (2 * self.ntP, S_S // T, self.ntQ),
        ]
        self.halo_tile = 2 * self.ntP + self.ntQ
        self.own = []
        for r, (t0, nk, no) in enumerate(self.regions):
            for i in range(no):
                self.own.append((t0 + i, r, i))

    def sample_order(self, rank):
        nch = self.S_S // 128
        qch = self.Q // 128
        own = list(range(rank * qch, (rank + 1) * qch))
        nxt = ((rank + 1) * qch) % nch
        prv = (rank * qch - 1) % nch
        rest = [c for c in range(nch) if c not in own and c != nxt and c != prv]
        return own + [nxt, prv] + rest


def _rope_table(positions):
    inv_freq = (500000.0 ** (-(np.arange(0, 16, 2, dtype=np.float32)) / np.float32(16))).astype(np.float32)
    ang = positions.astype(np.float32)[:, None] * inv_freq[None, :]
    ang = ang.astype(np.float32)
    return np.concatenate([np.cos(ang), np.sin(ang)], axis=1).astype(np.float32)


def build_program(geo):
    nc = bass.Bass("TRN2", target_bir_lowering=False)
    NA, NOWN = geo.NA, geo.NOWN
    ntA = geo.ntA

    def din(name, shape, dt=F32):
        return nc.dram_tensor(name, list(shape), dt, kind="ExternalInput").ap()

    def dscr(name, shape, dt):
        return nc.dram_tensor(name, list(shape), dt, kind="Internal").ap()

    xA = din("xA", [NA, D])
    rope = din("rope", [NA, 16])
    smallp = din("smallp", [640, 128])
    ident_in = din("ident", [128, 128])
    hmask_in = din("hmask", [128, 2])
    ada_w = din("ada_w", [DEPTH, D, 9 * D])
    w_gate = din("ffn_w_gate", [DEPTH, 2, D, FF])
    w_up = din("ffn_w_up", [DEPTH, 2, D, FF])
    w_down = din("ffn_w_down", [DEPTH, 2, FF, D])
    sg_w_in = din("sg_w_in", [2, D, 6144])
    sg_ln_g = din("sg_ln_g", [2, SGH])
    sg_ln_b = din("sg_ln_b", [2, SGH])
    sg_w_s = din("sg_w_s", [2, 8, 128, 128])
    sg_b_s = din("sg_b_s", [2, 8, 128])
    sg_b_in = din("sg_b_in", [2, 6144])
    sg_w_out = din("sg_w_out", [2, SGH, D])
    da_w_qkv = din("da_w_qkv", [1, D, 3 * D])
    da_lambda = din("da_lambda", [1, 4, 64])
    da_w_out = din("da_w_out", [1, D, D])
    conv_w_in = din("conv_w_in", [1, D, 3 * D])
    conv_w_out = din("conv_w_out", [1, D, D])
    y_out = nc.dram_tensor("y", [NOWN, D], F32, kind="ExternalOutput").ap()
    import os as _os2
    DBG = int(_os2.environ.get("KSTOP", "99")) != 99
    dbg_out = nc.dram_tensor("dbg", [128, KC, T], F32, kind="ExternalOutput").ap() if DBG else None

    wg_s = dscr("wg_s", [DEPTH, 2, 11, 128, KC, 256], BF16)
    wu_s = dscr("wu_s", [DEPTH, 2, 11, 128, KC, 256], BF16)
    wd_s = dscr("wd_s", [DEPTH, 2, 4, 128, FC, 256], BF16)
    sgu_s = dscr("sgu_s", [2, 12, 128, KC, 256], BF16)
    sgv_s = dscr("sgv_s", [2, 6, 128, KC, 512], BF16)
    sgo_s = dscr("sgo_s", [2, 4, 128, VC, 256], BF16)
    qkv_s = dscr("qkv_s", [6, 128, KC, 512], BF16)
    wo_s = dscr("wo_s", [4, 128, KC, 256], BF16)
    cin_s = dscr("cin_s", [4, 128, KC, 3, 256], BF16)
    cout_s = dscr("cout_s", [4, 128, KC, 256], BF16)
    xa_s = dscr("xa_s", [ntA, 128, KC, T], F32)
    qT_s = dscr("qT_s", [ntA, 128, KC, T], BF16)
    kT_s = dscr("kT_s", [8, 128, NA], BF16)
    v_s = dscr("v_s", [NA, D], BF16)
    nown_t = len(geo.own)
    xb_s = dscr("xb_s", [nown_t, 128, KC, T], F32)
    bg_s = dscr("bg_s", [nown_t, 128, KC, T], F32)
    g_s = [dscr(f"g_s{r}", [128, KC, geo.regions[r][2] * T + 2], F32) for r in range(3)]

    es = ExitStack()
    with es:
        def sb(name, shape, dt=F32):
            return es.enter_context(nc.sbuf_tensor("sb_" + name, list(shape), dt))

        sems = {e: es.enter_context(nc.semaphore("sem_" + e)) for e in Sched.ENGS}
        dsem_names = (["slot%d" % i for i in range(NSLOTS)] +
                      ["x", "qt", "st_x", "st_q", "st_k", "st_v", "misc", "big", "prep", "rope", "gw", "bgl", "out"])
        dsems = {n: es.enter_context(nc.semaphore("dsem_" + n)) for n in dsem_names}
        block = es.enter_context(nc.Block())

        S = Sched()
        psum = [es.enter_context(nc.psum_tensor("ps%d" % b, [128, 512], F32)) for b in range(8)]
        ps_rr = [0]

        def ps_next(banks=(0, 1, 2, 3, 4, 5, 6, 7)):
            b = banks[ps_rr[0] % len(banks)]
            ps_rr[0] += 1
            return b

        x = sb("x", [128, KC, T])
        h = sb("h", [128, KC, T], BF16)
        big = sb("big", [128, 48 * 512], BF16)
        slots = [sb("slot%d" % i, [128, SLOT_ELEMS], BF16) for i in range(NSLOTS)]
        rstd = sb("rstd", [128, T])
        tmpA = sb("tmpA", [128, T])
        tmpB = sb("tmpB", [128, T])
        tmpC = sb("tmpC", [128, T])
        ones_f = sb("ones_f", [128, 128])
        ones_b = sb("ones_b", [128, 128], BF16)
        ident_f = sb("ident_f", [128, 128])
        ident_b = sb("ident_b", [128, 128], BF16)
        smallT = sb("smallT", [128, 640])
        stage = sb("stage", [128, 5, 128])
        A_t = sb("A_t", [128, DEPTH, 3, 3, KC])
        B_t = sb("B_t", [128, DEPTH, 3, 3, KC])
        G_t = sb("G_t", [128, DEPTH, 3, 3, KC])
        cact = sb("cact", [128, KC, 3])
        neglam = sb("neglam", [128, 1])
        sublng = sb("sublng", [128, 1])
        hmask = sb("hmask", [128, 2])
        lamrow = sb("lamrow", [1, 256])
        lamw = sb("lamw", [1, 8])
        vg = sb("vg", [128, SGH])
        modt = vg[:, 0:DEPTH * 72 * 3].rearrange("p (l n b) -> p l n b", l=DEPTH, n=72, b=3)
        vn = [sb("vn%d" % i, [128, SGH], BF16) for i in range(2)]
        biasT = sb("biasT", [128, VC, 128])
        gsc = sb("gsc", [128, VC])
        wsT = sb("wsT", [128, 8, 128], BF16)
        bvrow = sb("bvrow", [1, SGH], BF16)
        bn6 = sb("bn6", [128, 6, 6])
        mv = sb("mv", [128, 4])
        ropet = sb("ropet", [128, 4, 16])
        pT = [sb("pT%d" % i, [128, T], BF16) for i in range(4)]
        spt_bufs = [sb("spt%d" % i, [128, T]) for i in range(2)]
        sqb = [sb("sqb%d" % i, [128, T]) for i in range(2)]
        sqacc = sb("sqacc", [128, T])
        pre_state = {"bank": None}
        lnb_fm = sb("lnb_fm", [128, VC])
        ob = sb("ob", [128, KC, T], BF16)
        zcol = sb("zcol", [128, KC, 1])
        eps_t = sb("eps_t", [128, 1])
        hcol = sb("hcol", [128, KC, 2])

        C_ADAB, C_NORMG, C_BIN, C_CONVK, C_FING, C_SUBLN, C_C = 0, 288, 384, 480, 504, 512, 513

        def bigrows(r0, n):
            return [("big", r) for r in range(r0, r0 + n)]

        def big_bf(r0, n):
            return big[:, r0 * 512:(r0 + n) * 512].rearrange("p (f t) -> p f t", t=512)

        def big_f32(r0, nrows, inner):
            v = big[:, r0 * 512:(r0 + nrows) * 512].bitcast(F32)
            return v.rearrange("p (a b) -> p a b", b=inner)

        class WStream:
            def __init__(self, rec=None):
                self.rec = rec
                self.out = []
                self.i = 0
                self.issued = 0

            def _issue(self, S, idx):
                req, rds = self.rec[idx]
                s = idx % NSLOTS
                for (off, n, src, dshape) in req:
                    dst = slots[s][:, off:off + n]
                    if dshape is not None:
                        dst = dst.rearrange(dshape[0], **dshape[1])
                    S.add("sp", (lambda e, dst=dst, src=src: e.dma_start(out=dst, in_=src)),
                          reads=rds, writes=[("slot", s)], dsem="slot%d" % s)

            def next(self, S, req, rds=()):
                idx = self.i
                self.i += 1
                if self.rec is None:
                    self.out.append((req, list(rds)))
                    return slots[idx % NSLOTS], idx % NSLOTS
                while self.issued < len(self.rec) and self.issued <= idx + NSLOTS - 2:
                    self._issue(S, self.issued)
                    self.issued += 1
                return slots[idx % NSLOTS], idx % NSLOTS

        def mm(S, out, lhsT, rhs, start, stop, reads, bank):
            S.add("pe", (lambda e: e.matmul(out=out, lhsT=lhsT, rhs=rhs, start=start, stop=stop)),
                  reads=reads, writes=[("ps", bank)])

        def stats_mm(S, c, sbank):
            if c == KC - 1:
                mm(S, psum[sbank][:, :], ones_f[:, :], sqacc[:, :], True, True, ["sqacc"], sbank)

        def resid(S, bk, dc, g_ap, sbank):
            S.add("dve", (lambda e, bk=bk, dc=dc, g_ap=g_ap: e.scalar_tensor_tensor(
                out=x[:, dc, :], in0=psum[bk][:, :], scalar=g_ap, in1=x[:, dc, :],
                op0=ALU.mult, op1=ALU.add)),
                  reads=[("ps", bk), ("x", dc)], writes=[("x", dc)])
            S.add("act", (lambda e, dc=dc: e.activation(out=sqb[dc % 2][:, :], in_=x[:, dc, :], func=AF.Square)),
                  reads=[("x", dc)], writes=[("sqb", dc % 2)])
            if dc == 0:
                S.add("pool", (lambda e: e.tensor_copy(out=sqacc[:, :], in_=sqb[0][:, :])),
                      reads=[("sqb", 0)], writes=["sqacc"])
            else:
                S.add("pool", (lambda e, dc=dc: e.tensor_tensor(out=sqacc[:, :], in0=sqacc[:, :], in1=sqb[dc % 2][:, :],
                                                                 op=ALU.add)),
                      reads=[("sqb", dc % 2), "sqacc"], writes=["sqacc"])

        def norm_mod(S, li, sl, b, hout=None, final=False):
            if pre_state["bank"] is not None:
                bk = pre_state["bank"]
                pre_state["bank"] = None
            else:
                sq = big_f32(0, 16, T)
                for c in range(KC):
                    S.add("act", (lambda e, c=c: e.activation(out=sq[:, c, :], in_=x[:, c, :], func=AF.Square)),
                          reads=[("x", c)], writes=bigrows(2 * c, 2))
                bk = ps_next()
                for c in range(KC):
                    mm(S, psum[bk][:, :], ones_f[:, :], sq[:, c, :], c == 0, c == KC - 1,
                       bigrows(2 * c, 2), bk)
            S.add("act", (lambda e: e.activation(out=tmpA[:, :], in_=psum[bk][:, :], func=AF.Sqrt,
                                                 bias=eps_t[:, 0:1], scale=1.0 / D)),
                  reads=[("ps", bk), "eps_t"], writes=["tmpA"])
            S.add("dve", (lambda e: e.reciprocal(out=rstd[:, :], in_=tmpA[:, :])),
                  reads=["tmpA"], writes=["rstd"])
            for c in range(KC):
                if final:
                    a_ap = smallT[:, C_FING + c:C_FING + c + 1]
                    S.add("dve", (lambda e, c=c, a_ap=a_ap: e.scalar_tensor_tensor(
                        out=hout[:, c, :], in0=x[:, c, :], scalar=a_ap, in1=rstd[:, :],
                        op0=ALU.mult, op1=ALU.mult)),
                          reads=[("x", c), "rstd"], writes=bigrows(2 * c, 2))
                    continue
                tt = tmpB if c % 2 == 0 else tmpC
                tk = "tmpB" if c % 2 == 0 else "tmpC"
                a_ap = A_t[:, li, sl, b, c:c + 1]
                b_ap = B_t[:, li, sl, b, c:c + 1]
                S.add("dve", (lambda e, c=c, tt=tt, a_ap=a_ap: e.scalar_tensor_tensor(
                    out=tt[:, :], in0=x[:, c, :], scalar=a_ap, in1=rstd[:, :],
                    op0=ALU.mult, op1=ALU.mult)),
                      reads=[("x", c), "rstd"], writes=[tk])
                S.add("act", (lambda e, c=c, tt=tt, b_ap=b_ap: e.activation(
                    out=h[:, c, :], in_=tt[:, :], func=AF.Identity, bias=b_ap, scale=1.0)),
                      reads=[tk], writes=[("h", c)])

        def ffn(S, ws, li, fi, b, sl):
            norm_mod(S, li, sl, b)
            hid = big_bf(0, FC)
            HB = (0, 1, 2, 3)
            for fb in range(11):
                sl_t, sidx = ws.next(S, [(0, 2048, wg_s[li, fi, fb], ("p (k f) -> p k f", dict(f=256))),
                                         (2048, 2048, wu_s[li, fi, fb], ("p (k f) -> p k f", dict(f=256)))])
                wgv = sl_t[:, 0:2048].rearrange("p (k f) -> p k f", f=256)
                wuv = sl_t[:, 2048:4096].rearrange("p (k f) -> p k f", f=256)
                pre_banks = None
                if fb == 0:
                    pre_banks = [ps_next(HB) for _ in range(4)]
                    for kc in range(KC):
                        for gi, (wv_, fcl_) in enumerate(((wgv, 0), (wuv, 0), (wgv, 1), (wuv, 1))):
                            mm(S, psum[pre_banks[gi]][:, :], wv_[:, kc, fcl_ * 128:(fcl_ + 1) * 128], h[:, kc, :],
                               kc == 0, kc == KC - 1, [("slot", sidx), ("h", kc)], pre_banks[gi])
                for fcl in range(2):
                    f = fb * 2 + fcl
                    if pre_banks is not None:
                        bg = pre_banks[2 * fcl]
                        bu = pre_banks[2 * fcl + 1]
                    else:
                        bg = ps_next(HB)
                        for kc in range(KC):
                            mm(S, psum[bg][:, :], wgv[:, kc, fcl * 128:(fcl + 1) * 128], h[:, kc, :],
                               kc == 0, kc == KC - 1, [("slot", sidx), ("h", kc)], bg)
                        bu = ps_next(HB)
                        for kc in range(KC):
                            mm(S, psum[bu][:, :], wuv[:, kc, fcl * 128:(fcl + 1) * 128], h[:, kc, :],
                               kc == 0, kc == KC - 1, [("slot", sidx), ("h", kc)], bu)
                    tt = tmpB if f % 2 == 0 else tmpC
                    tk = "tmpB" if f % 2 == 0 else "tmpC"
                    S.add("act", (lambda e, bg=bg, tt=tt: e.activation(out=tt[:, :], in_=psum[bg][:, :], func=AF.Silu)),
                          reads=[("ps", bg)], writes=[tk])
                    S.add("dve", (lambda e, bu=bu, tt=tt, f=f: e.tensor_tensor(
                        out=hid[:, f, :], in0=psum[bu][:, :], in1=tt[:, :], op=ALU.mult)),
                          reads=[("ps", bu), tk], writes=bigrows(f, 1))
            DB = (4, 5, 6, 7)
            for db in range(4):
                sl_t, sidx = ws.next(S, [(0, FC * 256, wd_s[li, fi, db], ("p (k f) -> p k f", dict(f=256)))])
                wdv = sl_t[:, 0:FC * 256].rearrange("p (k f) -> p k f", f=256)
                for dcl in range(2):
                    dc = db * 2 + dcl
                    bk = ps_next(DB)
                    for f in range(FC):
                        mm(S, psum[bk][:, :], wdv[:, f, dcl * 128:(dcl + 1) * 128], hid[:, f, :],
                           f == 0, f == FC - 1, [("slot", sidx)] + bigrows(f, 1), bk)
                    if dc >= 1:
                        stats_mm(S, dc - 1, 0)
                    g_ap = G_t[:, li, sl, b, dc:dc + 1]
                    resid(S, bk, dc, g_ap, 0)
            stats_mm(S, KC - 1, 0)
            pre_state["bank"] = 0

        def gmlp_setup(S, j):
            S.add("sp", (lambda e: e.dma_start(out=stage[0:24, 0, :], in_=sg_ln_g[j].rearrange("(a f) -> a f", f=128))),
                  reads=(), writes=[("stage", 0)], dsem="misc")
            S.add("sp", (lambda e: e.dma_start(out=stage[0:24, 1, :], in_=sg_ln_b[j].rearrange("(a f) -> a f", f=128))),
                  reads=(), writes=[("stage", 1)], dsem="misc")
            bk = ps_next()
            S.add("pe", (lambda e, bk=bk: e.transpose(out=psum[bk][:, 0:24], in_=stage[0:24, 0, :], identity=ident_f[0:24, 0:24])),
                  reads=[("stage", 0), "ident_f"], writes=[("ps", bk)])
            S.add("dve", (lambda e, bk=bk: e.tensor_copy(out=gsc[:, :], in_=psum[bk][:, 0:24])),
                  reads=[("ps", bk)], writes=["gsc"])
            bk2 = ps_next()
            S.add("pe", (lambda e, bk2=bk2: e.transpose(out=psum[bk2][:, 0:24], in_=stage[0:24, 1, :], identity=ident_f[0:24, 0:24])),
                  reads=[("stage", 1), "ident_f"], writes=[("ps", bk2)])
            S.add("dve", (lambda e, bk2=bk2: e.tensor_copy(out=lnb_fm[:, :], in_=psum[bk2][:, 0:24])),
                  reads=[("ps", bk2)], writes=["lnb_fm"])
            S.add("sp", (lambda e: e.dma_start(out=vg[:, 0:1024],
                                               in_=sg_b_s[j].rearrange("g q -> (g q)").partition_broadcast(128))),
                  reads=(), writes=[("vg", 0), ("vg", 1), "modt"], dsem="misc")
            for g in range(8):
                sidx = 2 + g % 2
                S.add("sp", (lambda e, g=g, sidx=sidx: e.dma_start(out=stage[:, sidx, :], in_=sg_w_s[j, g])),
                      reads=(), writes=[("stage", sidx)], dsem="misc")
                bk = ps_next()
                S.add("pe", (lambda e, sidx=sidx, bk=bk: e.transpose(out=psum[bk][:, 0:128], in_=stage[:, sidx, :],
                                                                      identity=ident_f[:, :])),
                      reads=[("stage", sidx), "ident_f"], writes=[("ps", bk)])
                S.add("act", (lambda e, bk=bk: e.activation(out=stage[:, 4, :], in_=psum[bk][:, 0:128], func=AF.Copy)),
                      reads=[("ps", bk)], writes=[("stage", 4)])
                S.add("dve", (lambda e, g=g: e.tensor_copy(out=wsT[:, g, :], in_=stage[:, 4, :])),
                      reads=[("stage", 4)], writes=["wsT"])
                bk3 = ps_next()
                S.add("pe", (lambda e, bk3=bk3: e.matmul(out=psum[bk3][:, 0:128], lhsT=ones_f[:, :], rhs=stage[:, 4, :],
                                                          start=True, stop=True)),
                      reads=[("stage", 4), "ones_f"], writes=[("ps", bk3)])
                for dcl in range(3):
                    vc = g * 3 + dcl
                    S.add("dve", (lambda e, g=g, vc=vc, bk3=bk3: e.scalar_tensor_tensor(
                        out=biasT[:, vc, :], in0=psum[bk3][:, 0:128], scalar=lnb_fm[:, vc:vc + 1],
                        in1=vg[:, g * 128:(g + 1) * 128], op0=ALU.mult, op1=ALU.add)),
                          reads=[("ps", bk3), "lnb_fm", ("vg", 0), ("vg", 1)], writes=[("biasT", vc)])
            S.add("sp", (lambda e: e.dma_start(out=vg[0:1, :], in_=sg_b_in[j:j + 1, SGH:2 * SGH])),
                  reads=[("biasT", v_) for v_ in range(VC)], writes=[("vg", i_) for i_ in range(6)], dsem="misc")
            S.add("dve", (lambda e: e.tensor_copy(out=bvrow[:, :], in_=vg[0:1, :])),
                  reads=[("vg", i_) for i_ in range(6)], writes=["bvrow"])

        def gmlp(S, ws, li, j, b):
            norm_mod(S, li, 1, b)
            u = big_bf(0, VC)
            m = big_bf(24, VC)
            UB = (0, 1)
            for ub in range(12):
                sl_t, sidx = ws.next(S, [(0, 2048, sgu_s[j, ub], None)])
                wv = sl_t[:, 0:2048].rearrange("p (k f) -> p k f", f=256)
                for fcl in range(2):
                    fcx = ub * 2 + fcl
                    bk = ps_next(UB)
                    for kc in range(KC):
                        mm(S, psum[bk][:, :], wv[:, kc, fcl * 128:(fcl + 1) * 128], h[:, kc, :],
                           kc == 0, kc == KC - 1, [("slot", sidx), ("h", kc)], bk)
                    bias_ap = smallT[:, C_BIN + j * 48 + fcx:C_BIN + j * 48 + fcx + 1]
                    S.add("act", (lambda e, bk=bk, fcx=fcx, bias_ap=bias_ap: e.activation(
                        out=u[:, fcx, :], in_=psum[bk][:, :], func=AF.Gelu, bias=bias_ap, scale=1.0)),
                          reads=[("ps", bk)], writes=bigrows(fcx, 1))
            VB = (2, 3, 4, 5)
            SB_ = (6, 7)
            def vpart(st):
                vnb = vn[st % 2]
                vnk = "vn%d" % (st % 2)
                for vb in range(6):
                    sl_t, sidx = ws.next(S, [(0, 4096, sgv_s[j, vb], None)])
                    wv = sl_t[:, 0:4096].rearrange("p (k f) -> p k f", f=512)
                    bk = ps_next(VB)
                    for kc in range(KC):
                        mm(S, psum[bk][:, :], h[:, kc, st * 128:(st + 1) * 128], wv[:, kc, :],
                           kc == 0, False, [("slot", sidx), ("h", kc)], bk)
                    mm(S, psum[bk][:, :], ones_b[0:1, :], bvrow[0:1, vb * 512:(vb + 1) * 512],
                       False, True, ["bvrow"], bk)
                    S.add("act", (lambda e, bk=bk, vb=vb: e.activation(
                        out=vg[:, vb * 512:(vb + 1) * 512], in_=psum[bk][:, :], func=AF.Gelu)),
                          reads=[("ps", bk)], writes=[("vg", vb)])
                    S.add("dve", (lambda e, vb=vb: e.bn_stats(out=bn6[:, vb, :], in_=vg[:, vb * 512:(vb + 1) * 512])),
                          reads=[("vg", vb)], writes=[("bn6", vb)])

            def lnpart(st):
                vnb = vn[st % 2]
                vnk = "vn%d" % (st % 2)
                S.add("dve", (lambda e: e.bn_aggr(out=mv[:, 0:2], in_=bn6[:, :, :])),
                      reads=[("bn6", i) for i in range(6)], writes=["mv"])
                S.add("act", (lambda e: e.activation(out=mv[:, 2:3], in_=mv[:, 1:2], func=AF.Sqrt,
                                                     bias=eps_t[:, 0:1], scale=1.0)),
                      reads=["mv", "eps_t"], writes=["mv2"])
                S.add("dve", (lambda e: e.reciprocal(out=mv[:, 2:3], in_=mv[:, 2:3])),
                      reads=["mv2"], writes=["mv2"])
                S.add("dve", (lambda e: e.scalar_tensor_tensor(out=mv[:, 3:4], in0=mv[:, 0:1], scalar=-1.0,
                                                               in1=mv[:, 2:3], op0=ALU.mult, op1=ALU.mult)),
                      reads=["mv", "mv2"], writes=["mv3"])
                for vb in range(6):
                    S.add("act", (lambda e, vb=vb, vnb=vnb: e.activation(
                        out=vnb[:, vb * 512:(vb + 1) * 512], in_=vg[:, vb * 512:(vb + 1) * 512],
                        func=AF.Identity, bias=mv[:, 3:4], scale=mv[:, 2:3])),
                          reads=[("vg", vb), "mv2", "mv3"], writes=[(vnk, vb)])

            def sppart(st):
                vnb = vn[st % 2]
                vnk = "vn%d" % (st % 2)
                for grp in range(6):
                    bk = ps_next(SB_)
                    for i4 in range(4):
                        vc = grp * 4 + i4
                        g = vc // 3
                        mm(S, psum[bk][:, i4 * 128:(i4 + 1) * 128], vnb[:, vc * 128:(vc + 1) * 128], wsT[:, g, :],
                           True, True, [(vnk, vc // 4), "wsT"], bk)
                    sbi = (st * 6 + grp) % 2
                    spt = spt_bufs[sbi]
                    for i4 in range(4):
                        vc = grp * 4 + i4
                        S.add("dve", (lambda e, bk=bk, i4=i4, vc=vc, spt=spt: e.scalar_tensor_tensor(
                            out=spt[:, i4 * 128:(i4 + 1) * 128], in0=psum[bk][:, i4 * 128:(i4 + 1) * 128],
                            scalar=gsc[:, vc:vc + 1], in1=biasT[:, vc, :], op0=ALU.mult, op1=ALU.add)),
                              reads=[("ps", bk), ("biasT", vc), "gsc"], writes=[("spt", sbi, i4)])
                    S.add("pool", (lambda e, grp=grp, st=st, spt=spt: e.tensor_tensor(
                        out=m[:, grp * 4:(grp + 1) * 4, st * 128:(st + 1) * 128],
                        in0=spt[:, :].rearrange("p (a q) -> p a q", q=128),
                        in1=u[:, grp * 4:(grp + 1) * 4, st * 128:(st + 1) * 128], op=ALU.mult)),
                          reads=[("spt", sbi, i) for i in range(4)] + bigrows(grp * 4, 4),
                          writes=bigrows(24 + grp * 4, 4))

            vpart(0)
            lnpart(0)
            for st in range(4):
                if st + 1 < 4:
                    vpart(st + 1)
                sppart(st)
                if st + 1 < 4:
                    lnpart(st + 1)
            OB = (0, 1, 2, 3)
            for db in range(4):
                sl_t, sidx = ws.next(S, [(0, VC * 256, sgo_s[j, db], None)])
                wv = sl_t[:, 0:VC * 256].rearrange("p (k f) -> p k f", f=256)
                for dcl in range(2):
                    dc = db * 2 + dcl
                    bk = ps_next(OB)
                    for vc in range(VC):
                        mm(S, psum[bk][:, :], wv[:, vc, dcl * 128:(dcl + 1) * 128], m[:, vc, :],
                           vc == 0, vc == VC - 1, [("slot", sidx)] + bigrows(24 + vc, 1), bk)
                    if dc >= 1:
                        stats_mm(S, dc - 1, 4)
                    g_ap = G_t[:, li, 1, b, dc:dc + 1]
                    resid(S, bk, dc, g_ap, 4)
            stats_mm(S, KC - 1, 4)
            pre_state["bank"] = 4

        def load_x_tokens(S, tile_idx):
            xin = big_f32(0, 16, D)
            S.add("sp", (lambda e: e.dma_start(out=xin, in_=xA[tile_idx * T:(tile_idx + 1) * T, :]
                                               .rearrange("(s p) d -> p s d", p=128))),
                  reads=(), writes=bigrows(0, 16), dsem="big")
            for c in range(KC):
                bk = ps_next()
                for st in range(4):
                    S.add("pe", (lambda e, c=c, st=st, bk=bk: e.transpose(
                        out=psum[bk][:, st * 128:(st + 1) * 128], in_=xin[:, st, c * 128:(c + 1) * 128],
                        identity=ident_f[:, :])),
                          reads=bigrows(4 * st, 4) + ["ident_f"], writes=[("ps", bk)])
                eng = "act" if c % 2 == 0 else "dve"
                if eng == "act":
                    S.add("act", (lambda e, c=c, bk=bk: e.activation(out=x[:, c, :], in_=psum[bk][:, :], func=AF.Copy)),
                          reads=[("ps", bk)], writes=[("x", c)])
                else:
                    S.add("dve", (lambda e, c=c, bk=bk: e.tensor_copy(out=x[:, c, :], in_=psum[bk][:, :])),
                          reads=[("ps", bk)], writes=[("x", c)])

        def qkv_rotary(S, ws, tile_idx, b):
            norm_mod(S, 1, 1, b)
            qk = big_f32(0, 32, 2048)
            vbf = big_bf(32, 8).rearrange("p a t -> p (a t)").rearrange("p (s d) -> p s d", d=D)
            qT = big_bf(40, 8)
            S.add("sp", (lambda e: e.dma_start(out=ropet[:, :, :], in_=rope[tile_idx * T:(tile_idx + 1) * T, :]
                                               .rearrange("(s p) c -> p s c", p=128))),
                  reads=(), writes=["ropet"], dsem="rope")
            QB = (0, 1, 2, 3)
            for st in range(4):
                for cb in range(6):
                    sl_t, sidx = ws.next(S, [(0, 4096, qkv_s[cb], None)])
                    wv = sl_t[:, 0:4096].rearrange("p (k f) -> p k f", f=512)
                    bk = ps_next(QB)
                    for kc in range(KC):
                        mm(S, psum[bk][:, :], h[:, kc, st * 128:(st + 1) * 128], wv[:, kc, :],
                           kc == 0, kc == KC - 1, [("slot", sidx), ("h", kc)], bk)
                    if cb < 2:
                        S.add("act", (lambda e, bk=bk, st=st, cb=cb: e.activation(
                            out=qk[:, st, cb * 512:(cb + 1) * 512], in_=psum[bk][:, :], func=AF.Copy, scale=0.125)),
                              reads=[("ps", bk)], writes=bigrows(8 * st + 2 * cb, 2))
                    elif cb < 4:
                        S.add("dve", (lambda e, bk=bk, st=st, cb=cb: e.tensor_copy(
                            out=qk[:, st, cb * 512:(cb + 1) * 512], in_=psum[bk][:, :])),
                              reads=[("ps", bk)], writes=bigrows(8 * st + 2 * cb, 2))
                    else:
                        S.add("act", (lambda e, bk=bk, st=st, cb=cb: e.activation(
                            out=vbf[:, st, (cb - 4) * 512:(cb - 3) * 512], in_=psum[bk][:, :], func=AF.Copy)),
                              reads=[("ps", bk)], writes=bigrows(32 + 2 * st + (cb - 4), 1))
                blk = qk[:, st, :].rearrange("p (a d) -> p a d", d=64)
                x1 = blk[:, :, 0:8]
                x2 = blk[:, :, 8:16]
                cosb = ropet[:, st, 0:8].unsqueeze(1).broadcast_to([128, 32, 8])
                sinb = ropet[:, st, 8:16].unsqueeze(1).broadcast_to([128, 32, 8])
                t1 = tmpA[:, 0:256].rearrange("p (a d) -> p a d", d=8)
                t2 = tmpA[:, 256:512].rearrange("p (a d) -> p a d", d=8)
                t3 = tmpB[:, 0:256].rearrange("p (a d) -> p a d", d=8)
                t4 = tmpB[:, 256:512].rearrange("p (a d) -> p a d", d=8)
                rows = bigrows(8 * st, 8)
                S.add("dve", (lambda e, x1=x1, cosb=cosb, t1=t1: e.tensor_tensor(out=t1, in0=x1, in1=cosb, op=ALU.mult)),
                      reads=rows + ["ropet"], writes=["t1"])
                S.add("pool", (lambda e, x2=x2, sinb=sinb, t2=t2: e.tensor_tensor(out=t2, in0=x2, in1=sinb, op=ALU.mult)),
                      reads=rows + ["ropet"], writes=["t2"])
                S.add("dve", (lambda e, x2=x2, cosb=cosb, t3=t3: e.tensor_tensor(out=t3, in0=x2, in1=cosb, op=ALU.mult)),
                      reads=rows + ["ropet"], writes=["t3"])
                S.add("pool", (lambda e, x1=x1, sinb=sinb, t4=t4: e.tensor_tensor(out=t4, in0=x1, in1=sinb, op=ALU.mult)),
                      reads=rows + ["ropet"], writes=["t4"])
                S.add("dve", (lambda e, x1=x1, t1=t1, t2=t2: e.tensor_tensor(out=x1, in0=t1, in1=t2, op=ALU.subtract)),
                      reads=["t1", "t2"], writes=rows)
                S.add("pool", (lambda e, x2=x2, t3=t3, t4=t4: e.tensor_tensor(out=x2, in0=t3, in1=t4, op=ALU.add)),
                      reads=["t3", "t4"], writes=rows)
                qkb = vn[st % 2][:, 0:2048]
                qkbk = "vn%d" % (st % 2)
                S.add("act", (lambda e, st=st, qkb=qkb: e.activation(out=qkb, in_=qk[:, st, :], func=AF.Copy)),
                      reads=rows, writes=[(qkbk, i) for i in range(6)])
                for half in range(2):
                    bk = ps_next((4, 5, 6, 7))
                    pst = psum[bk][:, :].bitcast(BF16)
                    for hh in range(8):
                        S.add("pe", (lambda e, pst=pst, hh=hh, half=half, qkb=qkb: e.transpose(
                            out=pst[:, hh * 128:(hh + 1) * 128],
                            in_=qkb[:, half * 1024 + hh * 128: half * 1024 + (hh + 1) * 128],
                            identity=ident_b[:, :])),
                              reads=[(qkbk, i) for i in range(6)] + ["ident_b"], writes=[("ps", bk)])
                    if half == 0:
                        S.add("dve", (lambda e, pst=pst, st=st: e.tensor_copy(
                            out=qT[:, :, st * 128:(st + 1) * 128], in_=pst.rearrange("p (a t) -> p a t", t=128))),
                              reads=[("ps", bk)], writes=bigrows(40, 8))
                    else:
                        S.add("dve", (lambda e, pst=pst, st=st: e.tensor_copy(
                            out=ob[:, :, st * 128:(st + 1) * 128], in_=pst.rearrange("p (a t) -> p a t", t=128))),
                              reads=[("ps", bk)], writes=[("kT", st)])
            S.add("pool", (lambda e: e.dma_start(out=xa_s[tile_idx], in_=x[:, :, :])),
                  reads=[("x", c) for c in range(KC)], writes=[("xa_s", tile_idx)], dsem="st_x")
            S.add("pool", (lambda e: e.dma_start(out=qT_s[tile_idx], in_=qT)),
                  reads=bigrows(40, 8), writes=[("qT_s", tile_idx)], dsem="st_q")
            S.add("pool", (lambda e: e.dma_start(
                out=kT_s[:, :, tile_idx * T:(tile_idx + 1) * T].rearrange("a p t -> p a t"), in_=ob[:, :, :])),
                  reads=[("kT", s_) for s_ in range(4)], writes=[("kT_s", tile_idx)], dsem="st_k")
            S.add("pool", (lambda e: e.dma_start(
                out=v_s[tile_idx * T:(tile_idx + 1) * T, :].rearrange("(s p) d -> p s d", p=128), in_=vbf)),
                  reads=bigrows(32, 8), writes=[("v_s", tile_idx)], dsem="st_v")

        def attention(S, ws, tile_idx, region, b):
            t0k, nkt, _ = geo.regions[region]
            nkeys = nkt * T
            key0 = t0k * T
            qT = big_bf(40, 8)
            S.add("sp", (lambda e: e.dma_start(out=qT, in_=qT_s[tile_idx])),
                  reads=[("qT_s", tile_idx)], writes=bigrows(40, 8), dsem="qt")
            KB = 2048 if nkeys >= 2048 else nkeys
            nkb = nkeys // KB
            SCB = (0, 1, 2, 3)
            lam_init = 0.8 - 0.6 * math.exp(-0.3 * 1)
            for hh in range(8):
                nchunks = nkeys // 128
                cpb = KB // 128

                state = {}

                def get_chunk(ci, hh=hh, state=state):
                    kb = ci // cpb
                    if state.get("kb") != kb:
                        k_src = kT_s[hh, :, key0 + kb * KB: key0 + (kb + 1) * KB]
                        v_src = v_s[key0 + kb * KB: key0 + (kb + 1) * KB, hh * 128:(hh + 1) * 128] \
                            .rearrange("(c p) e -> p c e", p=128)
                        kt0 = (key0 + kb * KB) // T
                        kt1 = (key0 + (kb + 1) * KB - 1) // T
                        rds = [("kT_s", t_) for t_ in range(kt0, kt1 + 1)] + [("v_s", t_) for t_ in range(kt0, kt1 + 1)]
                        sl_t, sidx = ws.next(S, [(0, KB, k_src, None),
                                                 (2048, KB, v_src, ("p (c e) -> p c e", dict(e=128)))], rds)
                        state["kb"] = kb
                        state["sl"] = (sl_t, sidx)
                    sl_t, sidx = state["sl"]
                    kcl = ci % cpb
                    return (sl_t[:, kcl * 128:(kcl + 1) * 128],
                            sl_t[:, 2048 + kcl * 128: 2048 + (kcl + 1) * 128], [("slot", sidx)])

                def emit_scores(ci, hh=hh):
                    kTc, vch, kv_reads = get_chunk(ci)
                    b0 = SCB[(2 * ci) % 4]
                    b1 = SCB[(2 * ci + 1) % 4]
                    S.add("pe", (lambda e, b0=b0, kTc=kTc, hh=hh: e.matmul(
                        out=psum[b0][:, :], lhsT=kTc[0:64, :], rhs=qT[0:64, hh, :], start=True, stop=True)),
                          reads=kv_reads + bigrows(40 + hh, 1), writes=[("ps", b0)])
                    S.add("pe", (lambda e, b1=b1, kTc=kTc, hh=hh: e.matmul(
                        out=psum[b1][:, :], lhsT=kTc[64:128, :], rhs=qT[64:128, hh, :], start=True, stop=True)),
                          reads=kv_reads + bigrows(40 + hh, 1), writes=[("ps", b1)])
                    p0 = pT[(ci % 2) * 2]
                    p1 = pT[(ci % 2) * 2 + 1]
                    k0 = "pT%d" % ((ci % 2) * 2)
                    k1 = "pT%d" % ((ci % 2) * 2 + 1)
                    S.add("act", (lambda e, b0=b0, p0=p0: e.activation(out=p0[:, :], in_=psum[b0][:, :], func=AF.Exp)),
                          reads=[("ps", b0)], writes=[k0])
                    S.add("act", (lambda e, b1=b1, p1=p1: e.activation(out=p1[:, :], in_=psum[b1][:, :], func=AF.Exp)),
                          reads=[("ps", b1)], writes=[k1])
                    return (vch, kv_reads, p0, p1, k0, k1)

                pend = emit_scores(0)
                for ci in range(nchunks):
                    cur = pend
                    if ci + 1 < nchunks:
                        pend = emit_scores(ci + 1)
                    vch, kv_reads, p0, p1, k0, k1 = cur
                    first = ci == 0
                    lastc = ci == nchunks - 1
                    mm(S, psum[4][:, :], vch, p0[:, :], first, lastc, kv_reads + [k0], 4)
                    mm(S, psum[5][:, :], ones_b[:, :], p0[:, :], first, lastc, [k0], 5)
                    mm(S, psum[6][:, :], vch, p1[:, :], first, lastc, kv_reads + [k1], 6)
                    mm(S, psum[7][:, :], ones_b[:, :], p1[:, :], first, lastc, [k1], 7)
                S.add("dve", (lambda e: e.reciprocal(out=tmpA[:, :], in_=psum[5][:, :])),
                      reads=[("ps", 5)], writes=["tmpA"])
                S.add("dve", (lambda e: e.reciprocal(out=tmpB[:, :], in_=psum[7][:, :])),
                      reads=[("ps", 7)], writes=["tmpB"])
                S.add("dve", (lambda e: e.tensor_tensor(out=tmpA[:, :], in0=psum[4][:, :], in1=tmpA[:, :], op=ALU.mult)),
                      reads=[("ps", 4), "tmpA"], writes=["tmpA"])
                S.add("dve", (lambda e: e.tensor_tensor(out=tmpB[:, :], in0=psum[6][:, :], in1=tmpB[:, :], op=ALU.mult)),
                      reads=[("ps", 6), "tmpB"], writes=["tmpB"])
                S.add("dve", (lambda e: e.scalar_tensor_tensor(out=tmpC[:, :], in0=tmpB[:, :], scalar=neglam[:, 0:1],
                                                               in1=tmpA[:, :], op0=ALU.mult, op1=ALU.add)),
                      reads=["tmpA", "tmpB", "neglam"], writes=["tmpC"])
                S.add("act", (lambda e: e.activation(out=rstd[:, :], in_=tmpC[:, :], func=AF.Square)),
                      reads=["tmpC"], writes=["rstd"])
                bk = ps_next(SCB)
                mm(S, psum[bk][:, :], ones_f[:, :], rstd[:, :], True, True, ["rstd"], bk)
                S.add("act", (lambda e, bk=bk: e.activation(out=tmpA[:, :], in_=psum[bk][:, :], func=AF.Sqrt,
                                                            bias=eps_t[:, 0:1], scale=1.0 / 128)),
                      reads=[("ps", bk), "eps_t"], writes=["tmpA"])
                S.add("dve", (lambda e: e.reciprocal(out=tmpB[:, :], in_=tmpA[:, :])),
                      reads=["tmpA"], writes=["tmpB"])
                S.add("dve", (lambda e, hh=hh: e.scalar_tensor_tensor(
                    out=ob[:, hh, :], in0=tmpC[:, :], scalar=sublng[:, 0:1], in1=tmpB[:, :],
                    op0=ALU.mult, op1=ALU.mult)),
                      reads=["tmpC", "tmpB", "sublng"], writes=[("ob", hh)])
            for db in range(4):
                sl_t, sidx = ws.next(S, [(0, 2048, wo_s[db], None)])
                wv = sl_t[:, 0:2048].rearrange("p (k f) -> p k f", f=256)
                for dcl in range(2):
                    dc = db * 2 + dcl
                    bk = ps_next(SCB)
                    for hh in range(8):
                        mm(S, psum[bk][:, :], wv[:, hh, dcl * 128:(dcl + 1) * 128], ob[:, hh, :],
                           hh == 0, hh == 7, [("slot", sidx), ("ob", hh)], bk)
                    if dc >= 1:
                        stats_mm(S, dc - 1, 4)
                    g_ap = G_t[:, 1, 1, b, dc:dc + 1]
                    resid(S, bk, dc, g_ap, 4)
            stats_mm(S, KC - 1, 4)
            pre_state["bank"] = 4

        def conv_in(S, ws, b, own_idx, region, tin, is_halo):
            norm_mod(S, 2, 1, b)
            bgt = big_f32(0, 16, T)
            gt = big_f32(16, 16, T)
            CB = (0, 1, 2, 3, 4, 5)
            for db in range(4):
                sl_t, sidx = ws.next(S, [(0, 6144, cin_s[db], None)])
                wv = sl_t[:, 0:6144].rearrange("p (k s f) -> p k s f", s=3, f=256)
                for dcl in range(2):
                    dc = db * 2 + dcl
                    bks = []
                    for sct in range(3):
                        bk = ps_next(CB)
                        bks.append(bk)
                        for kc in range(KC):
                            mm(S, psum[bk][:, :], wv[:, kc, sct, dcl * 128:(dcl + 1) * 128], h[:, kc, :],
                               kc == 0, kc == KC - 1, [("slot", sidx), ("h", kc)], bk)
                    S.add("act", (lambda e, dc=dc, bk=bks[0]: e.activation(out=bgt[:, dc, :], in_=psum[bk][:, :], func=AF.Copy)),
                          reads=[("ps", bks[0])], writes=bigrows(2 * dc, 2))
                    S.add("act", (lambda e, bk=bks[1]: e.activation(out=tmpA[:, :], in_=psum[bk][:, :], func=AF.Copy)),
                          reads=[("ps", bks[1])], writes=["tmpA"])
                    S.add("dve", (lambda e, dc=dc, bk=bks[2]: e.tensor_tensor(out=gt[:, dc, :], in0=psum[bk][:, :],
                                                                               in1=tmpA[:, :], op=ALU.mult)),
                          reads=[("ps", bks[2]), "tmpA"], writes=bigrows(16 + 2 * dc, 2))
            if is_halo:
                S.add("dve", (lambda e: e.tensor_scalar(out=hcol[:, :, 0:1], in0=gt[:, :, 255:256], scalar1=hmask[:, 0:1],
                                                        scalar2=None, op0=ALU.mult)),
                      reads=bigrows(16, 16) + ["hmask"], writes=["hcol0"])
                S.add("dve", (lambda e: e.tensor_scalar(out=hcol[:, :, 1:2], in0=gt[:, :, 0:1], scalar1=hmask[:, 1:2],
                                                        scalar2=None, op0=ALU.mult)),
                      reads=bigrows(16, 16) + ["hmask"], writes=["hcol1"])
                nq = geo.regions[2][2] * T
                S.add("pool", (lambda e: e.dma_start(out=g_s[2][:, :, 0:1], in_=hcol[:, :, 0:1], allow_slow_non_contiguous=True)),
                      reads=["hcol0"], writes=[("g_s", 2, "lo")], dsem="misc")
                S.add("pool", (lambda e: e.dma_start(out=g_s[2][:, :, nq + 1:nq + 2], in_=hcol[:, :, 1:2], allow_slow_non_contiguous=True)),
                      reads=["hcol1"], writes=[("g_s", 2, "hi")], dsem="misc")
                return
            S.add("pool", (lambda e: e.dma_start(out=xb_s[own_idx], in_=x[:, :, :])),
                  reads=[("x", c) for c in range(KC)], writes=[("xb_s", own_idx)], dsem="st_x")
            S.add("pool", (lambda e: e.dma_start(out=bg_s[own_idx], in_=bgt)),
                  reads=bigrows(0, 16), writes=[("bg_s", own_idx)], dsem="st_q")
            S.add("pool", (lambda e: e.dma_start(out=g_s[region][:, :, 1 + tin * T: 1 + (tin + 1) * T], in_=gt)),
                  reads=bigrows(16, 16), writes=[("g_s", region, tin)], dsem="st_k")

        def conv_mix(S, ws, b, own_idx, region, tin):
            gwin = big[:, 0:8224].bitcast(F32).rearrange("p (c t) -> p c t", t=514)
            bgt = big_f32(17, 16, T)
            mcv = big_bf(33, 8)
            nreg = geo.regions[region][2]
            deps = [("g_s", region, tin)]
            if tin > 0:
                deps.append(("g_s", region, tin - 1))
            else:
                deps.append(("g_s", region, "lo"))
            if tin < nreg - 1:
                deps.append(("g_s", region, tin + 1))
            else:
                deps.append(("g_s", region, "hi"))
            S.add("sp", (lambda e: e.dma_start(out=x[:, :, :], in_=xb_s[own_idx])),
                  reads=[("xb_s", own_idx)], writes=[("x", c) for c in range(KC)], dsem="x")
            S.add("sp", (lambda e: e.dma_start(out=gwin, in_=g_s[region][:, :, tin * T: tin * T + 514])),
                  reads=deps, writes=bigrows(0, 17), dsem="gw")
            S.add("sp", (lambda e: e.dma_start(out=bgt, in_=bg_s[own_idx])),
                  reads=[("bg_s", own_idx)], writes=bigrows(17, 16), dsem="bgl")
            for c in range(KC):
                w0 = smallT[:, C_CONVK + 0 * 8 + c: C_CONVK + 0 * 8 + c + 1]
                w1 = smallT[:, C_CONVK + 1 * 8 + c: C_CONVK + 1 * 8 + c + 1]
                w2 = smallT[:, C_CONVK + 2 * 8 + c: C_CONVK + 2 * 8 + c + 1]
                tt = tmpB if c % 2 == 0 else tmpC
                tk = "tmpB" if c % 2 == 0 else "tmpC"
                S.add("act", (lambda e, c=c, tt=tt, w0=w0: e.activation(out=tt[:, :], in_=gwin[:, c, 0:512],
                                                                        func=AF.Copy, scale=w0)),
                      reads=bigrows(0, 17), writes=[tk])
                S.add("dve", (lambda e, c=c, tt=tt, w1=w1: e.scalar_tensor_tensor(
                    out=tt[:, :], in0=gwin[:, c, 1:513], scalar=w1, in1=tt[:, :], op0=ALU.mult, op1=ALU.add)),
                      reads=bigrows(0, 17) + [tk], writes=[tk])
                S.add("dve", (lambda e, c=c, tt=tt, w2=w2: e.scalar_tensor_tensor(
                    out=tt[:, :], in0=gwin[:, c, 2:514], scalar=w2, in1=tt[:, :], op0=ALU.mult, op1=ALU.add)),
                      reads=bigrows(0, 17) + [tk], writes=[tk])
                S.add("pool", (lambda e, c=c, tt=tt: e.tensor_tensor(out=mcv[:, c, :], in0=tt[:, :], in1=bgt[:, c, :],
                                                                     op=ALU.mult)),
                      reads=[tk] + bigrows(17 + 2 * c, 2), writes=bigrows(33 + c, 1))
            OB_ = (0, 1, 2, 3)
            for db in range(4):
                sl_t, sidx = ws.next(S, [(0, 2048, cout_s[db], None)])
                wv = sl_t[:, 0:2048].rearrange("p (k f) -> p k f", f=256)
                for dcl in range(2):
                    dc = db * 2 + dcl
                    bk = ps_next(OB_)
                    for kc in range(KC):
                        mm(S, psum[bk][:, :], wv[:, kc, dcl * 128:(dcl + 1) * 128], mcv[:, kc, :],
                           kc == 0, kc == KC - 1, [("slot", sidx)] + bigrows(33 + kc, 1), bk)
                    if dc >= 1:
                        stats_mm(S, dc - 1, 4)
                    g_ap = G_t[:, 2, 1, b, dc:dc + 1]
                    resid(S, bk, dc, g_ap, 4)
            stats_mm(S, KC - 1, 4)
            pre_state["bank"] = 4

        def setup(S):
            S.add("pool", (lambda e: e.memset(ones_f[:, :], 1.0)), reads=(), writes=["ones_f"])
            S.add("pool", (lambda e: e.memset(ones_b[:, :], 1.0)), reads=(), writes=["ones_b"])
            S.add("pool", (lambda e: e.memset(zcol[:, :, :], 0.0)), reads=(), writes=["zcol"])
            S.add("pool", (lambda e: e.memset(eps_t[:, :], EPS)), reads=(), writes=["eps_t"])
            S.add("sp", (lambda e: e.dma_start(out=ident_f[:, :], in_=ident_in[:, :])), reads=(), writes=["ident_f"], dsem="misc")
            S.add("sp", (lambda e: e.dma_start(out=hmask[:, :], in_=hmask_in[:, :])), reads=(), writes=["hmask"], dsem="misc")
            S.add("sp", (lambda e: e.dma_start(out=stage[:, :, :], in_=smallp.rearrange("(a p) f -> p a f", p=128))),
                  reads=(), writes=[("stage", i) for i in range(5)], dsem="misc")
            S.add("dve", (lambda e: e.tensor_copy(out=ident_b[:, :], in_=ident_f[:, :])), reads=["ident_f"], writes=["ident_b"])
            for a in range(5):
                bk = ps_next()
                S.add("pe", (lambda e, a=a, bk=bk: e.transpose(out=psum[bk][:, 0:128], in_=stage[:, a, :], identity=ident_f[:, :])),
                      reads=[("stage", a), "ident_f"], writes=[("ps", bk)])
                S.add("dve", (lambda e, a=a, bk=bk: e.tensor_copy(out=smallT[:, a * 128:(a + 1) * 128], in_=psum[bk][:, 0:128])),
                      reads=[("ps", bk)], writes=["smallT"])
            def prep(dst, src):
                S.add("pool", (lambda e, dst=dst, src=src: e.dma_start(out=dst, in_=src)),
                      reads=(), writes=["wprep"], dsem="prep")
            for li in range(DEPTH):
                for fi in range(2):
                    for fb in range(11):
                        prep(wg_s[li, fi, fb], w_gate[li, fi][:, fb * 256:(fb + 1) * 256].rearrange("(k p) f -> p k f", p=128))
                        prep(wu_s[li, fi, fb], w_up[li, fi][:, fb * 256:(fb + 1) * 256].rearrange("(k p) f -> p k f", p=128))
                    for db in range(4):
                        prep(wd_s[li, fi, db], w_down[li, fi][:, db * 256:(db + 1) * 256].rearrange("(k p) f -> p k f", p=128))
            for j in range(2):
                for ub in range(12):
                    prep(sgu_s[j, ub], sg_w_in[j][:, ub * 256:(ub + 1) * 256].rearrange("(k p) f -> p k f", p=128))
                for vb in range(6):
                    prep(sgv_s[j, vb], sg_w_in[j][:, SGH + vb * 512:SGH + (vb + 1) * 512].rearrange("(k p) f -> p k f", p=128))
                for db in range(4):
                    prep(sgo_s[j, db], sg_w_out[j][:, db * 256:(db + 1) * 256].rearrange("(k p) f -> p k f", p=128))
            for cb in range(6):
                prep(qkv_s[cb], da_w_qkv[0][:, cb * 512:(cb + 1) * 512].rearrange("(k p) f -> p k f", p=128))
            for db in range(4):
                prep(wo_s[db], da_w_out[0][:, db * 256:(db + 1) * 256].rearrange("(k p) f -> p k f", p=128))
                prep(cout_s[db], conv_w_out[0][:, db * 256:(db + 1) * 256].rearrange("(k p) f -> p k f", p=128))
                for sct in range(3):
                    prep(cin_s[db][:, :, sct, :],
                         conv_w_in[0][:, sct * D + db * 256: sct * D + (db + 1) * 256].rearrange("(k p) f -> p k f", p=128))
            for bb in range(3):
                S.add("act", (lambda e, bb=bb: e.activation(out=cact[:, :, bb], in_=smallT[:, C_C + bb * 8: C_C + bb * 8 + 8],
                                                            func=AF.Silu)),
                      reads=["smallT"], writes=["cact"])
            for li in range(DEPTH):
                bk = ps_next()
                for nb in range(18):
                    sidx = nb % 2
                    blk = big[:, sidx * 8192:(sidx + 1) * 8192].bitcast(F32).rearrange("p (k f) -> p k f", f=512)
                    S.add("sp", (lambda e, blk=blk, li=li, nb=nb: e.dma_start(
                        out=blk, in_=ada_w[li][:, nb * 512:(nb + 1) * 512].rearrange("(k p) f -> p k f", p=128))),
                          reads=(), writes=[("adablk", sidx)], dsem="slot%d" % sidx)
                    for n4 in range(4):
                        n = nb * 4 + n4
                        for kc in range(KC):
                            S.add("pe", (lambda e, bk=bk, blk=blk, n=n, n4=n4, kc=kc: e.matmul(
                                out=psum[bk][:, n * 3:(n + 1) * 3], lhsT=blk[:, kc, n4 * 128:(n4 + 1) * 128],
                                rhs=cact[:, kc, :], start=(kc == 0), stop=(kc == KC - 1))),
                                  reads=[("adablk", sidx), "cact"], writes=[("ps", bk)])
                for bb in range(3):
                    S.add("dve", (lambda e, bk=bk, li=li, bb=bb: e.tensor_tensor(
                        out=modt[:, li, :, bb], in0=psum[bk][:, 0:216].rearrange("p (n b) -> p n b", b=3)[:, :, bb],
                        in1=smallT[:, C_ADAB + li * 72: C_ADAB + (li + 1) * 72], op=ALU.add)),
                          reads=[("ps", bk), "smallT"], writes=["modt"])
            for li in range(DEPTH):
                for sl in range(3):
                    for bb in range(3):
                        ng = smallT[:, C_NORMG + (li * 3 + sl) * 8: C_NORMG + (li * 3 + sl) * 8 + 8]
                        S.add("dve", (lambda e, li=li, sl=sl, bb=bb, ng=ng: e.scalar_tensor_tensor(
                            out=A_t[:, li, sl, bb, :], in0=modt[:, li, (3 * sl + 1) * 8:(3 * sl + 2) * 8, bb], scalar=1.0,
                            in1=ng, op0=ALU.add, op1=ALU.mult)),
                              reads=["modt", "smallT"], writes=["A_t"])
                        S.add("dve", (lambda e, li=li, sl=sl, bb=bb: e.tensor_copy(
                            out=B_t[:, li, sl, bb, :], in_=modt[:, li, (3 * sl) * 8:(3 * sl + 1) * 8, bb])),
                              reads=["modt"], writes=["B_t"])
                        S.add("dve", (lambda e, li=li, sl=sl, bb=bb: e.tensor_scalar(
                            out=G_t[:, li, sl, bb, :], in0=modt[:, li, (3 * sl + 2) * 8:(3 * sl + 3) * 8, bb],
                            scalar1=(1.0 if sl == 1 else 0.5), scalar2=None, op0=ALU.mult)),
                              reads=["modt"], writes=["G_t"])
            lam_init = 0.8 - 0.6 * math.exp(-0.3 * 1)
            S.add("sp", (lambda e: e.dma_start(out=lamrow[:, :], in_=da_lambda[0:1].rearrange("a r d -> a (r d)"))),
                  reads=(), writes=["lamrow"], dsem="misc")
            S.add("dve", (lambda e: e.tensor_tensor(out=lamrow[:, 0:64], in0=lamrow[:, 0:64], in1=lamrow[:, 64:128], op=ALU.mult)),
                  reads=["lamrow"], writes=["lamrow"])
            S.add("dve", (lambda e: e.tensor_tensor(out=lamrow[:, 128:192], in0=lamrow[:, 128:192], in1=lamrow[:, 192:256], op=ALU.mult)),
                  reads=["lamrow"], writes=["lamrow"])
            S.add("dve", (lambda e: e.reduce_sum(out=lamw[:, 0:1], in_=lamrow[:, 0:64], axis=AX.X)),
                  reads=["lamrow"], writes=["lamw"])
            S.add("dve", (lambda e: e.reduce_sum(out=lamw[:, 1:2], in_=lamrow[:, 128:192], axis=AX.X)),
                  reads=["lamrow"], writes=["lamw"])
            S.add("act", (lambda e: e.activation(out=lamw[:, 2:4], in_=lamw[:, 0:2], func=AF.Exp)),
                  reads=["lamw"], writes=["lamw"])
            S.add("dve", (lambda e: e.tensor_tensor(out=lamw[:, 4:5], in0=lamw[:, 3:4], in1=lamw[:, 2:3], op=ALU.subtract)),
                  reads=["lamw"], writes=["lamw"])
            S.add("dve", (lambda e: e.tensor_scalar(out=lamw[:, 5:6], in0=lamw[:, 4:5], scalar1=-lam_init, scalar2=None, op0=ALU.add)),
                  reads=["lamw"], writes=["lamw"])
            bk = ps_next()
            S.add("pe", (lambda e, bk=bk: e.matmul(out=psum[bk][:, 0:1], lhsT=ones_f[0:1, :], rhs=lamw[0:1, 5:6], start=True, stop=True)),
                  reads=["lamw", "ones_f"], writes=[("ps", bk)])
            S.add("dve", (lambda e, bk=bk: e.tensor_copy(out=neglam[:, :], in_=psum[bk][:, 0:1])),
                  reads=[("ps", bk)], writes=["neglam"])
            S.add("dve", (lambda e: e.tensor_scalar(out=sublng[:, :], in0=smallT[:, C_SUBLN:C_SUBLN + 1], scalar1=1.0 - lam_init,
                                                    scalar2=None, op0=ALU.mult)),
                  reads=["smallT"], writes=["sublng"])
            for r in range(2):
                n = geo.regions[r][2] * T
                S.add("pool", (lambda e, r=r: e.dma_start(out=g_s[r][:, :, 0:1], in_=zcol[:, :, :], allow_slow_non_contiguous=True)),
                      reads=["zcol"], writes=[("g_s", r, "lo")], dsem="misc")
                S.add("pool", (lambda e, r=r, n=n: e.dma_start(out=g_s[r][:, :, n + 1:n + 2], in_=zcol[:, :, :], allow_slow_non_contiguous=True)),
                      reads=["zcol"], writes=[("g_s", r, "hi")], dsem="misc")


        def final_out2(S, own_idx):
            fin_buf = big_f32(0, 16, T)
            norm_mod(S, 0, 0, 0, hout=fin_buf, final=True)
            otm = big_f32(16, 16, D)
            for st in range(4):
                for half in range(2):
                    bk = ps_next()
                    for c4 in range(4):
                        c = half * 4 + c4
                        S.add("pe", (lambda e, bk=bk, c=c, c4=c4, st=st: e.transpose(
                            out=psum[bk][:, c4 * 128:(c4 + 1) * 128], in_=fin_buf[:, c, st * 128:(st + 1) * 128],
                            identity=ident_f[:, :])),
                              reads=bigrows(2 * c, 2) + ["ident_f"], writes=[("ps", bk)])
                    rws = bigrows(16 + 4 * st + 2 * half, 2)
                    if half == 0:
                        S.add("act", (lambda e, bk=bk, st=st: e.activation(out=otm[:, st, 0:512], in_=psum[bk][:, :], func=AF.Copy)),
                              reads=[("ps", bk)], writes=rws)
                    else:
                        S.add("dve", (lambda e, bk=bk, st=st: e.tensor_copy(out=otm[:, st, 512:1024], in_=psum[bk][:, :])),
                              reads=[("ps", bk)], writes=rws)
            S.add("pool", (lambda e: e.dma_start(out=y_out[own_idx * T:(own_idx + 1) * T, :].rearrange("(s p) d -> p s d", p=128),
                                                 in_=otm)),
                  reads=bigrows(16, 16), writes=[("y", own_idx)], dsem="out")

        def tile_batch(tile_idx):
            if tile_idx < geo.ntP:
                return 0
            if tile_idx < 2 * geo.ntP:
                return 1
            return 2

        import os as _os
        STOP = int(_os.environ.get("KSTOP", "99"))

        def dump_x(S):
            S.add("pool", (lambda e: e.dma_start(out=dbg_out, in_=x[:, :, :])),
                  reads=[("x", c) for c in range(KC)], writes=["dbg"], dsem="out")

        def program(S, ws):
            _program(S, ws)
            if STOP != 99:
                dump_x(S)

        def _program(S, ws):
            setup(S)
            if STOP == 0:
                return
            gmlp_setup(S, 0)
            if STOP == 1:
                return
            S.barrier()
            for ti in range(geo.ntA):
                b = tile_batch(ti)
                load_x_tokens(S, ti)
                if STOP == 2:
                    return
                ffn(S, ws, 0, 0, b, 0)
                if STOP == 3:
                    return
                gmlp(S, ws, 0, 0, b)
                if STOP == 4:
                    return
                ffn(S, ws, 0, 1, b, 2)
                ffn(S, ws, 1, 0, b, 0)
                qkv_rotary(S, ws, ti, b)
                if STOP == 5:
                    return
            S.barrier()
            if STOP == 6:
                return
            btiles = [(ti, r, i, oi) for oi, (ti, r, i) in enumerate(geo.own)] + [(geo.halo_tile, 2, -1, -1)]
            for (ti, r, tin, oi) in btiles:
                b = r
                S.add("sp", (lambda e, ti=ti: e.dma_start(out=x[:, :, :], in_=xa_s[ti])),
                      reads=[("xa_s", ti)], writes=[("x", c) for c in range(KC)], dsem="x")
                if STOP == 10:
                    return
                attention(S, ws, ti, r, b)
                if STOP == 7:
                    return
                ffn(S, ws, 1, 1, b, 2)
                ffn(S, ws, 2, 0, b, 0)
                conv_in(S, ws, b, oi, r, tin, tin < 0)
                if STOP == 8:
                    return
            S.barrier()
            gmlp_setup(S, 1)
            S.barrier()
            for oi, (ti, r, tin) in enumerate(geo.own):
                b = r
                conv_mix(S, ws, b, oi, r, tin)
                if STOP == 9:
                    return
                ffn(S, ws, 2, 1, b, 2)
                ffn(S, ws, 3, 0, b, 0)
                gmlp(S, ws, 3, 1, b)
                ffn(S, ws, 3, 1, b, 2)
                final_out2(S, oi)
                if STOP == 11:
                    return

        rec = WStream(None)
        S0 = Sched()
        ps_rr[0] = 0
        pre_state["bank"] = None
        program(S0, rec)
        ps_rr[0] = 0
        pre_state["bank"] = None
        ws = WStream(rec.out)
        program(S, ws)
        S.emit(nc, block, sems, dsems)
    return nc


def _run(inputs, S_P, S_S):
    geo = Geo(S_P, S_S)
    f32 = np.float32
    xp = np.asarray(inputs["x_prompt"], f32)
    xs = np.asarray(inputs["x_sample"], f32)
    cp = np.asarray(inputs["c_prompt"], f32)
    cs = np.asarray(inputs["c_sample"], f32)
    assert xp.shape == (16, S_P, D) and xs.shape == (2, S_S, D)
    nc = build_program(geo)
    ident = np.eye(128, dtype=f32)
    shared = {k: np.ascontiguousarray(np.asarray(inputs[k], f32)) for k in
              ["ada_w", "ffn_w_gate", "ffn_w_up", "ffn_w_down", "sg_w_in", "sg_ln_g", "sg_ln_b", "sg_w_s", "sg_b_s",
               "sg_b_in", "sg_w_out", "da_w_qkv", "da_lambda", "da_w_out", "conv_w_in", "conv_w_out"]}
    in_maps = []
    orders = []
    for core in range(NCORES):
        sseq = core // 4
        rank = core % 4
        order = geo.sample_order(rank)
        orders.append(order)
        xs_perm = xs[sseq].reshape(S_S // 128, 128, D)[order].reshape(S_S, D)
        xA = np.concatenate([xp[2 * core], xp[2 * core + 1], xs_perm], axis=0)
        pos_s = (np.asarray(order)[:, None] * 128 + np.arange(128)[None, :]).reshape(-1)
        pos = np.concatenate([np.arange(S_P), np.arange(S_P), pos_s])
        rope = _rope_table(pos)
        small = np.zeros((640, 128), f32)
        small[0:288] = np.asarray(inputs["ada_b"], f32).reshape(288, 128)
        small[288:384] = np.asarray(inputs["norm_g"], f32).reshape(96, 128)
        small[384:480] = np.asarray(inputs["sg_b_in"], f32).reshape(96, 128)
        small[480:504] = np.asarray(inputs["conv_kernel"], f32).reshape(24, 128)
        small[504:512] = np.asarray(inputs["final_norm_g"], f32).reshape(8, 128)
        small[512:513] = np.asarray(inputs["da_subln_g"], f32).reshape(1, 128)
        small[513:521] = cp[2 * core].reshape(8, 128)
        small[521:529] = cp[2 * core + 1].reshape(8, 128)
        small[529:537] = cs[sseq].reshape(8, 128)
        hm = np.zeros((128, 2), f32)
        hm[:, 0] = 0.0 if rank == 0 else 1.0
        hm[:, 1] = 0.0 if rank == 3 else 1.0
        m = {"xA": np.ascontiguousarray(xA), "rope": rope, "smallp": small, "ident": ident, "hmask": hm}
        m.update(shared)
        in_maps.append(m)
    res = run_bass_kernel_spmd(nc, in_maps, core_ids=list(range(NCORES)))
    global _last_res
    _last_res = res
    yp = np.empty((16, S_P, D), f32)
    ys = np.empty((2, S_S, D), f32)
    for core in range(NCORES):
        y = np.asarray(res.results[core]["y"], f32)
        yp[2 * core] = y[0:S_P]
        yp[2 * core + 1] = y[S_P:2 * S_P]
        rank = core % 4
        ys[core // 4, rank * geo.Q:(rank + 1) * geo.Q] = y[2 * S_P:2 * S_P + geo.Q]
    return yp, ys


def kernel(**inputs):
    S_P = int(np.asarray(inputs["x_prompt"]).shape[1])
    S_S = int(np.asarray(inputs["x_sample"]).shape[1])
    return _run(inputs, S_P, S_S)
```
